# Optimizing a Trainium2 kernel written in Bass

```python
import math
import jax, jax.numpy as jnp
from jax import lax
import numpy as np

D_MODEL = 1024
BATCH = 8
SEQ = 8192
DEPTH = 2

GRID_W = 64
CTX_LEN = 256
N_BRANCH = 4
BRANCH_W = 512
HY_CH = 512
HY_ORDER = 2
HY_BANDS = 16
HY_EMB = 1 + 2 * HY_BANDS
HY_FFN = 64
HY_MIN_DECAY = 3.07
HY_MAX_DECAY = 15.35
GA_HEADS = 4
GA_KV = 2
GA_HD = 128
ML_HEADS = 4
ML_HD = 128
ML_CHUNK = 64
ML_GATES = 2 * 2 * ML_HEADS
WA_HEADS = 8
WA_KV = 2
WA_HD = 64
WINDOW = 128
Q_BLOCK = 128
ROPE_BASE = 10000.0
D_FF = 4 * D_MODEL
EPS = 1e-6
NEG_INF = -1e30

HY_COLS = 3 * HY_CH
GA_COLS = (GA_HEADS + 2 * GA_KV) * GA_HD
ML_COLS = 4 * ML_HEADS * ML_HD + ML_GATES
WA_COLS = (WA_HEADS + 2 * WA_KV) * WA_HD
GATE_COLS = N_BRANCH * D_MODEL
N_IN = HY_COLS + GA_COLS + ML_COLS + WA_COLS + GATE_COLS

kernel_name = "hybrid_parallel_gated_diffusion_block"


def rmsnorm(x, g):
    xf = x.astype(jnp.float32)
    y = xf * lax.rsqrt(jnp.mean(xf * xf, axis=-1, keepdims=True) + EPS)
    return (y * g.astype(jnp.float32)).astype(x.dtype)


def short_conv3(u, w, b):
    up = jnp.pad(u, ((0, 0), (1, 1), (0, 0)))
    return w[0] * up[:, :-2] + w[1] * up[:, 1:-1] + w[2] * up[:, 2:] + b


def axial_rope(x):
    L, d = x.shape[1], x.shape[-1]
    n_rows = L // GRID_W
    row = jnp.repeat(jnp.arange(n_rows, dtype=jnp.float32), GRID_W)
    col = jnp.tile(jnp.arange(GRID_W, dtype=jnp.float32), n_rows)
    quarter = d // 4
    inv = ROPE_BASE ** (-jnp.arange(quarter, dtype=jnp.float32) / quarter)

    def rotate(xa, pos):
        ang = pos[:, None] * inv[None, :]
        cos = jnp.cos(ang)[None, :, None, :]
        sin = jnp.sin(ang)[None, :, None, :]
        a, b = jnp.split(xa, 2, axis=-1)
        return jnp.concatenate([a * cos - b * sin, a * sin + b * cos], axis=-1)

    x_row, x_col = jnp.split(x.astype(jnp.float32), 2, axis=-1)
    return jnp.concatenate([rotate(x_row, row), rotate(x_col, col)], axis=-1).astype(x.dtype)


def scores(qg, k):
    return jnp.einsum('bqkgd,bskd->bkgqs', qg, k).astype(jnp.float32) * (qg.shape[-1] ** -0.5)


def mix_values(p, v):
    return jnp.einsum('bkgqs,bskd->bqkgd', p.astype(v.dtype), v)


def with_sink(s, sink):
    kv, g = s.shape[1], s.shape[2]
    col = jnp.broadcast_to(sink.astype(jnp.float32).reshape(1, kv, g, 1, 1), s.shape[:-1] + (1,))
    return jnp.concatenate([s, col], axis=-1)


def dense_attention(q, k, v, sink):
    B, L, Hq, d = q.shape
    kv = k.shape[2]
    qg = q.reshape(B, L, kv, Hq // kv, d)
    s = scores(qg, k)
    if sink is None:
        p = jax.nn.softmax(s, axis=-1)
    else:
        p = jax.nn.softmax(with_sink(s, sink), axis=-1)[..., :-1]
    return mix_values(p, v).reshape(B, L, Hq * d)


def global_block_attention(q, k_all, v_all):
    B, L, Hq, d = q.shape
    kv = k_all.shape[2]
    nb = L // Q_BLOCK
    qb = jnp.moveaxis(q.reshape(B, nb, Q_BLOCK, kv, Hq // kv, d), 1, 0)

    def one(qblk):
        p = jax.nn.softmax(scores(qblk, k_all), axis=-1)
        return mix_values(p, v_all)

    o = lax.map(one, qb)
    return jnp.moveaxis(o, 0, 1).reshape(B, L, Hq * d)


def window_block_attention(q, k, v, kc, vc, sink):
    B, L, Hq, d = q.shape
    kv = k.shape[2]
    nb = L // Q_BLOCK
    band = Q_BLOCK + 2 * WINDOW
    pad = ((0, 0), (WINDOW, WINDOW), (0, 0), (0, 0))
    kp = jnp.pad(k, pad)
    vp = jnp.pad(v, pad)
    qb = jnp.moveaxis(q.reshape(B, nb, Q_BLOCK, kv, Hq // kv, d), 1, 0)
    q_off = jnp.arange(Q_BLOCK)
    k_off = jnp.arange(band) - WINDOW

    def one(args):
        i, qblk = args
        start = i * Q_BLOCK
        kb = lax.dynamic_slice_in_dim(kp, start, band, axis=1)
        vb = lax.dynamic_slice_in_dim(vp, start, band, axis=1)
        k_pos = start + k_off
        q_pos = start + q_off
        valid = ((jnp.abs(q_pos[:, None] - k_pos[None, :]) <= WINDOW)
                 & (k_pos >= 0)[None, :] & (k_pos < L)[None, :])
        s_loc = jnp.where(valid, scores(qblk, kb), NEG_INF)
        s = with_sink(jnp.concatenate([s_loc, scores(qblk, kc)], axis=-1), sink)
        p = jax.nn.softmax(s, axis=-1)
        return mix_values(p[..., :band], vb) + mix_values(p[..., band:-1], vc)

    o = lax.map(one, (jnp.arange(nb), qb))
    return jnp.moveaxis(o, 0, 1).reshape(B, L, Hq * d)


def hyena_filter_spectrum(L, w1, b1, freq, w2, b2, w3, decay):
    f32 = jnp.float32
    t = jnp.arange(L, dtype=f32)
    tn = t / (L - 1)
    w = 2.0 * math.pi * t / L
    bands = jnp.linspace(1e-4, HY_BANDS - 1, HY_BANDS, dtype=f32)
    ang = w[:, None] * bands[None, :]
    z = jnp.concatenate([tn[:, None], jnp.cos(ang), -jnp.sin(ang)], axis=-1)
    fr = freq.astype(f32)
    h = jnp.sin(fr[0] * (z @ w1.astype(f32) + b1.astype(f32)))
    h = jnp.sin(fr[1] * (h @ w2.astype(f32) + b2.astype(f32)))
    h = (h @ w3.astype(f32)) * jnp.exp(-tn[:, None] * jnp.abs(decay.astype(f32)))
    h = h.reshape(L, HY_ORDER, 2, HY_CH)
    fwd, bwd = h[:, :, 0], h[:, :, 1]
    k = jnp.concatenate([fwd, jnp.zeros((1, HY_ORDER, HY_CH), f32), jnp.flip(bwd[1:], axis=0)], axis=0)
    k = k * lax.rsqrt(jnp.sum(k * k, axis=0, keepdims=True) + EPS)
    return jnp.fft.rfft(k, axis=0)


def hyena_branch(z, p):
    L = z.shape[1]
    u = short_conv3(z, p['hy_conv_w'], p['hy_conv_b']).astype(jnp.float32)
    v, x1, x2 = jnp.split(u, 3, axis=-1)
    spec = hyena_filter_spectrum(L, p['hy_pe_w1'], p['hy_pe_b1'], p['hy_freq'], p['hy_pe_w2'],
                                 p['hy_pe_b2'], p['hy_pe_w3'], p['hy_decay'])
    skip = p['hy_skip'].astype(jnp.float32)
    y = v
    for n, gate in enumerate((x1, x2)):
        yf = jnp.fft.irfft(jnp.fft.rfft(y, n=2 * L, axis=1) * spec[None, :, n], n=2 * L, axis=1)[:, :L]
        y = gate * (yf + skip[n] * y)
    return y.astype(z.dtype)


def global_attention_branch(zl, zc, p, with_ctx_out):
    def heads(z):
        B, L, _ = z.shape
        q, k, v = jnp.split(z, [GA_HEADS * GA_HD, (GA_HEADS + GA_KV) * GA_HD], axis=-1)
        q = rmsnorm(q.reshape(B, L, GA_HEADS, GA_HD), p['ga_q_g'])
        k = rmsnorm(k.reshape(B, L, GA_KV, GA_HD), p['ga_k_g'])
        return q, k, v.reshape(B, L, GA_KV, GA_HD)

    ql, kl, vl = heads(zl)
    qc, kc, vc = heads(zc)
    ql, kl = axial_rope(ql), axial_rope(kl)
    out_l = global_block_attention(ql, jnp.concatenate([kl, kc], axis=1), jnp.concatenate([vl, vc], axis=1))
    out_c = dense_attention(qc, kc, vc, None) if with_ctx_out else None
    return out_l, out_c


def window_attention_branch(zl, zc, p, with_ctx_out):
    def heads(z):
        B, L, _ = z.shape
        q, k, v = jnp.split(z, [WA_HEADS * WA_HD, (WA_HEADS + WA_KV) * WA_HD], axis=-1)
        return (q.reshape(B, L, WA_HEADS, WA_HD), k.reshape(B, L, WA_KV, WA_HD),
                v.reshape(B, L, WA_KV, WA_HD))

    ql, kl, vl = heads(zl)
    qc, kc, vc = heads(zc)
    ql, kl = axial_rope(ql), axial_rope(kl)
    out_l = window_block_attention(ql, kl, vl, kc, vc, p['wa_sink'])
    out_c = dense_attention(qc, kc, vc, p['wa_sink']) if with_ctx_out else None
    return out_l, out_c


def mlstm_chunkwise(q, k, v, i_pre, f_pre, state):
    B, L, H, d = q.shape
    nc = L // ML_CHUNK

    def to_chunks(a):
        a = a.reshape((B, nc, ML_CHUNK) + a.shape[2:])
        return jnp.moveaxis(jnp.moveaxis(a, 1, 0), 3, 2)

    causal = jnp.tril(jnp.ones((ML_CHUNK, ML_CHUNK), dtype=bool))

    def step(carry, inp):
        C, n, m = carry
        qc, kc, vc, ic, lfc = inp
        b = jnp.cumsum(lfc, axis=-1)
        log_d = jnp.where(causal, b[..., :, None] - b[..., None, :] + ic[..., None, :], -jnp.inf)
        m_inter = b + m[..., None]
        m_t = jnp.maximum(m_inter, jnp.max(log_d, axis=-1))
        w_qk = jnp.einsum('bhtd,bhsd->bhts', qc, kc) * jnp.exp(log_d - m_t[..., None])
        carry_scale = jnp.exp(m_inter - m_t)
        num = (jnp.einsum('bhts,bhse->bhte', w_qk, vc)
               + carry_scale[..., None] * jnp.einsum('bhed,bhtd->bhte', C, qc))
        den = jnp.sum(w_qk, axis=-1) + carry_scale * jnp.einsum('bhd,bhtd->bht', n, qc)
        h = num / jnp.maximum(jnp.abs(den), jnp.exp(-m_t))[..., None]
        b_end = b[..., -1]
        log_w = b_end[..., None] - b + ic
        m_next = jnp.maximum(b_end + m, jnp.max(log_w, axis=-1))
        w = jnp.exp(log_w - m_next[..., None])
        decay = jnp.exp(b_end + m - m_next)
        C = decay[..., None, None] * C + jnp.einsum('bhs,bhse,bhsd->bhed', w, vc, kc)
        n = decay[..., None] * n + jnp.einsum('bhs,bhsd->bhd', w, kc)
        return (C, n, m_next), h

    xs = (to_chunks(q), to_chunks(k), to_chunks(v), to_chunks(i_pre),
          to_chunks(jax.nn.log_sigmoid(f_pre)))
    state, h = lax.scan(step, state, xs)
    h = jnp.moveaxis(jnp.moveaxis(h, 2, 3), 0, 1).reshape(B, L, H, d)
    return h, state


def mlstm_streams(z, p):
    B, L, _ = z.shape
    w = ML_HEADS * ML_HD
    q, k, v, o, g = jnp.split(z, [w, 2 * w, 3 * w, 4 * w], axis=-1)
    qk = jax.nn.silu(short_conv3(jnp.concatenate([q, k], axis=-1), p['ml_conv_w'], p['ml_conv_b']))
    q, k = jnp.split(qk.astype(jnp.float32), 2, axis=-1)
    heads = lambda a: a.reshape(B, L, ML_HEADS, ML_HD)
    g = (g + p['ml_gate_b']).astype(jnp.float32).reshape(B, L, 2, 2, ML_HEADS)
    return heads(q), heads(k) * (ML_HD ** -0.5), heads(v.astype(jnp.float32)), o, g


def mlstm_branch(zl, zc, p, with_ctx_out):
    flip = lambda a: jnp.flip(a, axis=1)
    ql, kl, vl, ol, gl = mlstm_streams(zl, p)
    qc, kc, vc, oc, gc = mlstm_streams(zc, p)
    B = zl.shape[0]
    zero = (jnp.zeros((B, ML_HEADS, ML_HD, ML_HD), jnp.float32),
            jnp.zeros((B, ML_HEADS, ML_HD), jnp.float32),
            jnp.zeros((B, ML_HEADS), jnp.float32))
    h_cf, st_f = mlstm_chunkwise(qc, kc, vc, gc[:, :, 0, 0], gc[:, :, 0, 1], zero)
    h_cb, st_b = mlstm_chunkwise(flip(qc), flip(kc), flip(vc), flip(gc[:, :, 1, 0]), flip(gc[:, :, 1, 1]), zero)
    h_lf, _ = mlstm_chunkwise(ql, kl, vl, gl[:, :, 0, 0], gl[:, :, 0, 1], st_f)
    h_lb, _ = mlstm_chunkwise(flip(ql), flip(kl), flip(vl), flip(gl[:, :, 1, 0]), flip(gl[:, :, 1, 1]), st_b)
    g_norm = p['ml_norm_g'].reshape(ML_HEADS, ML_HD)

    def finish(h, o):
        Bh, Lh = o.shape[:2]
        h = rmsnorm(h, g_norm).reshape(Bh, Lh, ML_HEADS * ML_HD)
        return (jax.nn.sigmoid(o.astype(jnp.float32)) * h).astype(o.dtype)

    out_l = finish(h_lf + flip(h_lb), ol)
    out_c = finish(h_cf + flip(h_cb), oc) if with_ctx_out else None
    return out_l, out_c


def merge_branches(ys, gate_logits, w_up, w_out):
    B, L, _ = gate_logits.shape
    g = jax.nn.sigmoid(gate_logits.astype(jnp.float32)).reshape(B, L, N_BRANCH, D_MODEL)
    acc = g[:, :, 0] * (ys[0] @ w_up[0]).astype(jnp.float32)
    for n in range(1, N_BRANCH):
        acc = acc + g[:, :, n] * (ys[n] @ w_up[n]).astype(jnp.float32)
    return (acc.astype(gate_logits.dtype) @ w_out).astype(gate_logits.dtype)


def token_mixers(zl, zc, p, with_ctx_out):
    idx = np.cumsum([HY_COLS, GA_COLS, ML_COLS, WA_COLS]).tolist()
    hy_l, ga_l, ml_l, wa_l, gate_l = jnp.split(zl, idx, axis=-1)
    hy_c, ga_c, ml_c, wa_c, gate_c = jnp.split(zc, idx, axis=-1)
    ya_l = hyena_branch(hy_l, p)
    yb_l, yb_c = global_attention_branch(ga_l, ga_c, p, with_ctx_out)
    yc_l, yc_c = mlstm_branch(ml_l, ml_c, p, with_ctx_out)
    yd_l, yd_c = window_attention_branch(wa_l, wa_c, p, with_ctx_out)
    out_l = merge_branches([ya_l, yb_l, yc_l, yd_l], gate_l, p['w_up'], p['w_out'])
    if with_ctx_out:
        ya_c = hyena_branch(hy_c, p)
        out_c = merge_branches([ya_c, yb_c, yc_c, yd_c], gate_c, p['w_up'], p['w_out'])
    else:
        out_c = None
    return out_l, out_c


def sq_relu_mlp(h, w1, b1, w2, b2):
    a = jax.nn.relu(h @ w1 + b1)
    return (a * a) @ w2 + b2


def setup_inputs(seed: int = 0) -> dict:
    key = jax.random.key(seed)
    ks = iter(jax.random.split(key, 64))
    f32 = jnp.float32

    def nrm(shape, scale):
        return jax.random.normal(next(ks), shape, f32) * scale

    Dm = D_MODEL
    n_filt = HY_ORDER * 2 * HY_CH
    gate_base = jnp.stack([jnp.zeros((ML_HEADS,), f32), jnp.linspace(3.0, 6.0, ML_HEADS, dtype=f32)])
    return {
        'x': nrm((BATCH, SEQ, Dm), 1.0),
        'c': nrm((BATCH, Dm), 1.0),
        'ctx': nrm((BATCH, CTX_LEN, Dm), 1.0),
        'c_ctx': nrm((Dm,), 1.0),
        'w_mod': nrm((DEPTH, Dm, 6 * Dm), Dm ** -0.5),
        'b_mod': nrm((DEPTH, 6 * Dm), 0.02),
        'ln1_g': 1.0 + nrm((DEPTH, Dm), 0.02),
        'ln2_g': 1.0 + nrm((DEPTH, Dm), 0.02),
        'w_in': nrm((DEPTH, Dm, N_IN), Dm ** -0.5),
        'hy_conv_w': nrm((DEPTH, 3, HY_COLS), 3 ** -0.5),
        'hy_conv_b': nrm((DEPTH, HY_COLS), 0.02),
        'hy_pe_w1': nrm((DEPTH, HY_EMB, HY_FFN), HY_EMB ** -0.5),
        'hy_pe_b1': nrm((DEPTH, HY_FFN), 0.1),
        'hy_freq': 1.0 + nrm((DEPTH, 2, HY_FFN), 0.1),
        'hy_pe_w2': nrm((DEPTH, HY_FFN, HY_FFN), HY_FFN ** -0.5),
        'hy_pe_b2': nrm((DEPTH, HY_FFN), 0.1),
        'hy_pe_w3': nrm((DEPTH, HY_FFN, n_filt), HY_FFN ** -0.5),
        'hy_decay': jnp.linspace(HY_MIN_DECAY, HY_MAX_DECAY, n_filt, dtype=f32)[None, :] * (1.0 + nrm((DEPTH, n_filt), 0.05)),
        'hy_skip': nrm((DEPTH, HY_ORDER, HY_CH), 0.5),
        'ga_q_g': 1.0 + nrm((DEPTH, GA_HD), 0.02),
        'ga_k_g': 1.0 + nrm((DEPTH, GA_HD), 0.02),
        'ml_conv_w': nrm((DEPTH, 3, 2 * ML_HEADS * ML_HD), 3 ** -0.5),
        'ml_conv_b': nrm((DEPTH, 2 * ML_HEADS * ML_HD), 0.02),
        'ml_gate_b': (jnp.broadcast_to(gate_base, (DEPTH, 2, 2, ML_HEADS)) + nrm((DEPTH, 2, 2, ML_HEADS), 0.1)).reshape(DEPTH, ML_GATES),
        'ml_norm_g': 1.0 + nrm((DEPTH, ML_HEADS * ML_HD), 0.02),
        'wa_sink': nrm((DEPTH, WA_HEADS), 0.5),
        'w_up': nrm((DEPTH, N_BRANCH, BRANCH_W, Dm), BRANCH_W ** -0.5),
        'w_out': nrm((DEPTH, Dm, Dm), Dm ** -0.5),
        'mlp_w1': nrm((DEPTH, Dm, D_FF), Dm ** -0.5),
        'mlp_b1': nrm((DEPTH, D_FF), 0.02),
        'mlp_w2': nrm((DEPTH, D_FF, Dm), D_FF ** -0.5),
        'mlp_b2': nrm((DEPTH, Dm), 0.02),
        'final_g': 1.0 + nrm((Dm,), 0.02),
    }


def reference(x, c, ctx, c_ctx, w_mod, b_mod, ln1_g, ln2_g, w_in, hy_conv_w, hy_conv_b,
              hy_pe_w1, hy_pe_b1, hy_freq, hy_pe_w2, hy_pe_b2, hy_pe_w3, hy_decay, hy_skip,
              ga_q_g, ga_k_g, ml_conv_w, ml_conv_b, ml_gate_b, ml_norm_g, wa_sink, w_up, w_out,
              mlp_w1, mlp_b1, mlp_w2, mlp_b2, final_g):
    xc = ctx
    silu_c = jax.nn.silu(c)
    silu_cc = jax.nn.silu(c_ctx)
    for l in range(DEPTH):
        with_ctx_out = l < DEPTH - 1
        p = dict(hy_conv_w=hy_conv_w[l], hy_conv_b=hy_conv_b[l], hy_pe_w1=hy_pe_w1[l],
                 hy_pe_b1=hy_pe_b1[l], hy_freq=hy_freq[l], hy_pe_w2=hy_pe_w2[l], hy_pe_b2=hy_pe_b2[l],
                 hy_pe_w3=hy_pe_w3[l], hy_decay=hy_decay[l], hy_skip=hy_skip[l],
                 ga_q_g=ga_q_g[l], ga_k_g=ga_k_g[l], ml_conv_w=ml_conv_w[l], ml_conv_b=ml_conv_b[l],
                 ml_gate_b=ml_gate_b[l], ml_norm_g=ml_norm_g[l], wa_sink=wa_sink[l],
                 w_up=w_up[l], w_out=w_out[l])
        mod_l = (silu_c @ w_mod[l] + b_mod[l])[:, None, :]
        mod_c = silu_cc @ w_mod[l] + b_mod[l]
        sh1, sc1, g1, sh2, sc2, g2 = jnp.split(mod_l, 6, axis=-1)
        csh1, csc1, cg1, csh2, csc2, cg2 = jnp.split(mod_c, 6, axis=-1)
        hl = rmsnorm(x, ln1_g[l]) * (1.0 + sc1) + sh1
        hc = rmsnorm(xc, ln1_g[l]) * (1.0 + csc1) + csh1
        yl, yc = token_mixers(hl @ w_in[l], hc @ w_in[l], p, with_ctx_out)
        x = x + g1 * yl
        h2 = rmsnorm(x, ln2_g[l]) * (1.0 + sc2) + sh2
        x = x + g2 * sq_relu_mlp(h2, mlp_w1[l], mlp_b1[l], mlp_w2[l], mlp_b2[l])
        if with_ctx_out:
            xc = xc + cg1 * yc
            hc2 = rmsnorm(xc, ln2_g[l]) * (1.0 + csc2) + csh2
            xc = xc + cg2 * sq_relu_mlp(hc2, mlp_w1[l], mlp_b1[l], mlp_w2[l], mlp_b2[l])
    return rmsnorm(x, final_g)
```

```python
from concourse.bass_utils import run_bass_kernel_spmd
import sys
import numpy as np
import concourse.bass as bass
import concourse.mybir as mybir
from contextlib import ExitStack

F32 = mybir.dt.float32
BF16 = mybir.dt.bfloat16
AF = mybir.ActivationFunctionType
ALU = mybir.AluOpType
AX = mybir.AxisListType


class V:
    __slots__ = ("ap", "key")

    def __init__(self, ap, key):
        self.ap = ap
        self.key = key


class Buf:
    def __init__(self, handle, key, track=True):
        self.h = handle
        self.key = key
        self.track = track

    def __getitem__(self, idx):
        return V(self.h[idx], self.key if self.track else None)

    def v(self, idx, sub):
        return V(self.h[idx], (self.key, sub))

    def ap(self, ap, sub=None):
        return V(ap, (self.key, sub) if sub is not None else (self.key if self.track else None))


class Op:
    __slots__ = ("eng", "fn", "reads", "writes", "dma", "deps", "tok", "waits", "marked")


ENGS = ["pe", "act", "dve", "pool", "sp"]
BLOCKNAME = {"pe": "tensor", "act": "scalar", "dve": "vector", "pool": "gpsimd", "sp": "sync"}
NDMASEM = 12


class Prog:
    def __init__(self, nc):
        self.nc = nc
        self.ops = []
        self.es = ExitStack()
        self.sems = {e: self.es.enter_context(nc.semaphore("cs_" + e)) for e in ENGS}
        self.dsems = {q: [self.es.enter_context(nc.semaphore(f"ds_{q}{i}")) for i in range(NDMASEM)]
                      for q in ("sp", "pool", "act")}
        self.allsems = list(self.sems.values()) + [s for l in self.dsems.values() for s in l]
        self.nphase = 0
        self.phase_names = []
        self.uid = 0

    def sb(self, es, name, shape, dtype):
        self.uid += 1
        nm = f"{name}_{self.uid}"
        h = es.enter_context(self.nc.sbuf_tensor(nm, list(shape), dtype))
        return Buf(h, nm)

    def sbpool(self, es, name, n, shape, dtype):
        return [self.sb(es, f"{name}{i}", shape, dtype) for i in range(n)]

    def ps(self, es, name, shape=(128, 512), dtype=F32):
        self.uid += 1
        nm = f"{name}_{self.uid}"
        h = es.enter_context(self.nc.psum_tensor(nm, list(shape), dtype))
        return Buf(h, nm)

    def op(self, eng, fn, reads, writes, dma=False):
        o = Op()
        o.eng = eng
        o.fn = fn
        o.reads = [k for k in reads if k is not None]
        o.writes = [k for k in writes if k is not None]
        o.dma = dma
        self.ops.append(o)
        return o

    @staticmethod
    def _ks(vs):
        return [v.key for v in vs if isinstance(v, V)]

    @staticmethod
    def _a(v):
        return v.ap if isinstance(v, V) else v

    def dma(self, q, out, in_, **kw):
        return self.op(q, lambda e: e.dma_start(out=out.ap, in_=in_.ap, **kw), [in_.key], [out.key], dma=True)

    def mm(self, out, lhsT, rhs, start=True, stop=True, **kw):
        return self.op("pe", lambda e: e.matmul(out.ap, lhsT.ap, rhs.ap, start=start, stop=stop, **kw),
                       [lhsT.key, rhs.key], [out.key])

    def transpose(self, out, in_, ident):
        return self.op("pe", lambda e: e.transpose(out.ap, in_.ap, ident.ap), [in_.key, ident.key], [out.key])

    def act(self, out, in_, func, bias=0.0, scale=1.0, accum_out=None, eng="act"):
        a = self._a
        rd = self._ks([in_, bias, scale])
        wr = self._ks([out, accum_out])
        kw = {}
        if accum_out is not None:
            kw["accum_out"] = accum_out.ap
        return self.op(eng, lambda e: e.activation(out.ap, in_.ap, func, bias=a(bias), scale=a(scale), **kw), rd, wr)

    def tt(self, out, in0, in1, op, eng="dve"):
        return self.op(eng, lambda e: e.tensor_tensor(out.ap, in0.ap, in1.ap, op), [in0.key, in1.key], [out.key])

    def ts(self, out, in0, s1, s2=None, op0=ALU.mult, op1=None, accum_out=None, eng="dve"):
        a = self._a
        rd = self._ks([in0, s1, s2])
        wr = self._ks([out, accum_out])
        kw = {}
        if op1 is not None:
            kw["op1"] = op1
        if accum_out is not None:
            kw["accum_out"] = accum_out.ap
        return self.op(eng, lambda e: e.tensor_scalar(out.ap, in0.ap, a(s1), a(s2) if s2 is not None else None, op0, **kw), rd, wr)

    def stt(self, out, in0, scalar, in1, op0, op1, eng="dve"):
        a = self._a
        rd = self._ks([in0, scalar, in1])
        return self.op(eng, lambda e: e.scalar_tensor_tensor(out.ap, in0.ap, a(scalar), in1.ap, op0, op1), rd, [out.key])

    def copy(self, out, in_, eng="dve"):
        return self.op(eng, lambda e: e.tensor_copy(out.ap, in_.ap), [in_.key], [out.key])

    def memset(self, out, val, eng="dve"):
        return self.op(eng, lambda e: e.memset(out.ap, val), [], [out.key])

    def recip(self, out, in_):
        return self.op("dve", lambda e: e.reciprocal(out.ap, in_.ap), [in_.key], [out.key])

    def reduce(self, out, in_, op=ALU.add, axis=AX.X, eng="dve"):
        return self.op(eng, lambda e: e.tensor_reduce(out.ap, in_.ap, axis, op), [in_.key], [out.key])

    def flush(self, name=None):
        ops = self.ops
        self.ops = []
        if not ops:
            return
        self.phase_names.append(name or sys._getframe(1).f_code.co_name)
        nc = self.nc
        last_w = {}
        readers = {}
        for i, o in enumerate(ops):
            deps = set()
            for k in o.reads:
                if k in last_w:
                    deps.add(last_w[k])
            for k in o.writes:
                if k in last_w:
                    deps.add(last_w[k])
                for r in readers.get(k, ()):
                    deps.add(r)
            deps.discard(i)
            o.deps = [d for d in deps if not (o.eng == "pe" and ops[d].eng == "pe" and not ops[d].dma and not o.dma)]
            for k in o.reads:
                readers.setdefault(k, []).append(i)
            for k in o.writes:
                last_w[k] = i
                readers[k] = []
            o.marked = False
            o.tok = None
            o.waits = []
        for o in ops:
            for d in o.deps:
                ops[d].marked = True
        cnt = {e: 0 for e in ENGS}
        dcum = {}
        dnext = {q: 0 for q in self.dsems}
        dprev = {}
        for o in ops:
            if o.dma:
                q = o.eng
                pool = self.dsems[q]
                s = pool[dnext[q] % NDMASEM]
                dnext[q] += 1
                if s in dprev:
                    o.waits.append(dprev[s])
                dcum[s] = dcum.get(s, 0) + 16
                o.tok = (s, dcum[s], 16)
                dprev[s] = (s, dcum[s])
            elif o.marked:
                cnt[o.eng] += 1
                o.tok = (self.sems[o.eng], cnt[o.eng], 1)
        for o in ops:
            for d in o.deps:
                t = ops[d].tok
                o.waits.append((t[0], t[1]))
        per = {e: [o for o in ops if o.eng == e] for e in ENGS}
        with nc.Block() as block:
            for e in ENGS:
                lst = per[e]
                if not lst:
                    continue

                def body(eng, lst=lst, e=e):
                    known = {}
                    for o in lst:
                        for (s, v) in o.waits:
                            if known.get(s, 0) < v:
                                eng.wait_ge(s, v)
                                known[s] = v
                        ins = o.fn(eng)
                        if o.tok is not None:
                            ins.then_inc(o.tok[0], o.tok[2])
                    if e in self.dsems:
                        for s in self.dsems[e]:
                            if s in dprev and known.get(s, 0) < dprev[s][1]:
                                eng.wait_ge(s, dprev[s][1])

                getattr(block, BLOCKNAME[e])(body)
        with nc.Block() as block:
            @block.sync
            def _(eng):
                for s in self.allsems:
                    eng.sem_clear(s)
        self.nphase += 1


def prefetch_loop(n, load_fn, body_fn, pf=2):
    for i in range(n + pf):
        if i < n:
            load_fn(i)
        if i >= pf:
            body_fn(i - pf)


def prefetch_pipelined(n, load_fn, body_gen, pf=2, G=2):
    nl = 0
    for g0 in range(0, n, G):
        while nl < min(n, g0 + G + pf):
            load_fn(nl)
            nl += 1
        alive = [body_gen(i) for i in range(g0, min(n, g0 + G))]
        while alive:
            nxt = []
            for g in alive:
                try:
                    next(g)
                    nxt.append(g)
                except StopIteration:
                    pass
            alive = nxt

L = 8192
LC = 256
T = L + LC
D = 1024
NIN = 9488
NPROJ = 5392
DEPTH = 2
EPS = 1e-6
TB = [(i * 512, 512) for i in range(16)] + [(8192, 256)]

DEBUG = {}


def dram(nc, name, shape, dtype=F32, kind=None):
    if kind is None and name in DEBUG:
        kind = "ExternalOutput"
    if kind is None:
        t = nc.dram_tensor(name, list(shape), dtype)
    else:
        t = nc.dram_tensor(name, list(shape), dtype, kind=kind)
    return Buf(t.ap(), name, track=False)


class K:
    pass


def declare(nc):
    k = K()
    inp = lambda n, s: dram(nc, n, s, F32, "ExternalInput")
    k.x = inp("x", [L, D])
    k.ctx = inp("ctx", [LC, D])
    k.cvec = inp("cvec", [2, D])
    k.w_mod = inp("w_mod", [DEPTH, D, 6 * D])
    k.b_mod = inp("b_mod", [DEPTH, 6 * D])
    k.ln1_g = inp("ln1_g", [DEPTH, D])
    k.ln2_g = inp("ln2_g", [DEPTH, D])
    k.w_in = inp("w_in", [DEPTH, D, NIN])
    k.w_up = inp("w_up", [DEPTH, 4, 512, D])
    k.w_out = inp("w_out", [DEPTH, D, D])
    k.mlp_w1 = inp("mlp_w1", [DEPTH, D, 4 * D])
    k.mlp_b1 = inp("mlp_b1", [DEPTH, 4 * D])
    k.mlp_w2 = inp("mlp_w2", [DEPTH, 4 * D, D])
    k.mlp_b2 = inp("mlp_b2", [DEPTH, D])
    k.final_g = inp("final_g", [1, D])
    k.ident = inp("ident", [128, 128])
    k.y = dram(nc, "y", [L, D], F32, "ExternalOutput")
    k.xs = dram(nc, "xs", [T, D])
    k.mod = dram(nc, "mod", [DEPTH, 2, 6 * D])
    k.hT = dram(nc, "hT", [D, T], BF16)
    k.hyT = dram(nc, "hyT", [1536, T])
    k.ga = dram(nc, "ga", [T, 1024])
    k.mlqkT = dram(nc, "mlqkT", [1024, T])
    k.mlvo = dram(nc, "mlvo", [T, 1024 + 16])
    k.wa = dram(nc, "wa", [T, 768])
    k.ga_q_g = inp("ga_q_g", [DEPTH, 128])
    k.ga_k_g = inp("ga_k_g", [DEPTH, 128])
    k.wa_sink = inp("wa_sink", [DEPTH, 8])
    k.rc128 = inp("rc128", [L, 64])
    k.rs128 = inp("rs128", [L, 64])
    k.rc64 = inp("rc64", [L, 32])
    k.rs64 = inp("rs64", [L, 32])
    k.mlo = inp("mlo", [128, 128])
    k.mhi = inp("mhi", [128, 128])
    k.gaQT = dram(nc, "gaQT", [4, 128, T], BF16)
    k.gaKT = dram(nc, "gaKT", [2, 128, T], BF16)
    k.gaV = dram(nc, "gaV", [T, 256], BF16)
    k.waQT = dram(nc, "waQT", [128, 4, T], BF16)
    k.waKT = dram(nc, "waKT", [128, T], BF16)
    k.waV = dram(nc, "waV", [T, 128], BF16)
    k.yT = dram(nc, "yT", [4, 512, T], BF16)
    return k


def load_ident(P, es, k):
    idf = P.sb(es, "idf", [128, 128], F32)
    idb = P.sb(es, "idb", [128, 128], BF16)
    P.dma("sp", idf[:, :], k.ident[:, :])
    P.copy(idb[:, :], idf[:, :])
    return idf, idb


_STG = [0]


def mk_stage(P, es, n=3, w=2048):
    return P.sbpool(es, "stg", n, [128, w], F32)


def load_cast(P, stg, dst, src, ncol, q="sp"):
    i = _STG[0]
    _STG[0] += 1
    st = stg[i % len(stg)]
    P.dma(q, st[:, 0:ncol], src)
    if i % 2 == 0:
        P.act(dst, st[:, 0:ncol], AF.Copy)
    else:
        P.copy(dst, st[:, 0:ncol])

def phase_mod(P, k):
    nc = P.nc
    with ExitStack() as es:
        cv = P.sb(es, "cv", [128, 2, 8], F32)
        sT = P.sb(es, "sT", [128, 8, 2], BF16)
        P.dma("sp", cv[:, :, :], V(k.cvec.h.rearrange("j (p c) -> p j c", c=8), None))
        sg = P.sb(es, "sg", [128, 2, 8], F32)
        P.act(sg[:, :, :], cv[:, :, :], AF.Sigmoid)
        for j in range(2):
            P.tt(sT[:, :, j], cv[:, j, :], sg[:, j, :], ALU.mult)
        wts = P.sbpool(es, "wm", 2, [128, 8, 2048], BF16)
        stg = mk_stage(P, es)
        bm = P.sb(es, "bm", [2, 6 * D], F32)
        ot = P.sb(es, "ot", [2, 6 * D], F32)
        pss = [P.ps(es, f"pm{i}") for i in range(2)]
        n = 0
        for l in range(DEPTH):
            P.dma("sp", bm[:, :], V(k.b_mod.h[l:l + 1, :].broadcast(0, 2) if False else k.b_mod.h[l:l + 1, :].to_broadcast([2, 6 * D]), None))
            for fb in range(3):
                w = wts[n % 2]
                wv = k.w_mod.h[l].rearrange("(p c) f -> p c f", c=8)
                for c in range(8):
                    load_cast(P, stg, w[:, c, :], V(wv[:, c, fb * 2048:(fb + 1) * 2048], None), 2048)
                for s in range(4):
                    ps = pss[s % 2]
                    for c in range(8):
                        P.mm(ps[0:2, :], sT[:, c, :], w[:, c, s * 512:(s + 1) * 512], start=(c == 0), stop=(c == 7))
                    f0 = fb * 2048 + s * 512
                    P.tt(ot[:, f0:f0 + 512], ps[0:2, :], bm[:, f0:f0 + 512], ALU.add)
                n += 1
            P.dma("sp", k.mod[l, :, :], ot[:, :])
            P.flush()


def phase_norm(P, k, l, src, g_dram, sc_idx, sh_idx, ntiles_lat=64, do_ctx=True):
    with ExitStack() as es:
        idf, idb = load_ident(P, es, k)
        A = P.sbpool(es, "A", 2, [128, D], F32)
        Bt = P.sbpool(es, "B", 2, [128, D], F32)
        gt = P.sb(es, "gt", [128, D], F32)
        P.dma("sp", gt[:, :], V(g_dram.h[l:l + 1, :].to_broadcast([128, D]), None))
        for j in range(2):
            P.dma("sp", A[j][:, :], V(k.mod.h[l, j:j + 1, sc_idx * D:(sc_idx + 1) * D].to_broadcast([128, D]), None))
            P.dma("sp", Bt[j][:, :], V(k.mod.h[l, j:j + 1, sh_idx * D:(sh_idx + 1) * D].to_broadcast([128, D]), None))
            P.stt(A[j][:, :], A[j][:, :], 1.0, gt[:, :], ALU.add, ALU.mult)
        xt = P.sbpool(es, "xt", 4, [128, D], F32)
        junk2 = P.sbpool(es, "junk", 2, [128, D], F32)
        ssq = P.sbpool(es, "ssq", 3, [128, 1], F32)
        rstd = P.sbpool(es, "rstd", 3, [128, 1], F32)
        t1 = P.sbpool(es, "t1", 2, [128, D], F32)
        hb = P.sbpool(es, "hb", 2, [128, D], BF16)
        hTs = P.sbpool(es, "hTs", 3, [128, 8, 128], BF16)
        pst = [P.ps(es, f"pt{i}", [128, 8, 128], BF16) for i in range(2)]
        tiles = list(range(ntiles_lat)) + ([64, 65] if do_ctx else [])

        def load(n):
            ti = tiles[n]
            P.dma("sp", xt[n % 4][:, :], src[ti * 128:(ti + 1) * 128, :])

        def body(n):
            ti = tiles[n]
            j = 0 if ti < 64 else 1
            x_ = xt[n % 4]
            P.act(junk2[n % 2][:, :], x_[:, :], AF.Square, accum_out=ssq[n % 3][:, :])
            yield
            P.act(rstd[n % 3][:, :], ssq[n % 3][:, :], AF.Sqrt, bias=EPSB[0][:, :], scale=1.0 / D)
            yield
            P.recip(rstd[n % 3][:, :], rstd[n % 3][:, :])
            yield
            P.stt(t1[n % 2][:, :], x_[:, :], rstd[n % 3][:, :], A[j][:, :], ALU.mult, ALU.mult)
            yield
            P.tt(hb[n % 2][:, :], t1[n % 2][:, :], Bt[j][:, :], ALU.add)
            yield
            ps = pst[n % 2]
            for c in range(8):
                P.transpose(ps[:, c, :], hb[n % 2][:, c * 128:(c + 1) * 128], idb[:, :])
            yield
            hs = hTs[n % 3]
            P.copy(hs[:, :, :], ps[:, :, :])
            P.dma("sp", V(k.hT.h.rearrange("(c p) t -> p c t", p=128)[:, :, ti * 128:(ti + 1) * 128], None), hs[:, :, :])

        prefetch_pipelined(len(tiles), load, body, 2, 2)
        P.flush()


EPSB = [None]


def make_consts(P, es):
    e = P.sb(es, "epsb", [128, 1], F32)
    P.memset(e[:, :], EPS)
    EPSB[0] = e
    P.flush()


def phase_inproj(P, k, l):
    groups = [
        (0, 1536, "fm", k.hyT, 0),
        (1536, 1024, "tm", k.ga, 0),
        (2560, 1024, "fm", k.mlqkT, 0),
        (3584, 1024 + 16, "tm", k.mlvo, 0),
        (4624, 768, "tm", k.wa, 0),
    ]
    with ExitStack() as es:
        W = P.sb(es, "Win", [128, 8, NPROJ], BF16)
        wv = k.w_in.h[l].rearrange("(c p) n -> p c n", p=128)
        stg = mk_stage(P, es)
        for c in range(8):
            for h in range(4):
                load_cast(P, stg, W[:, c, h * 1348:(h + 1) * 1348], V(wv[:, c, h * 1348:(h + 1) * 1348], None), 1348)
        hts = P.sbpool(es, "ht", 2, [128, 8, 512], BF16)
        pss = [P.ps(es, f"pi{i}") for i in range(4)]
        osb = P.sbpool(es, "osb", 4, [128, 512], F32)
        hTv = k.hT.h.rearrange("(c p) t -> p c t", p=128)
        n = 0
        P.dma("sp", hts[0][:, :, 0:TB[0][1]], V(hTv[:, :, TB[0][0]:TB[0][0] + TB[0][1]], None))
        for bi, (t0, tn) in enumerate(TB):
            ht = hts[bi % 2]
            if bi + 1 < len(TB):
                t0n, tnn = TB[bi + 1]
                P.dma("sp", hts[(bi + 1) % 2][:, :, 0:tnn], V(hTv[:, :, t0n:t0n + tnn], None))
            for (c0, ncol, mode, dst, doff) in groups:
                if mode == "fm":
                    for cc in range(0, ncol, 128):
                        ps = pss[n % 4]
                        for c in range(8):
                            P.mm(ps[:, 0:tn], W[:, c, c0 + cc:c0 + cc + 128], ht[:, c, 0:tn], start=(c == 0), stop=(c == 7))
                        o = osb[n % 4]
                        P.act(o[:, 0:tn], ps[:, 0:tn], AF.Copy)
                        P.dma("sp", dst[doff + cc:doff + cc + 128, t0:t0 + tn], o[:, 0:tn])
                        n += 1
                else:
                    for ts_ in range(0, tn, 128):
                        for cc in range(0, ncol, 512):
                            w_ = min(512, ncol - cc)
                            ps = pss[n % 4]
                            for c in range(8):
                                P.mm(ps[:, 0:w_], ht[:, c, ts_:ts_ + 128], W[:, c, c0 + cc:c0 + cc + w_], start=(c == 0), stop=(c == 7))
                            o = osb[n % 4]
                            P.act(o[:, 0:w_], ps[:, 0:w_], AF.Copy)
                            P.dma("sp", dst[t0 + ts_:t0 + ts_ + 128, doff + cc:doff + cc + w_], o[:, 0:w_])
                            n += 1
        P.flush()


def phase_init(P, k):
    for i in range(8):
        P.dma("sp", k.xs[i * 1024:(i + 1) * 1024, :], k.x[i * 1024:(i + 1) * 1024, :])
    P.dma("sp", k.xs[L:T, :], k.ctx[:, :])
    P.flush()


def _rope_tables(d):
    quarter = d // 4
    t = np.arange(L)
    row = (t // 64).astype(np.float64)
    col = (t % 64).astype(np.float64)
    inv = 10000.0 ** (-np.arange(quarter, dtype=np.float64) / quarter)
    ang = np.stack([row[:, None] * inv[None, :], col[:, None] * inv[None, :]], axis=1)
    return (np.cos(ang).reshape(L, 2 * quarter).astype(np.float32),
            np.sin(ang).reshape(L, 2 * quarter).astype(np.float32))


def _make_consts():
    c = {}
    c["rc128"], c["rs128"] = _rope_tables(128)
    c["rc64"], c["rs64"] = _rope_tables(64)
    p = np.arange(128)[:, None]
    f = np.arange(128)[None, :]
    c["mlo"] = (p >= f).astype(np.float32)
    c["mhi"] = (p <= f).astype(np.float32)
    return c


CONSTS = _make_consts()


NT = T // 128
NTL = L // 128


def bcast(v, shape, axis):
    return V(v.ap.unsqueeze(axis).to_broadcast(list(shape)), v.key)


def rope_apply(P, out, xin, Ct, St, nh, q, tmp):
    xv = lambda b, half: V(b.h[:, :, :].rearrange("p h (b f j) -> p h b f j", b=2, f=2)[:, :, :, half, :], b.key)
    Cb = V(Ct.h[:, :, :].unsqueeze(1).to_broadcast([128, nh, 2, q]), Ct.key)
    Sb = V(St.h[:, :, :].unsqueeze(1).to_broadcast([128, nh, 2, q]), St.key)
    a, b_ = xv(xin, 0), xv(xin, 1)
    t1, t2, t3, t4 = [t[:, :, :, :] for t in tmp]
    P.tt(t1, a, Cb, ALU.mult)
    P.tt(t2, b_, Sb, ALU.mult)
    P.tt(t3, a, Sb, ALU.mult)
    P.tt(t4, b_, Cb, ALU.mult)
    P.tt(xv(out, 0), t1, t2, ALU.subtract)
    P.tt(xv(out, 1), t3, t4, ALU.add)


def phase_ga_prep(P, k, l):
    with ExitStack() as es:
        idf, idb = load_ident(P, es, k)
        gq = P.sb(es, "gq", [128, 6, 128], F32)
        for h in range(4):
            P.dma("sp", gq[:, h, :], V(k.ga_q_g.h[l:l + 1, :].to_broadcast([128, 128]), None))
        for h in range(2):
            P.dma("sp", gq[:, 4 + h, :], V(k.ga_k_g.h[l:l + 1, :].to_broadcast([128, 128]), None))
        P.ts(gq[:, 0:4, :], gq[:, 0:4, :], 128 ** -0.5, None, ALU.mult)
        xt = P.sbpool(es, "gx", 4, [128, 1024], F32)
        sq2 = P.sbpool(es, "gsq", 2, [128, 6, 128], F32)
        ssq = P.sbpool(es, "gss", 2, [128, 6], F32)
        rstd = P.sbpool(es, "grs", 2, [128, 6], F32)
        xn = P.sbpool(es, "gxn", 2, [128, 6, 128], F32)
        xr = P.sbpool(es, "gxr", 2, [128, 6, 128], BF16)
        vb = P.sbpool(es, "gvb", 2, [128, 256], BF16)
        Ct = P.sbpool(es, "gC", 4, [128, 2, 32], F32)
        St = P.sbpool(es, "gS", 4, [128, 2, 32], F32)
        tmp2 = [[P.sb(es, f"gt{j}_{i}", [128, 6, 2, 32], F32) for i in range(4)] for j in range(2)]
        pst = [P.ps(es, f"gpt{i}", [128, 6, 128], BF16) for i in range(2)]
        qkT = P.sbpool(es, "gqkT", 2, [128, 6, 128], BF16)
        def load(ti):
            t0 = ti * 128
            P.dma("sp", xt[ti % 4][:, :], k.ga[t0:t0 + 128, :])
            if ti < NTL:
                P.dma("sp", Ct[ti % 4][:, :, :], V(k.rc128.h[t0:t0 + 128, :].rearrange("p (b j) -> p b j", b=2), None))
                P.dma("sp", St[ti % 4][:, :, :], V(k.rs128.h[t0:t0 + 128, :].rearrange("p (b j) -> p b j", b=2), None))

        def body(ti):
            t0 = ti * 128
            x_ = xt[ti % 4]
            qk = V(x_.h[:, 0:768].rearrange("p (h d) -> p h d", d=128), x_.key)
            sq_ = sq2[ti % 2]
            P.tt(sq_[:, :, :], qk, qk, ALU.mult)
            yield
            ss, rs = ssq[ti % 2], rstd[ti % 2]
            P.reduce(ss[:, :], sq_[:, :, :])
            yield
            P.act(rs[:, :], ss[:, :], AF.Sqrt, bias=EPSB[0][:, :], scale=1.0 / 128)
            yield
            P.recip(rs[:, :], rs[:, :])
            yield
            n_ = xn[ti % 2]
            P.tt(n_[:, :, :], qk, bcast(rs[:, :], [128, 6, 128], 2), ALU.mult)
            yield
            P.tt(n_[:, :, :], n_[:, :, :], gq[:, :, :], ALU.mult)
            yield
            r_ = xr[ti % 2]
            if ti < NTL:
                rope_apply(P, r_, n_, Ct[ti % 4], St[ti % 4], 6, 32, tmp2[ti % 2])
            else:
                P.copy(r_[:, :, :], n_[:, :, :])
            yield
            ps = pst[ti % 2]
            for h in range(6):
                P.transpose(ps[:, h, :], r_[:, h, :], idb[:, :])
            yield
            o = qkT[ti % 2]
            P.act(o[:, :, :], ps[:, :, :], AF.Copy)
            P.dma("sp", V(k.gaQT.h[:, :, t0:t0 + 128].rearrange("h d t -> d h t"), None), o[:, 0:4, :])
            P.dma("sp", V(k.gaKT.h[:, :, t0:t0 + 128].rearrange("h d t -> d h t"), None), o[:, 4:6, :])
            v_ = vb[ti % 2]
            P.act(v_[:, :], x_[:, 768:1024], AF.Copy)
            P.dma("sp", k.gaV[t0:t0 + 128, :], v_[:, :])

        prefetch_pipelined(NT, load, body, 2, 2)
        P.flush()


def attn_core(P, es, QTsrc, KT, Vaug, dh, qblocks, kblocks_fn, nsub_heads, finalize, psS, psO, scale_mask=None):
    raise NotImplementedError


def emit_skewed(units, skew):
    n = len(units)
    for i in range(n + skew):
        if i < n:
            units[i][0]()
        if i >= skew:
            units[i - skew][1]()


def phase_ga_attn(P, k, l, with_ctx_out):
    SK = 2
    with ExitStack() as es:
        KTs = P.sbpool(es, "aKT", 2, [128, T], BF16)
        Vts = P.sbpool(es, "aVt", 2, [128, NT, 128], BF16)
        ones = P.sb(es, "aones", [128, 128], F32)
        P.memset(ones[:, :], 1.0)
        QT = P.sbpool(es, "aQT", 3, [128, 512], BF16)
        PT = P.sbpool(es, "aPT", SK + 3, [128, 512], BF16)
        NACC = 3
        pairs = P.sbpool(es, "apair", 3, [128, 512], BF16)
        pprev = [None]
        accs = [P.sbpool(es, f"aacc{i}", NACC, [128, 512], F32) for i in range(2)]
        psS = [P.ps(es, f"apS{i}") for i in range(SK + 1)]
        psO = [P.ps(es, f"apO{i}") for i in range(2)]
        psD = P.ps(es, "apD")
        rec = P.sbpool(es, "arec", 2, [128, 512], F32)
        ob = P.sbpool(es, "aob", 2, [128, 512], BF16)
        units = []
        qloads = []
        nq = 0
        nu = 0
        for g in range(2):
            KT, Vt = KTs[g], Vts[g]
            P.dma("sp", KT[:, :], k.gaKT[g, :, :])
            P.dma("sp", Vt[:, :, :], V(k.gaV.h[:, g * 128:(g + 1) * 128].rearrange("(n p) c -> p n c", p=128), None))
            for hh in range(2):
                h = g * 2 + hh
                qbl = [(i * 512, 512, list(range(NT))) for i in range(L // 512)]
                if with_ctx_out:
                    qbl.append((L, LC, list(range(NTL, NT))))
                for (q0, qn, kbs) in qbl:
                    q_ = QT[nq % 3]
                    qloads.append(lambda q_=q_, qn=qn, h=h, q0=q0: P.dma("sp", q_[:, 0:qn], k.gaQT[h, :, q0:q0 + qn]))
                    pO = psO[nq % 2]
                    ac = accs[nq % 2]
                    r_, o_ = rec[nq % 2], ob[nq % 2]
                    nkb = len(kbs)
                    for ki, kb in enumerate(kbs):
                        pS = psS[nu % (SK + 1)]
                        p_ = PT[nu % (SK + 3)]
                        nu += 1

                        def front(ki=ki, kb=kb, pS=pS, p_=p_, q_=q_, q0=q0, qn=qn, h=h, KT=KT, nq=nq):
                            if ki == 0:
                                if nq == 0:
                                    qloads[0]()
                                if nq + 1 < len(qloads):
                                    qloads[nq + 1]()
                            P.mm(pS[:, 0:qn], KT[:, kb * 128:(kb + 1) * 128], q_[:, 0:qn])
                            P.act(p_[:, 0:qn], pS[:, 0:qn], AF.Exp)

                        def back(ki=ki, kb=kb, p_=p_, qn=qn, pO=pO, ac=ac, nkb=nkb, Vt=Vt, r_=r_, o_=o_, h=h, q0=q0):
                            P.mm(pO[:, 0:qn], Vt[:, kb, :], p_[:, 0:qn], start=(ki == 0), stop=(ki == nkb - 1))
                            if ki % 2 == 1:
                                pr = pairs[(ki // 2) % 3]
                                P.tt(pr[:, 0:qn], pprev[0][:, 0:qn], p_[:, 0:qn], ALU.add)
                                a_ = ac[(ki // 2) % NACC]
                                if ki // 2 < NACC:
                                    P.copy(a_[:, 0:qn], pr[:, 0:qn])
                                else:
                                    P.tt(a_[:, 0:qn], a_[:, 0:qn], pr[:, 0:qn], ALU.add)
                            pprev[0] = p_
                            if ki == nkb - 1:
                                na = min(NACC, nkb // 2)
                                for i in range(na):
                                    P.mm(psD[:, 0:qn], ones[:, :], ac[i][:, 0:qn], start=(i == 0), stop=(i == na - 1))
                                P.recip(r_[:, 0:qn], psD[:, 0:qn])
                                P.tt(o_[:, 0:qn], pO[:, 0:qn], r_[:, 0:qn], ALU.mult)
                                P.dma("sp", k.yT[1, h * 128:(h + 1) * 128, q0:q0 + qn], o_[:, 0:qn])

                        units.append((front, back))
                    nq += 1
        emit_skewed(units, SK)
        P.flush()


def phase_wa_prep(P, k, l):
    with ExitStack() as es:
        idf, idb = load_ident(P, es, k)
        xt = P.sbpool(es, "wx", 4, [128, 768], F32)
        xs_ = P.sbpool(es, "wxs", 2, [128, 10, 64], F32)
        xr = P.sbpool(es, "wxr", 2, [128, 10, 64], BF16)
        vb = P.sbpool(es, "wvb", 2, [128, 128], BF16)
        Ct = P.sbpool(es, "wC", 4, [128, 2, 16], F32)
        St = P.sbpool(es, "wS", 4, [128, 2, 16], F32)
        tmp2 = [[P.sb(es, f"wt{j}_{i}", [128, 10, 2, 16], F32) for i in range(4)] for j in range(2)]
        pst = [P.ps(es, f"wpt{i}", [128, 5, 128], BF16) for i in range(2)]
        qkT = P.sbpool(es, "wqkT", 2, [128, 5, 128], BF16)
        def load(ti):
            t0 = ti * 128
            P.dma("sp", xt[ti % 4][:, :], k.wa[t0:t0 + 128, :])
            if ti < NTL:
                P.dma("sp", Ct[ti % 4][:, :, :], V(k.rc64.h[t0:t0 + 128, :].rearrange("p (b j) -> p b j", b=2), None))
                P.dma("sp", St[ti % 4][:, :, :], V(k.rs64.h[t0:t0 + 128, :].rearrange("p (b j) -> p b j", b=2), None))

        def body(ti):
            t0 = ti * 128
            x_ = xt[ti % 4]
            qk = V(x_.h[:, 0:640].rearrange("p (h d) -> p h d", d=64), x_.key)
            s_ = xs_[ti % 2]
            P.ts(V(s_.h[:, 0:8, :].rearrange("p (j g) d -> p g j d", g=2), s_.key),
                 V(x_.h[:, 0:512].rearrange("p (g j d) -> p g j d", g=2, j=4), x_.key), 64 ** -0.5, None, ALU.mult)
            P.act(s_[:, 8:10, :], V(qk.ap[:, 8:10, :], x_.key), AF.Copy)
            yield
            r_ = xr[ti % 2]
            if ti < NTL:
                rope_apply(P, r_, s_, Ct[ti % 4], St[ti % 4], 10, 16, tmp2[ti % 2])
            else:
                P.copy(r_[:, :, :], s_[:, :, :])
            yield
            ps = pst[ti % 2]
            for j in range(4):
                src = V(r_.h[:, 2 * j:2 * j + 2, :].rearrange("p h d -> p (h d)"), r_.key)
                P.transpose(ps[:, j, :], src, idb[:, :])
            P.transpose(ps[:, 4, :], V(r_.h[:, 8:10, :].rearrange("p h d -> p (h d)"), r_.key), idb[:, :])
            yield
            o = qkT[ti % 2]
            P.act(o[:, :, :], ps[:, :, :], AF.Copy)
            P.dma("sp", k.waQT[:, :, t0:t0 + 128], o[:, 0:4, :])
            P.dma("sp", k.waKT[:, t0:t0 + 128], o[:, 4, :])
            v_ = vb[ti % 2]
            P.act(v_[:, :], x_[:, 640:768], AF.Copy)
            P.dma("sp", k.waV[t0:t0 + 128, :], v_[:, :])

        prefetch_pipelined(NT, load, body, 2, 2)
        P.flush()


def phase_wa_attn(P, k, l, with_ctx_out):
    SK = 2
    with ExitStack() as es:
        idf, idb = load_ident(P, es, k)
        KT = P.sb(es, "bKT", [128, T], BF16)
        Va = P.sb(es, "bVa", [128, NT, 2, 65], BF16)
        P.dma("sp", KT[:, :], k.waKT[:, :])
        for g in range(2):
            P.dma("sp", Va[:, :, g, 0:64], V(k.waV.h[:, g * 64:(g + 1) * 64].rearrange("(n p) c -> p n c", p=128), None))
        P.memset(Va[:, :, :, 64:65], 1.0)
        mf = P.sb(es, "bmf", [128, 2, 128], F32)
        mb = P.sb(es, "bmb", [128, 2, 128], BF16)
        P.dma("sp", mf[:, 0, :], k.mlo[:, :])
        P.dma("sp", mf[:, 1, :], k.mhi[:, :])
        P.copy(mb[:, :, :], mf[:, :, :])
        esk = P.sb(es, "besk", [128, 8], F32)
        P.dma("sp", esk[:, :], V(k.wa_sink.h[l:l + 1, :].to_broadcast([128, 8]), None))
        P.act(esk[:, :], esk[:, :], AF.Exp)
        QT = P.sbpool(es, "bQT", 3, [128, 4, 128], BF16)
        PT = P.sbpool(es, "bPT", SK + 3, [128, 4, 128], BF16)
        psS = [P.ps(es, f"bpS{i}") for i in range(SK + 1)]
        psO = [P.ps(es, f"bpO{i}") for i in range(4)]
        psT = P.ps(es, "bpT", [128, 4, 128], BF16)
        den = P.sbpool(es, "bden", 8, [128, 1], F32)
        ob = P.sbpool(es, "bob", 2, [128, 8, 64], BF16)
        oT = P.sbpool(es, "boT", 2, [128, 4, 128], BF16)
        qtiles = list(range(NTL)) + (list(range(NTL, NT)) if with_ctx_out else [])
        units = []
        nu = 0
        for qi, ti in enumerate(qtiles):
            t0 = ti * 128
            q_ = QT[qi % 3]
            if ti < NTL:
                kbs = [(ti - 1, 0)] if ti > 0 else []
                kbs.append((ti, None))
                if ti < NTL - 1:
                    kbs.append((ti + 1, 1))
                kbs += [(NTL, None), (NTL + 1, None)]
            else:
                kbs = [(NTL, None), (NTL + 1, None)]
            o_ = ob[qi % 2]
            ot = oT[qi % 2]
            nkb = len(kbs)
            for g in range(2):
                for ki, (kb, mk) in enumerate(kbs):
                    pS = psS[nu % (SK + 1)]
                    p_ = PT[nu % (SK + 3)]
                    nu += 1

                    def front(g=g, ki=ki, kb=kb, mk=mk, pS=pS, p_=p_, q_=q_, t0=t0, qi=qi):
                        if g == 0 and ki == 0:
                            if qi == 0:
                                P.dma("sp", QT[0][:, :, :], k.waQT[:, :, qtiles[0] * 128:qtiles[0] * 128 + 128])
                            if qi + 1 < len(qtiles):
                                tn_ = qtiles[qi + 1] * 128
                                P.dma("sp", QT[(qi + 1) % 3][:, :, :], k.waQT[:, :, tn_:tn_ + 128])
                        P.mm(pS[:, :], KT[g * 64:(g + 1) * 64, kb * 128:(kb + 1) * 128],
                             V(q_.h[g * 64:(g + 1) * 64, :, :].rearrange("p j q -> p (j q)"), q_.key))
                        pv = V(p_.h[:, :, :].rearrange("p j q -> p (j q)"), p_.key)
                        P.act(pv, pS[:, :], AF.Exp)
                        if mk is not None:
                            P.tt(p_[:, :, :], p_[:, :, :], bcast(mb[:, mk, :], [128, 4, 128], 1), ALU.mult)

                    def back(g=g, ki=ki, kb=kb, p_=p_, nkb=nkb, o_=o_, ot=ot, t0=t0):
                        for j in range(4):
                            P.mm(psO[j][:, 0:65], p_[:, j, :], Va[:, kb, g, :], start=(ki == 0), stop=(ki == nkb - 1))
                        if ki == nkb - 1:
                            for j in range(4):
                                h = g * 4 + j
                                d_ = den[h]
                                P.tt(d_[:, :], psO[j][:, 64:65], esk[:, h:h + 1], ALU.add)
                                P.recip(d_[:, :], d_[:, :])
                                P.ts(o_[:, h, :], psO[j][:, 0:64], d_[:, :], None, ALU.mult)
                            if g == 1:
                                for c in range(4):
                                    P.transpose(psT[:, c, :], V(o_.h[:, 2 * c:2 * c + 2, :].rearrange("p h d -> p (h d)"), o_.key), idb[:, :])
                                P.act(ot[:, :, :], psT[:, :, :], AF.Copy)
                                P.dma("sp", V(k.yT.h[3, :, t0:t0 + 128].rearrange("(c p) t -> p c t", p=128), None), ot[:, :, :])

                    units.append((front, back))
        emit_skewed(units, SK)
        P.flush()

SEGS = [(0, L), (L, T)]


def conv3_fm(P, u, z, w, b, segs):
    P.act(u[:, 0:T], z[:, 0:T], AF.Identity, bias=b[:, 0:1], scale=w[:, 1:2])
    for (s, e) in segs:
        P.stt(u[:, s + 1:e], z[:, s:e - 1], w[:, 0:1], u[:, s + 1:e], ALU.mult, ALU.add)
        P.stt(u[:, s:e - 1], z[:, s + 1:e], w[:, 2:3], u[:, s:e - 1], ALU.mult, ALU.add)


def conv3_fm_gen(P, u, z, w, b, segs):
    P.act(u[:, 0:T], z[:, 0:T], AF.Identity, bias=b[:, 0:1], scale=w[:, 1:2])
    yield
    for (s, e) in segs:
        P.stt(u[:, s + 1:e], z[:, s:e - 1], w[:, 0:1], u[:, s + 1:e], ALU.mult, ALU.add)
        yield
        P.stt(u[:, s:e - 1], z[:, s + 1:e], w[:, 2:3], u[:, s:e - 1], ALU.mult, ALU.add)
        yield


def declare_ml(nc, k):
    inp = lambda n, s: dram(nc, n, s, F32, "ExternalInput")
    k.ml_conv_w = inp("ml_conv_w", [DEPTH, 3, 1024])
    k.ml_conv_b = inp("ml_conv_b", [DEPTH, 1024])
    k.ml_gate_b = inp("ml_gate_b", [DEPTH, 16])
    k.ml_norm_g = inp("ml_norm_g", [DEPTH, 512])
    k.mlQKT = dram(nc, "mlQKT", [1024, T], BF16)
    k.mlKtm = dram(nc, "mlKtm", [T, 512], BF16)
    k.mlH = dram(nc, "mlH", [2, T, 512])
    k.mlVa = dram(nc, "mlVa", [T, 4, 129], BF16)


def phase_ml_prep(P, k, l):
    with ExitStack() as es:
        idf, idb = load_ident(P, es, k)
        zt = P.sbpool(es, "mz", 2, [128, T], F32)
        ut = P.sbpool(es, "mu", 2, [128, T], F32)
        ab = P.sbpool(es, "ma", 2, [128, T], BF16)
        wt = P.sbpool(es, "mw", 2, [128, 3], F32)
        bt = P.sbpool(es, "mb", 2, [128, 1], F32)
        pst = [P.ps(es, f"mpt{i}", [128, 4, 128], BF16) for i in range(2)]
        kt = P.sbpool(es, "mkt", 3, [128, 4, 128], BF16)
        npt = [0]

        def load(c):
            P.dma("sp", zt[c % 2][:, :], k.mlqkT[c * 128:(c + 1) * 128, :])
            P.dma("sp", wt[c % 2][:, :], V(k.ml_conv_w.h[l, :, c * 128:(c + 1) * 128].rearrange("w c -> c w"), None),
                  allow_slow_non_contiguous=True)
            P.dma("sp", bt[c % 2][:, :], V(k.ml_conv_b.h[l:l + 1, c * 128:(c + 1) * 128].rearrange("o c -> c o"), None),
                  allow_slow_non_contiguous=True)

        def body(c):
            z, u, a, w, b = zt[c % 2], ut[c % 2], ab[c % 2], wt[c % 2], bt[c % 2]
            for _ in conv3_fm_gen(P, u, z, w, b, SEGS):
                yield
            if c < 4:
                P.act(a[:, :], u[:, :], AF.Silu)
            else:
                P.act(u[:, :], u[:, :], AF.Silu)
                yield
                P.act(a[:, :], u[:, :], AF.Copy, scale=128 ** -0.5)
            P.dma("sp", k.mlQKT[c * 128:(c + 1) * 128, :], a[:, :])
            yield
            if c >= 4:
                hk = c - 4
                for n0 in range(0, NT, 4):
                    nn = min(4, NT - n0)
                    ps = pst[npt[0] % 2]
                    for j in range(nn):
                        P.transpose(ps[:, j, :], a[:, (n0 + j) * 128:(n0 + j + 1) * 128], idb[:, :])
                    o = kt[npt[0] % 3]
                    P.copy(o[:, 0:nn, :], ps[:, 0:nn, :])
                    P.dma("sp", V(k.mlKtm.h[n0 * 128:(n0 + nn) * 128, hk * 128:(hk + 1) * 128].rearrange("(n p) d -> p n d", p=128), None),
                          o[:, 0:nn, :])
                    npt[0] += 1
                    if n0 % 16 == 12:
                        yield

        prefetch_pipelined(8, load, body, 0, 2)
        vf = P.sbpool(es, "mvf", 3, [128, 512], F32)
        va = P.sbpool(es, "mva", 3, [128, 4, 129], BF16)
        for b_ in va:
            P.memset(b_[:, :, 128:129], 1.0)

        def vload(ti):
            P.dma("sp", vf[ti % 3][:, :], k.mlvo[ti * 128:(ti + 1) * 128, 0:512])

        def vbody(ti):
            P.act(va[ti % 3][:, :, 0:128], V(vf[ti % 3].h[:, :].rearrange("p (h c) -> p h c", h=4), vf[ti % 3].key), AF.Copy)
            P.dma("sp", k.mlVa[ti * 128:(ti + 1) * 128, :, :], va[ti % 3][:, :, :])

        prefetch_loop(NT, vload, vbody, 2)
        P.flush()


def phase_ml_scan(P, k, l):
    with ExitStack() as es:
        G = P.sb(es, "sG", [128, NT, 16], F32)
        gb = P.sb(es, "sgb", [128, 16], F32)
        P.dma("sp", G[:, :, :], V(k.mlvo.h[:, 1024:1040].rearrange("(n p) c -> p n c", p=128), None))
        P.dma("sp", gb[:, :], V(k.ml_gate_b.h[l:l + 1, :].to_broadcast([128, 16]), None))
        P.tt(G[:, :, :], G[:, :, :], bcast(gb[:, :], [128, NT, 16], 1), ALU.add)
        Gv = lambda d, kind: V(G.h[:, :, :].rearrange("p n (d k h) -> p d k n h", d=2, k=2)[:, d, kind, :, :], G.key)
        I_ = P.sb(es, "sI", [128, 2, NT, 4], F32)
        LF = P.sb(es, "sLF", [128, 2, NT, 4], F32)
        for d in range(2):
            P.copy(I_[:, d, :, :], Gv(d, 0))
            P.act(LF[:, d, :, :], Gv(d, 1), AF.Exp, scale=-1.0)
        P.ts(LF[:, :, :, :], LF[:, :, :, :], 1.0, None, ALU.add)
        P.act(LF[:, :, :, :], LF[:, :, :, :], AF.Ln)
        P.ts(LF[:, :, :, :], LF[:, :, :, :], -1.0, None, ALU.mult)
        tri = P.sb(es, "stri", [128, 3, 128], F32)
        P.dma("sp", tri[:, 0, :], k.mhi[:, :])
        P.dma("sp", tri[:, 1, :], k.mlo[:, :])
        P.memset(tri[:, 2, :], 1.0)
        trib = P.sb(es, "strib", [128, 2, 128], BF16)
        P.copy(trib[:, :, :], tri[:, 0:2, :])
        NC4 = NT * 4
        Bc = P.sb(es, "sB", [128, 2, NT, 4], F32)
        Be = P.sb(es, "sBe", [128, 2, NT, 4], F32)
        es_g = ExitStack()
        psg = [P.ps(es_g, f"spg{i}") for i in range(2)]
        for d in range(2):
            lfv = V(LF.h[:, d, :, :].rearrange("p n h -> p (n h)"), LF.key)
            P.mm(psg[0][:, 0:NC4], tri[:, d, :], lfv)
            P.copy(V(Bc.h[:, d, :, :].rearrange("p n h -> p (n h)"), Bc.key), psg[0][:, 0:NC4])
            P.mm(psg[1][:, 0:NC4], tri[:, 2, :], lfv)
            P.copy(V(Be.h[:, d, :, :].rearrange("p n h -> p (n h)"), Be.key), psg[1][:, 0:NC4])
        U_ = P.sb(es, "sU", [128, 2, NT, 4], F32)
        EB = P.sb(es, "sEB", [128, 2, NT, 4], F32)
        W_ = P.sb(es, "sW", [128, 2, NT, 4], F32)
        EE = P.sb(es, "sEE", [128, 2, NT, 4], F32)
        P.tt(U_[:, :, :, :], I_[:, :, :, :], Bc[:, :, :, :], ALU.subtract)
        P.tt(W_[:, :, :, :], U_[:, :, :, :], Be[:, :, :, :], ALU.add)
        P.act(U_[:, :, :, :], U_[:, :, :, :], AF.Exp)
        P.act(W_[:, :, :, :], W_[:, :, :, :], AF.Exp)
        P.act(EB[:, :, :, :], Bc[:, :, :, :], AF.Exp)
        P.act(EE[:, :, :, :], Be[:, :, :, :], AF.Exp)
        P.flush("phase_ml_gates")
        es_g.close()
        NB = 3
        pool2 = lambda nm, shp, dt: [[P.sb(es, f"{nm}{c}_{i}", shp, dt) for i in range(NB)] for c in range(2)]
        QKc = pool2("sQK", [128, 2, 4, 128], BF16)
        Kmc = pool2("sKm", [128, 512], BF16)
        Vac = pool2("sVa", [128, 4, 129], BF16)
        C32 = [P.sb(es, f"sC{i}", [128, 129], F32) for i in range(8)]
        Cbf = [P.sb(es, f"sCb{i}", [128, 129], BF16) for i in range(8)]
        Sm = P.sbpool(es, "sSm", 8, [128, 128], BF16)
        Vw = P.sbpool(es, "sVw", 8, [128, 129], BF16)
        dn = P.sbpool(es, "sdn", 8, [128, 1], F32)
        ho = P.sbpool(es, "sho", 4, [128, 4, 128], F32)
        bS = [P.ps(es, f"spS{i}", [128, 4, 128], F32) for i in range(2)]
        bO = [P.ps(es, f"spO{i}", [128, 3, 129], F32) for i in range(3)]
        bU = [P.ps(es, f"spU{i}", [128, 3, 129], F32) for i in range(3)]
        pSv = lambda ci, par: bS[ci // 4][:, ci % 4, :]
        pOv = lambda ci, a=0, b=129: bO[ci // 3][:, ci % 3, a:b]
        pUv = lambda ci: bU[ci // 3][:, ci % 3, :]

        def excl(op, v):
            op.writes.append(v.key)
        order = [[NTL, NTL + 1] + list(range(NTL)), [NTL + 1, NTL] + list(range(NTL - 1, -1, -1))]
        onec = P.sb(es, "sonec", [128, 1], F32)
        P.memset(onec[:, :], 1.0)

        def loads(j):
            for d in range(2):
                n = order[d][j]
                tsl = slice(n * 128, (n + 1) * 128)
                qkv = k.mlQKT.h[:, tsl].rearrange("(w hh d) t -> d w hh t", w=2, hh=4)
                for w in range(2):
                    P.dma("sp", QKc[d][j % NB][:, w, :, :], V(qkv[:, w, :, :], None))
                P.dma("sp", Kmc[d][j % NB][:, :], k.mlKtm[tsl, :])
                P.dma("sp", Vac[d][j % NB][:, :, :], k.mlVa[tsl, :, :])

        def round_(j):
            ch = [(h, d, h * 2 + d, order[d][j]) for h in range(4) for d in range(2)]
            par = j % 2
            for (h, d, ci, n) in ch:
                qk_ = QKc[d][j % NB]
                P.mm(pSv(ci, par), qk_[:, 1, h, :], qk_[:, 0, h, :])
            for (h, d, ci, n) in ch:
                excl(P.stt(Sm[ci][:, :], pSv(ci, par), U_[:, d, n, h:h + 1], trib[:, d, :], ALU.mult, ALU.mult), pSv(ci, par))
                P.act(Vw[ci][:, :], Vac[d][j % NB][:, h, :], AF.Identity, scale=W_[:, d, n, h:h + 1])
            for (h, d, ci, n) in ch:
                qk_, km_, vv_ = QKc[d][j % NB], Kmc[d][j % NB], Vac[d][j % NB]
                P.mm(pOv(ci), Sm[ci][:, :], vv_[:, h, :], start=True, stop=(j == 0))
                if j > 0:
                    P.mm(pOv(ci), qk_[:, 0, h, :], Cbf[ci][:, :], start=False, stop=True)
                P.mm(pUv(ci), km_[:, h * 128:(h + 1) * 128], Vw[ci][:, :])
            for (h, d, ci, n) in ch:
                excl(P.tt(dn[ci][:, :], pOv(ci, 128, 129), EB[:, d, n, h:h + 1], ALU.mult), pOv(ci))
            for (h, d, ci, n) in ch:
                P.act(dn[ci][:, :], dn[ci][:, :], AF.Abs)
            for (h, d, ci, n) in ch:
                P.ts(dn[ci][:, :], dn[ci][:, :], 1.0, None, ALU.max)
            for (h, d, ci, n) in ch:
                P.recip(dn[ci][:, :], dn[ci][:, :])
            for (h, d, ci, n) in ch:
                P.tt(dn[ci][:, :], dn[ci][:, :], EB[:, d, n, h:h + 1], ALU.mult)
            for (h, d, ci, n) in ch:
                ho_ = ho[(2 * j + d) % 4]
                excl(P.act(ho_[:, h, :], pOv(ci, 0, 128), AF.Identity, scale=dn[ci][:, :]), pOv(ci))
            for d in range(2):
                n = order[d][j]
                ho_ = ho[(2 * j + d) % 4]
                P.dma("act", k.mlH[d, n * 128:(n + 1) * 128, :], V(ho_.h[:, :, :].rearrange("p h e -> p (h e)"), ho_.key))
            for (h, d, ci, n) in ch:
                if j == 0:
                    excl(P.copy(C32[ci][:, :], pUv(ci)), pUv(ci))
                else:
                    excl(P.stt(C32[ci][:, :], C32[ci][:, :], EE[:, d, n, h:h + 1], pUv(ci), ALU.mult, ALU.add), pUv(ci))
            for (h, d, ci, n) in ch:
                P.act(Cbf[ci][:, :], C32[ci][:, :], AF.Copy)

        prefetch_loop(NT, loads, round_, 1)
        P.flush()


def phase_ml_finish(P, k, l, with_ctx_out):
    with ExitStack() as es:
        idf, idb = load_ident(P, es, k)
        g = P.sb(es, "fg", [128, 4, 128], F32)
        P.dma("sp", V(g.h[:, :, :].rearrange("p h d -> p (h d)"), g.key), V(k.ml_norm_g.h[l:l + 1, :].to_broadcast([128, 512]), None))
        hf = P.sbpool(es, "fhf", 4, [128, 4, 128], F32)
        hb = P.sbpool(es, "fhb", 4, [128, 4, 128], F32)
        og = P.sbpool(es, "fog", 4, [128, 4, 128], F32)
        sq2 = P.sbpool(es, "fsq", 2, [128, 4, 128], F32)
        ss = P.sbpool(es, "fss", 2, [128, 4], F32)
        yb = P.sbpool(es, "fyb", 2, [128, 4, 128], BF16)
        psT = [P.ps(es, f"fpT{i}", [128, 4, 128], BF16) for i in range(2)]
        oT = P.sbpool(es, "foT", 2, [128, 4, 128], BF16)
        flat = lambda b: V(b.h[:, :, :].rearrange("p h d -> p (h d)"), b.key)
        tiles = list(range(NTL)) + (list(range(NTL, NT)) if with_ctx_out else [])

        def load(i):
            t0 = tiles[i] * 128
            P.dma("sp", flat(hf[i % 4]), k.mlH[0, t0:t0 + 128, :])
            P.dma("sp", flat(hb[i % 4]), k.mlH[1, t0:t0 + 128, :])
            P.dma("sp", flat(og[i % 4]), k.mlvo[t0:t0 + 128, 512:1024])

        def body(i):
            t0 = tiles[i] * 128
            a, b, o = hf[i % 4], hb[i % 4], og[i % 4]
            P.tt(a[:, :, :], a[:, :, :], b[:, :, :], ALU.add)
            P.act(o[:, :, :], o[:, :, :], AF.Sigmoid)
            yield
            sq_ = sq2[i % 2]
            P.tt(sq_[:, :, :], a[:, :, :], a[:, :, :], ALU.mult)
            yield
            s_ = ss[i % 2]
            P.reduce(s_[:, :], sq_[:, :, :])
            yield
            P.act(s_[:, :], s_[:, :], AF.Sqrt, bias=EPSB[0][:, :], scale=1.0 / 128)
            yield
            P.recip(s_[:, :], s_[:, :])
            yield
            P.tt(a[:, :, :], a[:, :, :], bcast(s_[:, :], [128, 4, 128], 2), ALU.mult)
            yield
            P.tt(a[:, :, :], a[:, :, :], g[:, :, :], ALU.mult)
            yield
            y_ = yb[i % 2]
            P.tt(y_[:, :, :], a[:, :, :], o[:, :, :], ALU.mult)
            yield
            ps = psT[i % 2]
            for c in range(4):
                P.transpose(ps[:, c, :], y_[:, c, :], idb[:, :])
            yield
            ot = oT[i % 2]
            P.act(ot[:, :, :], ps[:, :, :], AF.Copy)
            P.dma("sp", V(k.yT.h[2, :, t0:t0 + 128].rearrange("(c p) t -> p c t", p=128), None), ot[:, :, :])

        prefetch_pipelined(len(tiles), load, body, 2, 2)
        P.flush()

NFFT = 2 * L
TWO_PI = 2.0 * np.pi


def _hy_consts():
    c = {}
    for nm, Lf in (("lat", L), ("ctx", LC)):
        t = np.arange(Lf, dtype=np.float64)
        tn = t / (Lf - 1)
        w = TWO_PI * t / Lf
        bands = np.linspace(1e-4, 15.0, 16)
        ang = w[:, None] * bands[None, :]
        z = np.concatenate([tn[:, None], np.cos(ang), -np.sin(ang)], axis=-1)
        c["hz_" + nm] = np.ascontiguousarray(z.T).astype(np.float32)
        c["htn_" + nm] = tn[None, :].astype(np.float32)
    c["hntn_ctx"] = (-(np.arange(LC, dtype=np.float64) / (LC - 1)))[:, None].astype(np.float32)
    a = np.arange(128, dtype=np.float64)
    ang = TWO_PI * np.outer(a, a) / 128.0
    C2, S2 = np.cos(ang), np.sin(ang)
    c["hFA"] = np.concatenate([C2[:64], -S2[:64]], axis=1).astype(np.float32)
    c["hC2"] = C2.astype(np.float32)
    c["hS2"] = S2.astype(np.float32)
    c["hIAre"] = np.concatenate([C2, S2], axis=1).astype(np.float32)
    c["hIAim"] = np.concatenate([-S2, C2], axis=1).astype(np.float32)
    angt = TWO_PI * np.outer(a, a) / NFFT
    c["hTc"] = np.cos(angt).astype(np.float32)
    c["hTs"] = np.sin(angt).astype(np.float32)
    r = np.arange(512, dtype=np.float64)
    a5 = TWO_PI * np.outer(r, r) / 512.0
    c["hW5c"] = np.cos(a5).astype(np.float32)
    c["hW5s"] = np.sin(a5).astype(np.float32)
    return c


CONSTS.update(_hy_consts())
HY_IN = ["hy_conv_w", "hy_conv_b", "hy_pe_w1", "hy_pe_b1", "hy_freq", "hy_pe_w2", "hy_pe_b2", "hy_pe_w3", "hy_decay", "hy_skip"]


def declare_hy(nc, k):
    inp = lambda n, s: dram(nc, n, s, F32, "ExternalInput")
    k.hy_conv_w = inp("hy_conv_w", [DEPTH, 3, 1536])
    k.hy_conv_b = inp("hy_conv_b", [DEPTH, 1536])
    k.hy_pe_w1 = inp("hy_pe_w1", [DEPTH, 33, 64])
    k.hy_pe_b1 = inp("hy_pe_b1", [DEPTH, 64])
    k.hy_freq = inp("hy_freq", [DEPTH, 2, 64])
    k.hy_pe_w2 = inp("hy_pe_w2", [DEPTH, 64, 64])
    k.hy_pe_b2 = inp("hy_pe_b2", [DEPTH, 64])
    k.hy_pe_w3 = inp("hy_pe_w3", [DEPTH, 64, 2048])
    k.hy_decay = inp("hy_decay", [DEPTH, 2048])
    k.hy_skip = inp("hy_skip", [DEPTH, 2, 512])
    for n, v in _hy_consts().items():
        setattr(k, n, inp(n, list(v.shape)))
    k.hyU = dram(nc, "hyU", [1536, T])
    k.hyVb = dram(nc, "hyVb", [512, L], BF16)
    k.hyK = dram(nc, "hyK", [2048, L], BF16)
    k.hyNrm = dram(nc, "hyNrm", [2, 512])
    k.hySpec = dram(nc, "hySpec", [2, 2, 128, 512, 128], BF16)


def phase_hy_conv(P, k, l):
    with ExitStack() as es:
        zt = P.sbpool(es, "hz", 2, [128, T], F32)
        ut = P.sbpool(es, "hu", 2, [128, T], F32)
        vb = P.sbpool(es, "hvb", 2, [128, L], BF16)
        wt = P.sbpool(es, "hw", 2, [128, 3], F32)
        bt = P.sbpool(es, "hb", 2, [128, 1], F32)
        def load(c):
            P.dma("sp", zt[c % 2][:, :], k.hyT[c * 128:(c + 1) * 128, :])
            P.dma("sp", wt[c % 2][:, :], V(k.hy_conv_w.h[l, :, c * 128:(c + 1) * 128].rearrange("w c -> c w"), None),
                  allow_slow_non_contiguous=True)
            P.dma("sp", bt[c % 2][:, :], V(k.hy_conv_b.h[l:l + 1, c * 128:(c + 1) * 128].rearrange("o c -> c o"), None),
                  allow_slow_non_contiguous=True)

        def body(c):
            z, u, w, b = zt[c % 2], ut[c % 2], wt[c % 2], bt[c % 2]
            for _ in conv3_fm_gen(P, u, z, w, b, SEGS):
                yield
            P.dma("sp", k.hyU[c * 128:(c + 1) * 128, :], u[:, :])
            if c < 4:
                v_ = vb[c % 2]
                P.act(v_[:, :], u[:, 0:L], AF.Copy)
                P.dma("sp", k.hyVb[c * 128:(c + 1) * 128, :], v_[:, :])
            yield

        prefetch_pipelined(12, load, body, 0, 2)
        P.flush()


def _sin_layer(P, dst, ps, scl, bia, tmp, m, n):
    a = tmp[0]
    P.act(a[0:64, 0:n], ps[0:64, 0:n], AF.Identity, bias=bia, scale=scl)
    for _ in range(2):
        P.ts(m[0:64, 0:n], a[0:64, 0:n], float(np.pi), -TWO_PI, ALU.is_gt, ALU.mult)
        P.tt(a[0:64, 0:n], a[0:64, 0:n], m[0:64, 0:n], ALU.add)
        P.ts(m[0:64, 0:n], a[0:64, 0:n], -float(np.pi), TWO_PI, ALU.is_lt, ALU.mult)
        P.tt(a[0:64, 0:n], a[0:64, 0:n], m[0:64, 0:n], ALU.add)
    P.act(dst, a[0:64, 0:n], AF.Sin)


def _filter_mlp(P, es, k, l, zsrc, Lf, h2T):
    w1 = P.sb(es, "hw1", [33, 64], F32)
    w2 = P.sb(es, "hw2", [64, 64], F32)
    col = P.sb(es, "hcol", [64, 8], F32)
    P.dma("sp", w1[:, :], k.hy_pe_w1[l, :, :])
    P.dma("sp", w2[:, :], k.hy_pe_w2[l, :, :])
    cl = lambda src: V(src, None)
    P.dma("sp", col[:, 0:1], cl(k.hy_pe_b1.h[l:l + 1, :].rearrange("o c -> c o")), allow_slow_non_contiguous=True)
    P.dma("sp", col[:, 1:2], cl(k.hy_freq.h[l, 0:1, :].rearrange("o c -> c o")), allow_slow_non_contiguous=True)
    P.dma("sp", col[:, 2:3], cl(k.hy_pe_b2.h[l:l + 1, :].rearrange("o c -> c o")), allow_slow_non_contiguous=True)
    P.dma("sp", col[:, 3:4], cl(k.hy_freq.h[l, 1:2, :].rearrange("o c -> c o")), allow_slow_non_contiguous=True)
    P.tt(col[:, 4:5], col[:, 0:1], col[:, 1:2], ALU.mult)
    P.tt(col[:, 5:6], col[:, 2:3], col[:, 3:4], ALU.mult)
    zT = P.sbpool(es, "hzT", 2, [33, 512], F32)
    h1 = P.sbpool(es, "hh1", 2, [64, 512], F32)
    tmp = P.sbpool(es, "hta", 2, [64, 512], F32)
    m = P.sb(es, "htm", [64, 512], F32)
    ps = [P.ps(es, f"hpm{i}") for i in range(2)]
    nb = 0
    for t0 in range(0, Lf, 512):
        n = min(512, Lf - t0)
        z_ = zT[nb % 2]
        P.dma("sp", z_[:, 0:n], zsrc[:, t0:t0 + n])
        P.mm(ps[0][0:64, 0:n], w1[:, :], z_[:, 0:n])
        h1_ = h1[nb % 2]
        _sin_layer(P, h1_[:, 0:n], ps[0], col[:, 1:2], col[:, 4:5], [tmp[0]], m, n)
        P.mm(ps[1][0:64, 0:n], w2[:, :], h1_[:, 0:n])
        _sin_layer(P, h2T[0:64, t0:t0 + n], ps[1], col[:, 3:4], col[:, 5:6], [tmp[1]], m, n)
        nb += 1


def phase_hy_filt(P, k, l):
    with ExitStack() as es:
        h2T = P.sb(es, "hh2T", [64, L], F32)
        _filter_mlp(P, es, k, l, k.hz_lat, L, h2T)
        w3 = P.sb(es, "hw3", [64, 2048], F32)
        P.dma("sp", w3[:, :], k.hy_pe_w3[l, :, :])
        tnb = P.sb(es, "htnb", [128, L], F32)
        P.dma("sp", tnb[:, :], V(k.htn_lat.h[0:1, :].to_broadcast([128, L]), None))
        nad = P.sb(es, "hnad", [128, 16], F32)
        P.dma("sp", nad[:, :], V(k.hy_decay.h[l:l + 1, :].rearrange("o (g p) -> p (o g)", p=128), None),
              allow_slow_non_contiguous=True)
        P.act(nad[:, :], nad[:, :], AF.Abs)
        P.ts(nad[:, :], nad[:, :], -1.0, None, ALU.mult)
        ssq = P.sb(es, "hssq", [128, 16, 16], F32)
        sst = P.sb(es, "hsst", [128, 16], F32)
        E = P.sbpool(es, "hE", 4, [128, 512], F32)
        kf = P.sbpool(es, "hkf", 2, [128, 512], F32)
        kb = P.sbpool(es, "hkb", 2, [128, L], BF16)
        junk = P.sb(es, "hjk", [128, 512], F32)
        ps = [P.ps(es, f"hpf{i}") for i in range(3)]
        nb = 0
        w3b = P.sb(es, "hw3b", [64, 2048], BF16)
        h2b = P.sb(es, "hh2b", [64, L], BF16)
        P.copy(w3b[:, :], w3[:, :])
        for q in range(4):
            P.act(h2b[:, q * 2048:(q + 1) * 2048], h2T[0:64, q * 2048:(q + 1) * 2048], AF.Copy)
        units = []
        for g in range(16):
            is_bwd = (g // 4) % 2 == 1
            kb_ = kb[g % 2]
            for bi in range(L // 512):
                t0 = bi * 512
                p_ = ps[nb % 3]
                e_ = E[nb % 4]
                nb += 1

                def front(g=g, t0=t0, p_=p_, e_=e_):
                    P.mm(p_[:, :], w3b[:, g * 128:(g + 1) * 128], h2b[0:64, t0:t0 + 512])
                    P.act(e_[:, :], tnb[:, t0:t0 + 512], AF.Exp, scale=nad[:, g:g + 1])

                def back(g=g, bi=bi, t0=t0, p_=p_, e_=e_, kb_=kb_, is_bwd=is_bwd):
                    P.tt(kb_[:, t0:t0 + 512], p_[:, :], e_[:, :], ALU.mult)
                    if is_bwd and bi == 0:
                        P.memset(kb_[:, 0:1], 0.0)
                    P.act(junk[:, :], kb_[:, t0:t0 + 512], AF.Square, accum_out=ssq[:, g, bi:bi + 1])
                    if bi == L // 512 - 1:
                        P.dma("sp", k.hyK[g * 128:(g + 1) * 128, :], kb_[:, :])

                units.append((front, back))
        emit_skewed(units, 2)
        P.reduce(sst[:, :], ssq[:, :, :])
        rn = P.sb(es, "hrn", [128, 8], F32)
        for n in range(2):
            P.tt(rn[:, n * 4:(n + 1) * 4], sst[:, n * 8:n * 8 + 4], sst[:, n * 8 + 4:n * 8 + 8], ALU.add)
        P.act(rn[:, :], rn[:, :], AF.Sqrt, bias=EPSB[0][:, :], scale=1.0)
        P.recip(rn[:, :], rn[:, :])
        P.ts(rn[:, :], rn[:, :], 1.0 / NFFT, None, ALU.mult)
        P.dma("sp", V(k.hyNrm.h[:, :].rearrange("n (cb p) -> p n cb", p=128), None),
              V(rn.h[:, :].rearrange("p (n cb) -> p n cb", n=2), rn.key), allow_slow_non_contiguous=True)
        P.flush()


SBQ = 4
GQ = 8
NH = GQ // SBQ


class HyTab:
    pass


def hy_tables(P, es, k):
    tb = HyTab()
    st = P.sb(es, "htst", [128, 256], F32)

    def ld(name, src, rows, cols):
        b = P.sb(es, name, [128, cols], BF16)
        P.dma("sp", st[0:rows, 0:cols], src[:, :])
        P.copy(b[0:rows, :], st[0:rows, 0:cols])
        return b
    tb.FA = ld("hFAb", k.hFA, 64, 256)
    tb.C2 = ld("hC2b", k.hC2, 128, 128)
    tb.S2 = ld("hS2b", k.hS2, 128, 128)
    tb.IAre = ld("hIAreb", k.hIAre, 128, 256)
    tb.IAim = ld("hIAimb", k.hIAim, 128, 256)
    tb.nS2 = P.sb(es, "hnS2b", [128, 128], BF16)
    tb.nC2 = P.sb(es, "hnC2b", [128, 128], BF16)
    P.ts(tb.nS2[:, :], tb.S2[:, :], -1.0, None, ALU.mult)
    P.ts(tb.nC2[:, :], tb.C2[:, :], -1.0, None, ALU.mult)
    tb.Tc = P.sb(es, "hTcf", [128, 128], F32)
    tb.Ts = P.sb(es, "hTsf", [128, 128], F32)
    P.dma("sp", tb.Tc[:, :], k.hTc[:, :])
    P.dma("sp", tb.Ts[:, :], k.hTs[:, :])
    tb.TcB = P.sb(es, "hTcB", [128, GQ, 128], BF16)
    tb.TsB = P.sb(es, "hTsB", [128, GQ, 128], BF16)
    P.copy(tb.TcB[:, :, :], bcast(tb.Tc[:, :], [128, GQ, 128], 1))
    P.copy(tb.TsB[:, :, :], bcast(tb.Ts[:, :], [128, GQ, 128], 1))
    return tb


def hy_evacA(P, pA, Ab, s0):
    P.act(Ab[:, 0, s0:s0 + SBQ, :], V(pA.h[:, :, 0:128], pA.key), AF.Copy)
    P.act(Ab[:, 1, s0:s0 + SBQ, :], V(pA.h[:, :, 128:256], pA.key), AF.Copy)


def hy_twiddle(P, tb, Bt, tmp, inverse, Ab):
    Are, Aim = Ab[:, 0, :, :], Ab[:, 1, :, :]
    Tc, Ts = tb.TcB[:, :, :], tb.TsB[:, :, :]
    t1, t2, t3, t4 = [t[:, :, :] for t in tmp]
    P.tt(t1, Are, Tc, ALU.mult)
    P.tt(t2, Aim, Ts, ALU.mult)
    P.tt(t3, Aim, Tc, ALU.mult)
    P.tt(t4, Are, Ts, ALU.mult)
    if not inverse:
        P.tt(Bt[:, 0, :, :], t1, t2, ALU.add)
        P.tt(Bt[:, 1, :, :], t3, t4, ALU.subtract)
    else:
        P.tt(Bt[:, 0, :, :], t1, t2, ALU.subtract)
        P.tt(Bt[:, 1, :, :], t3, t4, ALU.add)


def flat2(b, i, s0):
    return V(b.h[:, i, s0:s0 + SBQ, :].rearrange("p s l -> p (s l)"), b.key)


def hy_stageA_fwd(P, tb, X, pA, s0):
    for s in range(SBQ):
        P.mm(pA[:, s, :], X[0:64, s0 + s, :], tb.FA[0:64, :])


def hy_stageB_fwd(P, tb, Bts, pB, s0):
    n = len(Bts)
    for i, (Bt, conj) in enumerate(Bts):
        P.mm(pB[:, 0, :], tb.C2[:, :], flat2(Bt, 0, s0), start=(i == 0), stop=False)
        P.mm(pB[:, 0, :], tb.S2[:, :], flat2(Bt, 1, s0), start=False, stop=(i == n - 1))
    for i, (Bt, conj) in enumerate(Bts):
        if not conj:
            P.mm(pB[:, 1, :], tb.C2[:, :], flat2(Bt, 1, s0), start=(i == 0), stop=False)
            P.mm(pB[:, 1, :], tb.nS2[:, :], flat2(Bt, 0, s0), start=False, stop=(i == n - 1))
        else:
            P.mm(pB[:, 1, :], tb.nC2[:, :], flat2(Bt, 1, s0), start=(i == 0), stop=False)
            P.mm(pB[:, 1, :], tb.S2[:, :], flat2(Bt, 0, s0), start=False, stop=(i == n - 1))


def seqview(dr, row0, nrows):
    return V(dr.h[row0:row0 + nrows, 0:L].rearrange("c (h l) -> h c l", l=128), None)


def run_pipelined(gens_fn, items, npipe):
    for g0 in range(0, len(items), npipe):
        alive = [gens_fn(i, items[g0 + i]) for i in range(npipe) if g0 + i < len(items)]
        while alive:
            nxt = []
            for g in alive:
                try:
                    next(g)
                    nxt.append(g)
                except StopIteration:
                    pass
            alive = nxt


class PsPool:
    def __init__(self, tiles):
        self.t = tiles
        self.c = 0

    def next(self):
        self.c += 1
        return self.t[self.c % len(self.t)]


def phase_hy_spec(P, k, l):
    with ExitStack() as es:
        tb = hy_tables(P, es, k)
        NPIPE = 3
        mk = lambda nm, shp, dt: [P.sb(es, f"{nm}{i}", shp, dt) for i in range(NPIPE)]
        Xf = mk("hXf", [64, GQ, 128], BF16)
        Xb = mk("hXb", [64, GQ, 128], BF16)
        Bf = mk("hBf", [128, 2, GQ, 128], BF16)
        Bb = mk("hBb", [128, 2, GQ, 128], BF16)
        tmp = [mk(f"htw{i}", [128, GQ, 128], BF16) for i in range(4)]
        tmp2 = [mk(f"htx{i}", [128, GQ, 128], BF16) for i in range(4)]
        Ab1 = mk("hAb1", [128, 2, GQ, 128], BF16)
        Ab2 = mk("hAb2", [128, 2, GQ, 128], BF16)
        rnb = P.sb(es, "hrnb", [128, 2, 512], F32)
        P.dma("sp", V(rnb.h[:, :, :].rearrange("p n c -> p (n c)"), rnb.key),
              V(k.hyNrm.h[:, :].rearrange("n c -> (n c)").unsqueeze(0).to_broadcast([128, 1024]), None))
        Sp = mk("hSp", [128, 2, GQ, 128], BF16)
        Ys = mk("hYs", [128, 2, GQ, 128], F32)
        pA = PsPool([P.ps(es, f"hpA{i}", [128, SBQ, 256], F32) for i in range(2)])
        pB = PsPool([P.ps(es, f"hpB{i}", [128, 2, SBQ * 128], F32) for i in range(2)])

        def steps(i, item):
            n, c0 = item
            P.dma("sp", Xf[i][:, :, :], seqview(k.hyK, n * 1024 + c0, GQ))
            P.dma("sp", Xb[i][:, :, :], seqview(k.hyK, n * 1024 + 512 + c0, GQ))
            for hf in range(NH):
                pa = pA.next()
                hy_stageA_fwd(P, tb, Xf[i], pa, hf * SBQ)
                hy_evacA(P, pa, Ab1[i], hf * SBQ)
            yield
            hy_twiddle(P, tb, Bf[i], [t[i] for t in tmp], False, Ab1[i])
            for hf in range(NH):
                pa = pA.next()
                hy_stageA_fwd(P, tb, Xb[i], pa, hf * SBQ)
                hy_evacA(P, pa, Ab2[i], hf * SBQ)
            yield
            hy_twiddle(P, tb, Bb[i], [t[i] for t in tmp2], False, Ab2[i])
            yield
            for hf in range(NH):
                pb = pB.next()
                hy_stageB_fwd(P, tb, [(Bf[i], False), (Bb[i], True)], pb, hf * SBQ)
                P.act(Ys[i][:, :, hf * SBQ:(hf + 1) * SBQ, :],
                      V(pb.h[:, :, :].rearrange("p r (s l) -> p r s l", l=128), pb.key), AF.Copy)
            yield
            rv = V(rnb.h[:, n, c0:c0 + GQ].unsqueeze(1).unsqueeze(3).to_broadcast([128, 2, GQ, 128]), rnb.key)
            P.tt(Sp[i][:, :, :, :], Ys[i][:, :, :, :], rv, ALU.mult)
            for ri in range(2):
                P.dma("act", V(k.hySpec.h[n, ri, :, c0:c0 + GQ, :], None), Sp[i][:, ri, :, :])
            yield

        items = [(n, c0) for n in range(2) for c0 in range(0, 512, GQ)]
        run_pipelined(steps, items, NPIPE)
        P.flush()


def phase_hy_data(P, k, l):
    with ExitStack() as es:
        tb = hy_tables(P, es, k)
        skb = P.sb(es, "hskb", [64, 2, 512], F32)
        P.dma("sp", V(skb.h[:, :, :].rearrange("p n c -> p (n c)"), skb.key),
              V(k.hy_skip.h[l, :, :].rearrange("n c -> (n c)").unsqueeze(0).to_broadcast([64, 1024]), None))
        NPIPE = 3
        mk = lambda nm, shp, dt: [P.sb(es, f"{nm}{i}", shp, dt) for i in range(NPIPE)]
        X = mk("dX", [64, GQ, 128], BF16)
        yp = mk("dyp", [64, GQ, 128], F32)
        gt = [mk("dg0", [64, GQ, 128], F32), mk("dg1", [64, GQ, 128], F32)]
        Bt = mk("dB", [128, 2, GQ, 128], BF16)
        Zt = mk("dZ", [128, 2, GQ, 128], BF16)
        Sp = mk("dSp", [128, 2, GQ, 128], BF16)
        tmp = [mk(f"dtw{i}", [128, GQ, 128], BF16) for i in range(4)]
        Ab = mk("dAb", [128, 2, GQ, 128], BF16)
        Yb = mk("dYb", [128, 2, GQ, 128], BF16)
        cmb = mk("dcmb", [64, GQ, 128], F32)
        yfs = mk("dyfs", [64, GQ, 128], F32)
        yb = mk("dyb", [64, GQ, 128], BF16)
        pA = PsPool([P.ps(es, f"dpA{i}", [128, SBQ, 256], F32) for i in range(2)])
        pB = PsPool([P.ps(es, f"dpB{i}", [128, 2, SBQ * 128], F32) for i in range(2)])

        def steps(i, c0):
            P.dma("sp", X[i][:, :, :], seqview(k.hyVb, c0, GQ))
            P.dma("sp", yp[i][:, :, :], seqview(k.hyU, c0, GQ))
            P.dma("sp", gt[0][i][:, :, :], seqview(k.hyU, 512 + c0, GQ))
            P.dma("sp", gt[1][i][:, :, :], seqview(k.hyU, 1024 + c0, GQ))
            yield
            tw = [t[i] for t in tmp]
            for n in range(2):
                for ri in range(2):
                    P.dma("sp", Sp[i][:, ri, :, :], V(k.hySpec.h[n, ri, :, c0:c0 + GQ, :], None))
                for hf in range(NH):
                    pa = pA.next()
                    hy_stageA_fwd(P, tb, X[i], pa, hf * SBQ)
                    hy_evacA(P, pa, Ab[i], hf * SBQ)
                yield
                hy_twiddle(P, tb, Bt[i], tw, False, Ab[i])
                yield
                for hf in range(NH):
                    pb = pB.next()
                    hy_stageB_fwd(P, tb, [(Bt[i], False)], pb, hf * SBQ)
                    P.act(Yb[i][:, :, hf * SBQ:(hf + 1) * SBQ, :],
                          V(pb.h[:, :, :].rearrange("p r (s l) -> p r s l", l=128), pb.key), AF.Copy)
                yield
                t1, t2, t3, t4 = [t[:, :, :] for t in tw]
                Yre, Yim = Yb[i][:, 0, :, :], Yb[i][:, 1, :, :]
                P.tt(t1, Yre, Sp[i][:, 0, :, :], ALU.mult)
                P.tt(t2, Yim, Sp[i][:, 1, :, :], ALU.mult)
                P.tt(t3, Yre, Sp[i][:, 1, :, :], ALU.mult)
                P.tt(t4, Yim, Sp[i][:, 0, :, :], ALU.mult)
                P.tt(Zt[i][:, 0, :, :], t1, t2, ALU.subtract)
                P.tt(Zt[i][:, 1, :, :], t3, t4, ALU.add)
                yield
                for hf in range(NH):
                    pa = pA.next()
                    for s in range(SBQ):
                        P.mm(pa[:, s, :], Zt[i][:, 0, hf * SBQ + s, :], tb.IAre[:, :], start=True, stop=False)
                        P.mm(pa[:, s, :], Zt[i][:, 1, hf * SBQ + s, :], tb.IAim[:, :], start=False, stop=True)
                    hy_evacA(P, pa, Ab[i], hf * SBQ)
                yield
                hy_twiddle(P, tb, Bt[i], tw, True, Ab[i])
                yield
                for hf in range(NH):
                    pb = pB.next()
                    P.mm(pb[0:64, 0, :], tb.C2[:, 0:64], flat2(Bt[i], 0, hf * SBQ), start=True, stop=False)
                    P.mm(pb[0:64, 0, :], tb.nS2[:, 0:64], flat2(Bt[i], 1, hf * SBQ), start=False, stop=True)
                    P.act(yfs[i][:, hf * SBQ:(hf + 1) * SBQ, :],
                          V(pb.h[0:64, 0, :].rearrange("p (s l) -> p s l", l=128), pb.key), AF.Copy)
                yield
                a = cmb[i][:, :, :]
                sk = V(skb.h[:, n, c0:c0 + GQ].unsqueeze(2).to_broadcast([64, GQ, 128]), skb.key)
                P.tt(a, yp[i][:, :, :], sk, ALU.mult)
                P.tt(a, a, yfs[i][:, :, :], ALU.add)
                if n == 0:
                    P.tt(yp[i][:, :, :], a, gt[0][i][:, :, :], ALU.mult)
                    P.act(X[i][:, :, :], yp[i][:, :, :], AF.Copy)
                else:
                    P.tt(yb[i][:, :, :], a, gt[1][i][:, :, :], ALU.mult)
                    P.dma("act", V(k.yT.h[0, c0:c0 + GQ, 0:L].rearrange("c (h l) -> h c l", l=128), None), yb[i][:, :, :])
                yield

        run_pipelined(steps, list(range(0, 512, GQ)), NPIPE)
        P.flush()


def phase_hy_ctx(P, k, l):
    with ExitStack() as es:
        idf, idb = load_ident(P, es, k)
        h2T = P.sb(es, "ch2T", [64, LC], F32)
        _filter_mlp(P, es, k, l, k.hz_ctx, LC, h2T)
        w3 = P.sb(es, "cw3", [64, 2048], F32)
        P.dma("sp", w3[:, :], k.hy_pe_w3[l, :, :])
        adb = P.sb(es, "cadb", [128, 2048], F32)
        P.dma("sp", adb[:, :], V(k.hy_decay.h[l:l + 1, :].to_broadcast([128, 2048]), None))
        P.act(adb[:, :], adb[:, :], AF.Abs)
        ntn = P.sb(es, "cntn", [128, 2], F32)
        P.dma("sp", ntn[:, :], V(k.hntn_ctx.h[:, :].rearrange("(c p) o -> p (c o)", p=128), None), allow_slow_non_contiguous=True)
        kfc = P.sb(es, "ckfc", [128, 2, 2048], F32)
        E = P.sbpool(es, "cE", 2, [128, 512], F32)
        pf = [P.ps(es, f"cpf{i}") for i in range(2)]
        nb = 0
        for tc in range(2):
            for cb in range(4):
                cs = slice(cb * 512, (cb + 1) * 512)
                p_ = pf[nb % 2]
                P.mm(p_[:, :], h2T[0:64, tc * 128:(tc + 1) * 128], w3[:, cs])
                e_ = E[nb % 2]
                P.act(e_[:, :], adb[:, cs], AF.Exp, scale=ntn[:, tc:tc + 1])
                P.tt(kfc[:, tc, cs], p_[:, :], e_[:, :], ALU.mult)
                nb += 1
        for n in range(2):
            P.memset(kfc[0:1, 0, n * 1024 + 512:(n + 1) * 1024], 0.0)
        sq = P.sb(es, "csq", [128, 2, 2048], F32)
        P.tt(sq[:, :, :], kfc[:, :, :], kfc[:, :, :], ALU.mult)
        ones = P.sb(es, "cones", [128, 128], F32)
        P.memset(ones[:, :], 1.0)
        ssb = P.sb(es, "cssb", [128, 4, 512], F32)
        for cb in range(4):
            p_ = pf[cb % 2]
            for tc in range(2):
                P.mm(p_[:, :], ones[:, :], sq[:, tc, cb * 512:(cb + 1) * 512], start=(tc == 0), stop=(tc == 1))
            P.copy(ssb[:, cb, :], p_[:, :])
        rn = P.sb(es, "crn", [128, 2, 512], F32)
        for n in range(2):
            P.tt(rn[:, n, :], ssb[:, 2 * n, :], ssb[:, 2 * n + 1, :], ALU.add)
        P.act(rn[:, :, :], rn[:, :, :], AF.Sqrt, bias=EPSB[0][:, :], scale=1.0)
        P.recip(rn[:, :, :], rn[:, :, :])
        P.ts(rn[:, :, :], rn[:, :, :], 1.0 / (2 * LC), None, ALU.mult)
        Spl = P.sb(es, "cSpl", [128, 2, 2, 512], BF16)
        Smi = P.sb(es, "cSmi", [128, 2, 2, 512], BF16)
        kv = lambda dr: V(kfc.h[:, :, :].rearrange("p t (n d c) -> p t n d c", n=2, d=2)[:, :, :, dr, :], kfc.key)
        P.tt(Spl[:, :, :, :], kv(0), kv(1), ALU.add)
        P.tt(Smi[:, :, :, :], kv(1), kv(0), ALU.subtract)
        st = P.sb(es, "cst", [128, 4, 512], F32)
        W5c = P.sb(es, "cW5c", [128, 4, 512], BF16)
        W5s = P.sb(es, "cW5s", [128, 4, 512], BF16)
        nW5s = P.sb(es, "cnW5s", [128, 4, 512], BF16)
        P.dma("sp", st[:, :, :], V(k.hW5c.h[:, :].rearrange("(c p) k -> p c k", p=128), None))
        P.copy(W5c[:, :, :], st[:, :, :])
        P.dma("sp", st[:, :, :], V(k.hW5s.h[:, :].rearrange("(c p) k -> p c k", p=128), None))
        P.copy(W5s[:, :, :], st[:, :, :])
        P.ts(nW5s[:, :, :], st[:, :, :], -1.0, None, ALU.mult)
        Spc = P.sb(es, "cSpc", [128, 4, 2, 2, 512], F32)
        for kc in range(4):
            ks = slice(kc * 128, (kc + 1) * 128)
            for n in range(2):
                for ri, (tab, src) in enumerate(((W5c, Spl), (W5s, Smi))):
                    p_ = pf[nb % 2]
                    nb += 1
                    for tc in range(2):
                        P.mm(p_[:, :], tab[:, tc, ks], src[:, tc, n, :], start=(tc == 0), stop=(tc == 1))
                    P.tt(Spc[:, kc, n, ri, :], p_[:, :], rn[:, n, :], ALU.mult)
        uf = P.sb(es, "cuf", [128, 12, LC], F32)
        P.dma("sp", uf[:, :, :], V(k.hyU.h[:, L:T].rearrange("(g p) t -> p g t", p=128), None))
        uc = P.sb(es, "cuc", [128, 2, 1536], F32)
        pT = P.ps(es, "cpT")
        for tc in range(2):
            for g0 in range(0, 12, 4):
                for g in range(4):
                    P.transpose(pT[:, g * 128:(g + 1) * 128], uf[:, g0 + g, tc * 128:(tc + 1) * 128], idf[:, :])
                P.copy(uc[:, tc, g0 * 128:(g0 + 4) * 128], pT[:, :])
        skc = P.sb(es, "cskc", [128, 2, 512], F32)
        P.dma("sp", V(skc.h[:, :, :].rearrange("p n c -> p (n c)"), skc.key),
              V(k.hy_skip.h[l, :, :].rearrange("n c -> (n c)").unsqueeze(0).to_broadcast([128, 1024]), None))
        X = P.sb(es, "cX", [128, 2, 512], BF16)
        yp = P.sb(es, "cyp", [128, 2, 512], F32)
        P.copy(yp[:, :, :], uc[:, :, 0:512])
        P.copy(X[:, :, :], uc[:, :, 0:512])
        Z = P.sb(es, "cZ", [128, 4, 2, 512], BF16)
        tw = [P.sb(es, f"ctw{i}", [128, 512], F32) for i in range(4)]
        pY = [P.ps(es, f"cpY{i}") for i in range(2)]
        yb = P.sb(es, "cyb", [128, 2, 512], BF16)
        for n in range(2):
            for kc in range(4):
                ks = slice(kc * 128, (kc + 1) * 128)
                for ri, tab in enumerate((W5c, nW5s)):
                    for tc in range(2):
                        P.mm(pY[ri][:, :], tab[:, tc, ks], X[:, tc, :], start=(tc == 0), stop=(tc == 1))
                t1, t2, t3, t4 = [t[:, :] for t in tw]
                P.tt(t1, pY[0][:, :], Spc[:, kc, n, 0, :], ALU.mult)
                P.tt(t2, pY[1][:, :], Spc[:, kc, n, 1, :], ALU.mult)
                P.tt(t3, pY[0][:, :], Spc[:, kc, n, 1, :], ALU.mult)
                P.tt(t4, pY[1][:, :], Spc[:, kc, n, 0, :], ALU.mult)
                P.tt(Z[:, kc, 0, :], t1, t2, ALU.subtract, eng="pool")
                P.tt(Z[:, kc, 1, :], t3, t4, ALU.add, eng="pool")
            for tc in range(2):
                ts_ = slice(tc * 128, (tc + 1) * 128)
                p_ = pf[tc]
                for kc in range(4):
                    P.mm(p_[:, :], W5c[:, kc, ts_], Z[:, kc, 0, :], start=(kc == 0), stop=False)
                    P.mm(p_[:, :], nW5s[:, kc, ts_], Z[:, kc, 1, :], start=False, stop=(kc == 3))
                a = tw[tc][:, :]
                P.tt(a, yp[:, tc, :], skc[:, n, :], ALU.mult, eng="pool")
                P.tt(a, a, p_[:, :], ALU.add)
                if n == 0:
                    P.tt(yp[:, tc, :], a, uc[:, tc, 512:1024], ALU.mult, eng="pool")
                    P.copy(X[:, tc, :], yp[:, tc, :])
                else:
                    P.tt(yb[:, tc, :], a, uc[:, tc, 1024:1536], ALU.mult, eng="pool")
        pTb = P.ps(es, "cpTb", [128, 8, 128], BF16)
        oT = P.sb(es, "coT", [128, 4, 2, 128], BF16)
        for tc in range(2):
            for c in range(4):
                P.transpose(pTb[:, tc * 4 + c, :], yb[:, tc, c * 128:(c + 1) * 128], idb[:, :])
        P.copy(V(oT.h[:, :, :, :].rearrange("p c t q -> p t c q"), oT.key), V(pTb.h[:, :, :].rearrange("p (t c) q -> p t c q", t=2), pTb.key))
        P.dma("act", V(k.yT.h[0, :, L:T].rearrange("(c p) (t q) -> p c t q", p=128, q=128), None), oT[:, :, :, :])
        P.flush()

def phase_merge(P, k, l, with_ctx):
    with ExitStack() as es:
        idf, idb = load_ident(P, es, k)
        stg = mk_stage(P, es, 2, 2048)
        Wup = P.sb(es, "eWup", [128, 4, 4, 1024], BF16)
        Wg = P.sb(es, "eWg", [128, 8, 4096], BF16)
        Wout = P.sb(es, "eWout", [128, 8, 1024], BF16)
        for n in range(4):
            for c in range(4):
                load_cast(P, stg, Wup[:, n, c, :], k.w_up[l, n, c * 128:(c + 1) * 128, :], 1024)
        for c in range(8):
            for hh in range(2):
                load_cast(P, stg, Wg[:, c, hh * 2048:(hh + 1) * 2048],
                          k.w_in[l, c * 128:(c + 1) * 128, NPROJ + hh * 2048:NPROJ + (hh + 1) * 2048], 2048)
            load_cast(P, stg, Wout[:, c, :], k.w_out[l, c * 128:(c + 1) * 128, :], 1024)
        g1 = P.sbpool(es, "eg1", 2, [128, D], F32)
        for j in range(2):
            P.dma("sp", g1[j][:, :], V(k.mod.h[l, j:j + 1, 2 * D:3 * D].to_broadcast([128, D]), None))
        hts = P.sbpool(es, "eht", 3, [128, 8, 128], BF16)
        yts = P.sbpool(es, "eyt", 3, [128, 4, 4, 128], BF16)
        xts = P.sbpool(es, "ext", 3, [128, D], F32)
        acc = P.sbpool(es, "eacc", 2, [128, D], F32)
        accb = P.sbpool(es, "eaccb", 2, [128, D], BF16)
        accT = P.sbpool(es, "eaccT", 2, [128, 8, 128], BF16)
        sg = P.sbpool(es, "esg", 3, [128, 512], F32)
        tm = P.sbpool(es, "etm", 3, [128, 512], F32)
        psA = [P.ps(es, f"epA{i}") for i in range(3)]
        psB = [P.ps(es, f"epB{i}") for i in range(3)]
        psT = P.ps(es, "epT", [128, 8, 128], BF16)
        hTv = k.hT.h.rearrange("(c p) t -> p c t", p=128)
        tiles = list(range(NTL)) + (list(range(NTL, NT)) if with_ctx else [])
        na = nb = 0
        cn = [0, 0]

        def load(i):
            t0 = tiles[i] * 128
            P.dma("sp", hts[i % 3][:, :, :], V(hTv[:, :, t0:t0 + 128], None))
            for n in range(4):
                P.dma("sp", yts[i % 3][:, n, :, :], V(k.yT.h[n, :, t0:t0 + 128].rearrange("(c p) t -> p c t", p=128), None))
            P.dma("sp", xts[i % 3][:, :], k.xs[t0:t0 + 128, :])

        def body(i):
            ti = tiles[i]
            na, nb = cn
            t0 = ti * 128
            j = 0 if ti < NTL else 1
            ht, yt, xt, ac = hts[i % 3], yts[i % 3], xts[i % 3], acc[i % 2]
            for n in range(4):
                for hf in range(2):
                    fs = slice(hf * 512, (hf + 1) * 512)
                    pu = psA[na % 3]
                    na += 1
                    pg = psB[nb % 3]
                    nb += 1
                    for c in range(4):
                        P.mm(pu[:, :], yt[:, n, c, :], Wup[:, n, c, fs], start=(c == 0), stop=(c == 3))
                    for c in range(8):
                        P.mm(pg[:, :], ht[:, c, :], Wg[:, c, n * 1024 + hf * 512:n * 1024 + (hf + 1) * 512],
                             start=(c == 0), stop=(c == 7))
                    s_ = sg[nb % 3]
                    P.act(s_[:, :], pg[:, :], AF.Sigmoid)
                    if n == 0:
                        P.tt(ac[:, fs], s_[:, :], pu[:, :], ALU.mult)
                    else:
                        t_ = tm[nb % 3]
                        P.tt(t_[:, :], s_[:, :], pu[:, :], ALU.mult)
                        P.tt(ac[:, fs], ac[:, fs], t_[:, :], ALU.add)
            ab = accb[i % 2]
            P.act(ab[:, :], ac[:, :], AF.Copy)
            for c in range(8):
                P.transpose(psT[:, c, :], ab[:, c * 128:(c + 1) * 128], idb[:, :])
            at = accT[i % 2]
            P.act(at[:, :, :], psT[:, :, :], AF.Copy)
            for hf in range(2):
                fs = slice(hf * 512, (hf + 1) * 512)
                po = psA[na % 3]
                na += 1
                for c in range(8):
                    P.mm(po[:, :], at[:, c, :], Wout[:, c, fs], start=(c == 0), stop=(c == 7))
                t_ = tm[(nb + hf) % 3]
                P.tt(t_[:, :], po[:, :], g1[j][:, fs], ALU.mult)
                P.tt(xt[:, fs], xt[:, fs], t_[:, :], ALU.add)
            P.dma("sp", k.xs[t0:t0 + 128, :], xt[:, :])
            cn[0], cn[1] = na, nb

        prefetch_loop(len(tiles), load, body, 1)
        P.flush()


def phase_mlp(P, k, l, with_ctx):
    TBK = 256
    with ExitStack() as es:
        stg = mk_stage(P, es, 2, 1024)
        W1 = P.sb(es, "fW1", [128, 8, 4096], BF16)
        W2 = P.sb(es, "fW2", [128, 32, 1024], BF16)
        for c in range(8):
            for q in range(4):
                load_cast(P, stg, W1[:, c, q * 1024:(q + 1) * 1024], k.mlp_w1[l, c * 128:(c + 1) * 128, q * 1024:(q + 1) * 1024], 1024)
        for c in range(32):
            load_cast(P, stg, W2[:, c, :], k.mlp_w2[l, c * 128:(c + 1) * 128, :], 1024)
        b1 = P.sb(es, "fb1", [128, 32], F32)
        P.dma("sp", b1[:, :], V(k.mlp_b1.h[l:l + 1, :].rearrange("o (c p) -> p (o c)", p=128), None), allow_slow_non_contiguous=True)
        b2 = P.sb(es, "fb2", [128, D], F32)
        P.dma("sp", b2[:, :], V(k.mlp_b2.h[l:l + 1, :].to_broadcast([128, D]), None))
        g2 = P.sbpool(es, "fg2", 2, [128, D], F32)
        for j in range(2):
            P.dma("sp", g2[j][:, :], V(k.mod.h[l, j:j + 1, 5 * D:6 * D].to_broadcast([128, D]), None))
        hts = P.sbpool(es, "fht", 2, [128, 8, TBK], BF16)
        aT = P.sb(es, "faT", [128, 32, TBK], BF16)
        rt = P.sbpool(es, "frt", 2, [128, TBK], F32)
        xts = P.sbpool(es, "fxt", 4, [128, D], F32)
        tm = P.sbpool(es, "ftm", 2, [128, 512], F32)
        psA = [P.ps(es, f"fpA{i}") for i in range(3)]
        psB = [P.ps(es, f"fpB{i}") for i in range(2)]
        hTv = k.hT.h.rearrange("(c p) t -> p c t", p=128)
        nblk = (T if with_ctx else L) // TBK
        na = nb = nx = 0
        def load_blk(bi):
            t0 = bi * TBK
            P.dma("sp", hts[bi % 2][:, :, :], V(hTv[:, :, t0:t0 + TBK], None))

        load_blk(0)
        for bi in range(nblk):
            t0 = bi * TBK
            j = 0 if t0 < L else 1
            ht = hts[bi % 2]
            nsub = TBK // 128
            for sub in range(nsub):
                P.dma("sp", xts[(bi * nsub + sub) % 4][:, :], k.xs[t0 + sub * 128:t0 + (sub + 1) * 128, :])
            for fc in range(32):
                ps = psA[na % 3]
                na += 1
                for c in range(8):
                    P.mm(ps[:, 0:TBK], W1[:, c, fc * 128:(fc + 1) * 128], ht[:, c, :], start=(c == 0), stop=(c == 7))
                r_ = rt[fc % 2]
                P.act(r_[:, :], ps[:, 0:TBK], AF.Relu, bias=b1[:, fc:fc + 1])
                P.act(aT[:, fc, :], r_[:, :], AF.Square)
            if bi + 1 < nblk:
                load_blk(bi + 1)
            for sub in range(nsub):
                xt = xts[(bi * nsub + sub) % 4]
                r0 = t0 + sub * 128
                for hf in range(2):
                    fs = slice(hf * 512, (hf + 1) * 512)
                    po = psB[nb % 2]
                    nb += 1
                    for fc in range(32):
                        P.mm(po[:, :], aT[:, fc, sub * 128:(sub + 1) * 128], W2[:, fc, fs], start=(fc == 0), stop=(fc == 31))
                    t_ = tm[nb % 2]
                    P.tt(t_[:, :], po[:, :], b2[:, fs], ALU.add)
                    P.tt(t_[:, :], t_[:, :], g2[j][:, fs], ALU.mult)
                    P.tt(xt[:, fs], xt[:, fs], t_[:, :], ALU.add)
                P.dma("sp", k.xs[r0:r0 + 128, :], xt[:, :])
        P.flush()


def phase_final(P, k):
    with ExitStack() as es:
        g = P.sb(es, "zg", [128, D], F32)
        P.dma("sp", g[:, :], V(k.final_g.h[0:1, :].to_broadcast([128, D]), None))
        xt = P.sbpool(es, "zx", 4, [128, D], F32)
        junk2 = P.sbpool(es, "zj", 2, [128, D], F32)
        ss = P.sbpool(es, "zs", 3, [128, 1], F32)
        yo = P.sbpool(es, "zy", 3, [128, D], F32)
        def load(ti):
            P.dma("sp", xt[ti % 4][:, :], k.xs[ti * 128:(ti + 1) * 128, :])

        def body(ti):
            x_, s_, y_ = xt[ti % 4], ss[ti % 3], yo[ti % 3]
            P.act(junk2[ti % 2][:, :], x_[:, :], AF.Square, accum_out=s_[:, :])
            yield
            P.act(s_[:, :], s_[:, :], AF.Sqrt, bias=EPSB[0][:, :], scale=1.0 / D)
            yield
            P.recip(s_[:, :], s_[:, :])
            yield
            P.stt(y_[:, :], x_[:, :], s_[:, :], g[:, :], ALU.mult, ALU.mult)
            P.dma("sp", k.y[ti * 128:(ti + 1) * 128, :], y_[:, :])

        prefetch_pipelined(NTL, load, body, 2, 2)
        P.flush()
def build(stages=99, layers=DEPTH):
    nc = bass.Bass("TRN2", target_bir_lowering=False)
    k = declare(nc)
    declare_ml(nc, k)
    declare_hy(nc, k)
    P = Prog(nc)
    with ExitStack() as ges:
        make_consts(P, ges)
        phase_init(P, k)
        if stages >= 0.5:
            phase_mod(P, k)
        for l in range(layers if stages >= 1 else 0):
            wco = l < DEPTH - 1
            phase_norm(P, k, l, k.xs, k.ln1_g, 1, 0)
            if stages >= 2:
                phase_inproj(P, k, l)
            if stages >= 3 and 'noattn' not in DEBUG:
                phase_ga_prep(P, k, l)
                phase_ga_attn(P, k, l, wco)
            if stages >= 4 and 'noattn' not in DEBUG:
                phase_wa_prep(P, k, l)
                phase_wa_attn(P, k, l, wco)
            if stages >= 5 and 'noml' not in DEBUG:
                phase_ml_prep(P, k, l)
                phase_ml_scan(P, k, l)
                phase_ml_finish(P, k, l, wco)
            if stages >= 6 and 'nohy' not in DEBUG:
                phase_hy_conv(P, k, l)
                phase_hy_filt(P, k, l)
                phase_hy_spec(P, k, l)
                phase_hy_data(P, k, l)
                if wco:
                    phase_hy_ctx(P, k, l)
            if 'ya_in' in DEBUG:
                ya_dbg = dram(nc, f"ya_dbg{l}", [512, T], BF16, "ExternalInput")
                P.dma("sp", k.yT[0, :, :], ya_dbg[:, :])
                P.flush()
            if stages >= 7:
                phase_merge(P, k, l, wco)
                phase_norm(P, k, l, k.xs, k.ln2_g, 4, 3, do_ctx=wco)
                phase_mlp(P, k, l, wco)
        if stages >= 8:
            phase_final(P, k)
        else:
            P.dma("sp", k.y[0:128, :], k.xs[0:128, :])
            P.flush()
    P.es.close()
    return nc


def make_inputs(inputs, b):
    f = lambda a: np.ascontiguousarray(np.asarray(a, dtype=np.float32))
    m = {
        "x": f(inputs["x"][b]),
        "ctx": f(inputs["ctx"][b]),
        "cvec": f(np.stack([inputs["c"][b], inputs["c_ctx"]])),
        "final_g": f(inputs["final_g"]).reshape(1, D),
        "ident": np.eye(128, dtype=np.float32),
    }
    m.update(CONSTS)
    for n in ["w_mod", "b_mod", "ln1_g", "ln2_g", "w_in", "w_up", "w_out", "mlp_w1", "mlp_b1", "mlp_w2", "mlp_b2",
              "ga_q_g", "ga_k_g", "wa_sink", "ml_conv_w", "ml_conv_b", "ml_gate_b", "ml_norm_g"] + HY_IN:
        m[n] = f(inputs[n])
    return m


_NC = [None]


def kernel(**inputs):
    if _NC[0] is None:
        _NC[0] = build()
    nc = _NC[0]
    in_maps = [make_inputs(inputs, b) for b in range(8)]
    res = run_bass_kernel_spmd(nc, in_maps, core_ids=list(range(8)))
    return np.stack([r["y"] for r in res.results], axis=0)
```

```python
from concourse.bass_utils import run_bass_kernel_spmd
import sys
import numpy as np
import concourse.bass as bass
import concourse.mybir as mybir
from contextlib import ExitStack

F32 = mybir.dt.float32
BF16 = mybir.dt.bfloat16
AF = mybir.ActivationFunctionType
ALU = mybir.AluOpType
AX = mybir.AxisListType


class V:
    __slots__ = ("ap", "key")

    def __init__(self, ap, key):
        self.ap = ap
        self.key = key


class Buf:
    def __init__(self, handle, key, track=True):
        self.h = handle
        self.key = key
        self.track = track

    def __getitem__(self, idx):
        return V(self.h[idx], self.key if self.track else None)

    def v(self, idx, sub):
        return V(self.h[idx], (self.key, sub))

    def ap(self, ap, sub=None):
        return V(ap, (self.key, sub) if sub is not None else (self.key if self.track else None))


class Op:
    __slots__ = ("eng", "fn", "reads", "writes", "dma", "deps", "tok", "waits", "marked")


ENGS = ["pe", "act", "dve", "pool", "sp"]
BLOCKNAME = {"pe": "tensor", "act": "scalar", "dve": "vector", "pool": "gpsimd", "sp": "sync"}
NDMASEM = 12


class Prog:
    def __init__(self, nc):
        self.nc = nc
        self.ops = []
        self.es = ExitStack()
        self.sems = {e: self.es.enter_context(nc.semaphore("cs_" + e)) for e in ENGS}
        self.dsems = {q: [self.es.enter_context(nc.semaphore(f"ds_{q}{i}")) for i in range(NDMASEM)]
                      for q in ("sp", "pool", "act")}
        self.allsems = list(self.sems.values()) + [s for l in self.dsems.values() for s in l]
        self.nphase = 0
        self.phase_names = []
        self.uid = 0

    def sb(self, es, name, shape, dtype):
        self.uid += 1
        nm = f"{name}_{self.uid}"
        h = es.enter_context(self.nc.sbuf_tensor(nm, list(shape), dtype))
        return Buf(h, nm)

    def sbpool(self, es, name, n, shape, dtype):
        return [self.sb(es, f"{name}{i}", shape, dtype) for i in range(n)]

    def ps(self, es, name, shape=(128, 512), dtype=F32):
        self.uid += 1
        nm = f"{name}_{self.uid}"
        h = es.enter_context(self.nc.psum_tensor(nm, list(shape), dtype))
        return Buf(h, nm)

    def op(self, eng, fn, reads, writes, dma=False):
        o = Op()
        o.eng = eng
        o.fn = fn
        o.reads = [k for k in reads if k is not None]
        o.writes = [k for k in writes if k is not None]
        o.dma = dma
        self.ops.append(o)
        return o

    @staticmethod
    def _ks(vs):
        return [v.key for v in vs if isinstance(v, V)]

    @staticmethod
    def _a(v):
        return v.ap if isinstance(v, V) else v

    def dma(self, q, out, in_, **kw):
        return self.op(q, lambda e: e.dma_start(out=out.ap, in_=in_.ap, **kw), [in_.key], [out.key], dma=True)

    def mm(self, out, lhsT, rhs, start=True, stop=True, **kw):
        return self.op("pe", lambda e: e.matmul(out.ap, lhsT.ap, rhs.ap, start=start, stop=stop, **kw),
                       [lhsT.key, rhs.key], [out.key])

    def transpose(self, out, in_, ident):
        return self.op("pe", lambda e: e.transpose(out.ap, in_.ap, ident.ap), [in_.key, ident.key], [out.key])

    def act(self, out, in_, func, bias=0.0, scale=1.0, accum_out=None, eng="act"):
        a = self._a
        rd = self._ks([in_, bias, scale])
        wr = self._ks([out, accum_out])
        kw = {}
        if accum_out is not None:
            kw["accum_out"] = accum_out.ap
        return self.op(eng, lambda e: e.activation(out.ap, in_.ap, func, bias=a(bias), scale=a(scale), **kw), rd, wr)

    def tt(self, out, in0, in1, op, eng="dve"):
        return self.op(eng, lambda e: e.tensor_tensor(out.ap, in0.ap, in1.ap, op), [in0.key, in1.key], [out.key])

    def ts(self, out, in0, s1, s2=None, op0=ALU.mult, op1=None, accum_out=None, eng="dve"):
        a = self._a
        rd = self._ks([in0, s1, s2])
        wr = self._ks([out, accum_out])
        kw = {}
        if op1 is not None:
            kw["op1"] = op1
        if accum_out is not None:
            kw["accum_out"] = accum_out.ap
        return self.op(eng, lambda e: e.tensor_scalar(out.ap, in0.ap, a(s1), a(s2) if s2 is not None else None, op0, **kw), rd, wr)

    def stt(self, out, in0, scalar, in1, op0, op1, eng="dve"):
        a = self._a
        rd = self._ks([in0, scalar, in1])
        return self.op(eng, lambda e: e.scalar_tensor_tensor(out.ap, in0.ap, a(scalar), in1.ap, op0, op1), rd, [out.key])

    def copy(self, out, in_, eng="dve"):
        return self.op(eng, lambda e: e.tensor_copy(out.ap, in_.ap), [in_.key], [out.key])

    def memset(self, out, val, eng="dve"):
        return self.op(eng, lambda e: e.memset(out.ap, val), [], [out.key])

    def recip(self, out, in_):
        return self.op("dve", lambda e: e.reciprocal(out.ap, in_.ap), [in_.key], [out.key])

    def reduce(self, out, in_, op=ALU.add, axis=AX.X, eng="dve"):
        return self.op(eng, lambda e: e.tensor_reduce(out.ap, in_.ap, axis, op), [in_.key], [out.key])

    def flush(self, name=None):
        ops = self.ops
        self.ops = []
        if not ops:
            return
        self.phase_names.append(name or sys._getframe(1).f_code.co_name)
        nc = self.nc
        last_w = {}
        readers = {}
        for i, o in enumerate(ops):
            deps = set()
            for k in o.reads:
                if k in last_w:
                    deps.add(last_w[k])
            for k in o.writes:
                if k in last_w:
                    deps.add(last_w[k])
                for r in readers.get(k, ()):
                    deps.add(r)
            deps.discard(i)
            o.deps = [d for d in deps if not (o.eng == "pe" and ops[d].eng == "pe" and not ops[d].dma and not o.dma)]
            for k in o.reads:
                readers.setdefault(k, []).append(i)
            for k in o.writes:
                last_w[k] = i
                readers[k] = []
            o.marked = False
            o.tok = None
            o.waits = []
        for o in ops:
            for d in o.deps:
                ops[d].marked = True
        cnt = {e: 0 for e in ENGS}
        dcum = {}
        dnext = {q: 0 for q in self.dsems}
        dprev = {}
        for o in ops:
            if o.dma:
                q = o.eng
                pool = self.dsems[q]
                s = pool[dnext[q] % NDMASEM]
                dnext[q] += 1
                if s in dprev:
                    o.waits.append(dprev[s])
                dcum[s] = dcum.get(s, 0) + 16
                o.tok = (s, dcum[s], 16)
                dprev[s] = (s, dcum[s])
            elif o.marked:
                cnt[o.eng] += 1
                o.tok = (self.sems[o.eng], cnt[o.eng], 1)
        for o in ops:
            for d in o.deps:
                t = ops[d].tok
                o.waits.append((t[0], t[1]))
        per = {e: [o for o in ops if o.eng == e] for e in ENGS}
        with nc.Block() as block:
            for e in ENGS:
                lst = per[e]
                if not lst:
                    continue

                def body(eng, lst=lst, e=e):
                    known = {}
                    for o in lst:
                        for (s, v) in o.waits:
                            if known.get(s, 0) < v:
                                eng.wait_ge(s, v)
                                known[s] = v
                        ins = o.fn(eng)
                        if o.tok is not None:
                            ins.then_inc(o.tok[0], o.tok[2])
                    if e in self.dsems:
                        for s in self.dsems[e]:
                            if s in dprev and known.get(s, 0) < dprev[s][1]:
                                eng.wait_ge(s, dprev[s][1])

                getattr(block, BLOCKNAME[e])(body)
        with nc.Block() as block:
            @block.sync
            def _(eng):
                for s in self.allsems:
                    eng.sem_clear(s)
        self.nphase += 1


def prefetch_loop(n, load_fn, body_fn, pf=2):
    for i in range(n + pf):
        if i < n:
            load_fn(i)
        if i >= pf:
            body_fn(i - pf)


def prefetch_pipelined(n, load_fn, body_gen, pf=2, G=2):
    nl = 0
    for g0 in range(0, n, G):
        while nl < min(n, g0 + G + pf):
            load_fn(nl)
            nl += 1
        alive = [body_gen(i) for i in range(g0, min(n, g0 + G))]
        while alive:
            nxt = []
            for g in alive:
                try:
                    next(g)
                    nxt.append(g)
                except StopIteration:
                    pass
            alive = nxt

L = 8192
LC = 256
T = L + LC
D = 1024
NIN = 9488
NPROJ = 5392
DEPTH = 2
EPS = 1e-6
TB = [(i * 512, 512) for i in range(16)] + [(8192, 256)]

DEBUG = {}


def dram(nc, name, shape, dtype=F32, kind=None):
    if kind is None and name in DEBUG:
        kind = "ExternalOutput"
    if kind is None:
        t = nc.dram_tensor(name, list(shape), dtype)
    else:
        t = nc.dram_tensor(name, list(shape), dtype, kind=kind)
    return Buf(t.ap(), name, track=False)


class K:
    pass


def declare(nc):
    k = K()
    inp = lambda n, s: dram(nc, n, s, F32, "ExternalInput")
    k.x = inp("x", [L, D])
    k.ctx = inp("ctx", [LC, D])
    k.cvec = inp("cvec", [2, D])
    k.w_mod = inp("w_mod", [DEPTH, D, 6 * D])
    k.b_mod = inp("b_mod", [DEPTH, 6 * D])
    k.ln1_g = inp("ln1_g", [DEPTH, D])
    k.ln2_g = inp("ln2_g", [DEPTH, D])
    k.w_in = inp("w_in", [DEPTH, D, NIN])
    k.w_up = inp("w_up", [DEPTH, 4, 512, D])
    k.w_out = inp("w_out", [DEPTH, D, D])
    k.mlp_w1 = inp("mlp_w1", [DEPTH, D, 4 * D])
    k.mlp_b1 = inp("mlp_b1", [DEPTH, 4 * D])
    k.mlp_w2 = inp("mlp_w2", [DEPTH, 4 * D, D])
    k.mlp_b2 = inp("mlp_b2", [DEPTH, D])
    k.final_g = inp("final_g", [1, D])
    k.ident = inp("ident", [128, 128])
    k.y = dram(nc, "y", [L, D], F32, "ExternalOutput")
    k.xs = dram(nc, "xs", [T, D])
    k.mod = dram(nc, "mod", [DEPTH, 2, 6 * D])
    k.hT = dram(nc, "hT", [D, T], BF16)
    k.hyT = dram(nc, "hyT", [1536, T])
    k.ga = dram(nc, "ga", [T, 1024])
    k.mlqkT = dram(nc, "mlqkT", [1024, T])
    k.mlvo = dram(nc, "mlvo", [T, 1024 + 16])
    k.wa = dram(nc, "wa", [T, 768])
    k.ga_q_g = inp("ga_q_g", [DEPTH, 128])
    k.ga_k_g = inp("ga_k_g", [DEPTH, 128])
    k.wa_sink = inp("wa_sink", [DEPTH, 8])
    k.rc128 = inp("rc128", [L, 64])
    k.rs128 = inp("rs128", [L, 64])
    k.rc64 = inp("rc64", [L, 32])
    k.rs64 = inp("rs64", [L, 32])
    k.mlo = inp("mlo", [128, 128])
    k.mhi = inp("mhi", [128, 128])
    k.gaQT = dram(nc, "gaQT", [4, 128, T], BF16)
    k.gaKT = dram(nc, "gaKT", [2, 128, T], BF16)
    k.gaV = dram(nc, "gaV", [T, 256], BF16)
    k.waQT = dram(nc, "waQT", [128, 4, T], BF16)
    k.waKT = dram(nc, "waKT", [128, T], BF16)
    k.waV = dram(nc, "waV", [T, 128], BF16)
    k.yT = dram(nc, "yT", [4, 512, T], BF16)
    return k


def load_ident(P, es, k):
    idf = P.sb(es, "idf", [128, 128], F32)
    idb = P.sb(es, "idb", [128, 128], BF16)
    P.dma("sp", idf[:, :], k.ident[:, :])
    P.copy(idb[:, :], idf[:, :])
    return idf, idb


_STG = [0]


def mk_stage(P, es, n=3, w=2048):
    return P.sbpool(es, "stg", n, [128, w], F32)


def load_cast(P, stg, dst, src, ncol, q="sp"):
    i = _STG[0]
    _STG[0] += 1
    st = stg[i % len(stg)]
    P.dma(q, st[:, 0:ncol], src)
    if i % 2 == 0:
        P.act(dst, st[:, 0:ncol], AF.Copy)
    else:
        P.copy(dst, st[:, 0:ncol])

def phase_mod(P, k):
    nc = P.nc
    with ExitStack() as es:
        cv = P.sb(es, "cv", [128, 2, 8], F32)
        sT = P.sb(es, "sT", [128, 8, 2], BF16)
        P.dma("sp", cv[:, :, :], V(k.cvec.h.rearrange("j (p c) -> p j c", c=8), None))
        sg = P.sb(es, "sg", [128, 2, 8], F32)
        P.act(sg[:, :, :], cv[:, :, :], AF.Sigmoid)
        for j in range(2):
            P.tt(sT[:, :, j], cv[:, j, :], sg[:, j, :], ALU.mult)
        wts = P.sbpool(es, "wm", 2, [128, 8, 2048], BF16)
        stg = mk_stage(P, es)
        bm = P.sb(es, "bm", [2, 6 * D], F32)
        ot = P.sb(es, "ot", [2, 6 * D], F32)
        pss = [P.ps(es, f"pm{i}") for i in range(2)]
        n = 0
        for l in range(DEPTH):
            P.dma("sp", bm[:, :], V(k.b_mod.h[l:l + 1, :].broadcast(0, 2) if False else k.b_mod.h[l:l + 1, :].to_broadcast([2, 6 * D]), None))
            for fb in range(3):
                w = wts[n % 2]
                wv = k.w_mod.h[l].rearrange("(p c) f -> p c f", c=8)
                for c in range(8):
                    load_cast(P, stg, w[:, c, :], V(wv[:, c, fb * 2048:(fb + 1) * 2048], None), 2048)
                for s in range(4):
                    ps = pss[s % 2]
                    for c in range(8):
                        P.mm(ps[0:2, :], sT[:, c, :], w[:, c, s * 512:(s + 1) * 512], start=(c == 0), stop=(c == 7))
                    f0 = fb * 2048 + s * 512
                    P.tt(ot[:, f0:f0 + 512], ps[0:2, :], bm[:, f0:f0 + 512], ALU.add)
                n += 1
            P.dma("sp", k.mod[l, :, :], ot[:, :])
            P.flush()


def phase_norm(P, k, l, src, g_dram, sc_idx, sh_idx, ntiles_lat=64, do_ctx=True):
    with ExitStack() as es:
        idf, idb = load_ident(P, es, k)
        A = P.sbpool(es, "A", 2, [128, D], F32)
        Bt = P.sbpool(es, "B", 2, [128, D], F32)
        gt = P.sb(es, "gt", [128, D], F32)
        P.dma("sp", gt[:, :], V(g_dram.h[l:l + 1, :].to_broadcast([128, D]), None))
        for j in range(2):
            P.dma("sp", A[j][:, :], V(k.mod.h[l, j:j + 1, sc_idx * D:(sc_idx + 1) * D].to_broadcast([128, D]), None))
            P.dma("sp", Bt[j][:, :], V(k.mod.h[l, j:j + 1, sh_idx * D:(sh_idx + 1) * D].to_broadcast([128, D]), None))
            P.stt(A[j][:, :], A[j][:, :], 1.0, gt[:, :], ALU.add, ALU.mult)
        xt = P.sbpool(es, "xt", 4, [128, D], F32)
        junk2 = P.sbpool(es, "junk", 2, [128, D], F32)
        ssq = P.sbpool(es, "ssq", 3, [128, 1], F32)
        rstd = P.sbpool(es, "rstd", 3, [128, 1], F32)
        t1 = P.sbpool(es, "t1", 2, [128, D], F32)
        hb = P.sbpool(es, "hb", 2, [128, D], BF16)
        hTs = P.sbpool(es, "hTs", 3, [128, 8, 128], BF16)
        pst = [P.ps(es, f"pt{i}", [128, 8, 128], BF16) for i in range(2)]
        tiles = list(range(ntiles_lat)) + ([64, 65] if do_ctx else [])

        def load(n):
            ti = tiles[n]
            P.dma("sp", xt[n % 4][:, :], src[ti * 128:(ti + 1) * 128, :])

        def body(n):
            ti = tiles[n]
            j = 0 if ti < 64 else 1
            x_ = xt[n % 4]
            P.act(junk2[n % 2][:, :], x_[:, :], AF.Square, accum_out=ssq[n % 3][:, :])
            yield
            P.act(rstd[n % 3][:, :], ssq[n % 3][:, :], AF.Sqrt, bias=EPSB[0][:, :], scale=1.0 / D)
            yield
            P.recip(rstd[n % 3][:, :], rstd[n % 3][:, :])
            yield
            P.stt(t1[n % 2][:, :], x_[:, :], rstd[n % 3][:, :], A[j][:, :], ALU.mult, ALU.mult)
            yield
            P.tt(hb[n % 2][:, :], t1[n % 2][:, :], Bt[j][:, :], ALU.add)
            yield
            ps = pst[n % 2]
            for c in range(8):
                P.transpose(ps[:, c, :], hb[n % 2][:, c * 128:(c + 1) * 128], idb[:, :])
            yield
            hs = hTs[n % 3]
            P.copy(hs[:, :, :], ps[:, :, :])
            P.dma("sp", V(k.hT.h.rearrange("(c p) t -> p c t", p=128)[:, :, ti * 128:(ti + 1) * 128], None), hs[:, :, :])

        prefetch_pipelined(len(tiles), load, body, 2, 2)
        P.flush()


EPSB = [None]


def make_consts(P, es):
    e = P.sb(es, "epsb", [128, 1], F32)
    P.memset(e[:, :], EPS)
    EPSB[0] = e
    P.flush()


def phase_inproj(P, k, l):
    groups = [
        (0, 1536, "fm", k.hyT, 0),
        (1536, 1024, "tm", k.ga, 0),
        (2560, 1024, "fm", k.mlqkT, 0),
        (3584, 1024 + 16, "tm", k.mlvo, 0),
        (4624, 768, "tm", k.wa, 0),
    ]
    with ExitStack() as es:
        W = P.sb(es, "Win", [128, 8, NPROJ], BF16)
        wv = k.w_in.h[l].rearrange("(c p) n -> p c n", p=128)
        stg = mk_stage(P, es)
        for c in range(8):
            for h in range(4):
                load_cast(P, stg, W[:, c, h * 1348:(h + 1) * 1348], V(wv[:, c, h * 1348:(h + 1) * 1348], None), 1348)
        hts = P.sbpool(es, "ht", 2, [128, 8, 512], BF16)
        pss = [P.ps(es, f"pi{i}") for i in range(4)]
        osb = P.sbpool(es, "osb", 4, [128, 512], F32)
        hTv = k.hT.h.rearrange("(c p) t -> p c t", p=128)
        n = 0
        P.dma("sp", hts[0][:, :, 0:TB[0][1]], V(hTv[:, :, TB[0][0]:TB[0][0] + TB[0][1]], None))
        for bi, (t0, tn) in enumerate(TB):
            ht = hts[bi % 2]
            if bi + 1 < len(TB):
                t0n, tnn = TB[bi + 1]
                P.dma("sp", hts[(bi + 1) % 2][:, :, 0:tnn], V(hTv[:, :, t0n:t0n + tnn], None))
            for (c0, ncol, mode, dst, doff) in groups:
                if mode == "fm":
                    for cc in range(0, ncol, 128):
                        ps = pss[n % 4]
                        for c in range(8):
                            P.mm(ps[:, 0:tn], W[:, c, c0 + cc:c0 + cc + 128], ht[:, c, 0:tn], start=(c == 0), stop=(c == 7))
                        o = osb[n % 4]
                        P.act(o[:, 0:tn], ps[:, 0:tn], AF.Copy)
                        P.dma("sp", dst[doff + cc:doff + cc + 128, t0:t0 + tn], o[:, 0:tn])
                        n += 1
                else:
                    for ts_ in range(0, tn, 128):
                        for cc in range(0, ncol, 512):
                            w_ = min(512, ncol - cc)
                            ps = pss[n % 4]
                            for c in range(8):
                                P.mm(ps[:, 0:w_], ht[:, c, ts_:ts_ + 128], W[:, c, c0 + cc:c0 + cc + w_], start=(c == 0), stop=(c == 7))
                            o = osb[n % 4]
                            P.act(o[:, 0:w_], ps[:, 0:w_], AF.Copy)
                            P.dma("sp", dst[t0 + ts_:t0 + ts_ + 128, doff + cc:doff + cc + w_], o[:, 0:w_])
                            n += 1
        P.flush()


def phase_init(P, k):
    for i in range(8):
        P.dma("sp", k.xs[i * 1024:(i + 1) * 1024, :], k.x[i * 1024:(i + 1) * 1024, :])
    P.dma("sp", k.xs[L:T, :], k.ctx[:, :])
    P.flush()


def _rope_tables(d):
    quarter = d // 4
    t = np.arange(L)
    row = (t // 64).astype(np.float64)
    col = (t % 64).astype(np.float64)
    inv = 10000.0 ** (-np.arange(quarter, dtype=np.float64) / quarter)
    ang = np.stack([row[:, None] * inv[None, :], col[:, None] * inv[None, :]], axis=1)
    return (np.cos(ang).reshape(L, 2 * quarter).astype(np.float32),
            np.sin(ang).reshape(L, 2 * quarter).astype(np.float32))


def _make_consts():
    c = {}
    c["rc128"], c["rs128"] = _rope_tables(128)
    c["rc64"], c["rs64"] = _rope_tables(64)
    p = np.arange(128)[:, None]
    f = np.arange(128)[None, :]
    c["mlo"] = (p >= f).astype(np.float32)
    c["mhi"] = (p <= f).astype(np.float32)
    return c


CONSTS = _make_consts()


NT = T // 128
NTL = L // 128


def bcast(v, shape, axis):
    return V(v.ap.unsqueeze(axis).to_broadcast(list(shape)), v.key)


def rope_apply(P, out, xin, Ct, St, nh, q, tmp):
    xv = lambda b, half: V(b.h[:, :, :].rearrange("p h (b f j) -> p h b f j", b=2, f=2)[:, :, :, half, :], b.key)
    Cb = V(Ct.h[:, :, :].unsqueeze(1).to_broadcast([128, nh, 2, q]), Ct.key)
    Sb = V(St.h[:, :, :].unsqueeze(1).to_broadcast([128, nh, 2, q]), St.key)
    a, b_ = xv(xin, 0), xv(xin, 1)
    t1, t2, t3, t4 = [t[:, :, :, :] for t in tmp]
    P.tt(t1, a, Cb, ALU.mult)
    P.tt(t2, b_, Sb, ALU.mult)
    P.tt(t3, a, Sb, ALU.mult)
    P.tt(t4, b_, Cb, ALU.mult)
    P.tt(xv(out, 0), t1, t2, ALU.subtract)
    P.tt(xv(out, 1), t3, t4, ALU.add)


def phase_ga_prep(P, k, l):
    with ExitStack() as es:
        idf, idb = load_ident(P, es, k)
        gq = P.sb(es, "gq", [128, 6, 128], F32)
        for h in range(4):
            P.dma("sp", gq[:, h, :], V(k.ga_q_g.h[l:l + 1, :].to_broadcast([128, 128]), None))
        for h in range(2):
            P.dma("sp", gq[:, 4 + h, :], V(k.ga_k_g.h[l:l + 1, :].to_broadcast([128, 128]), None))
        P.ts(gq[:, 0:4, :], gq[:, 0:4, :], 128 ** -0.5, None, ALU.mult)
        xt = P.sbpool(es, "gx", 4, [128, 1024], F32)
        sq2 = P.sbpool(es, "gsq", 2, [128, 6, 128], F32)
        ssq = P.sbpool(es, "gss", 2, [128, 6], F32)
        rstd = P.sbpool(es, "grs", 2, [128, 6], F32)
        xn = P.sbpool(es, "gxn", 2, [128, 6, 128], F32)
        xr = P.sbpool(es, "gxr", 2, [128, 6, 128], BF16)
        vb = P.sbpool(es, "gvb", 2, [128, 256], BF16)
        Ct = P.sbpool(es, "gC", 4, [128, 2, 32], F32)
        St = P.sbpool(es, "gS", 4, [128, 2, 32], F32)
        tmp2 = [[P.sb(es, f"gt{j}_{i}", [128, 6, 2, 32], F32) for i in range(4)] for j in range(2)]
        pst = [P.ps(es, f"gpt{i}", [128, 6, 128], BF16) for i in range(2)]
        qkT = P.sbpool(es, "gqkT", 2, [128, 6, 128], BF16)
        def load(ti):
            t0 = ti * 128
            P.dma("sp", xt[ti % 4][:, :], k.ga[t0:t0 + 128, :])
            if ti < NTL:
                P.dma("sp", Ct[ti % 4][:, :, :], V(k.rc128.h[t0:t0 + 128, :].rearrange("p (b j) -> p b j", b=2), None))
                P.dma("sp", St[ti % 4][:, :, :], V(k.rs128.h[t0:t0 + 128, :].rearrange("p (b j) -> p b j", b=2), None))

        def body(ti):
            t0 = ti * 128
            x_ = xt[ti % 4]
            qk = V(x_.h[:, 0:768].rearrange("p (h d) -> p h d", d=128), x_.key)
            sq_ = sq2[ti % 2]
            P.tt(sq_[:, :, :], qk, qk, ALU.mult)
            yield
            ss, rs = ssq[ti % 2], rstd[ti % 2]
            P.reduce(ss[:, :], sq_[:, :, :])
            yield
            P.act(rs[:, :], ss[:, :], AF.Sqrt, bias=EPSB[0][:, :], scale=1.0 / 128)
            yield
            P.recip(rs[:, :], rs[:, :])
            yield
            n_ = xn[ti % 2]
            P.tt(n_[:, :, :], qk, bcast(rs[:, :], [128, 6, 128], 2), ALU.mult)
            yield
            P.tt(n_[:, :, :], n_[:, :, :], gq[:, :, :], ALU.mult)
            yield
            r_ = xr[ti % 2]
            if ti < NTL:
                rope_apply(P, r_, n_, Ct[ti % 4], St[ti % 4], 6, 32, tmp2[ti % 2])
            else:
                P.copy(r_[:, :, :], n_[:, :, :])
            yield
            ps = pst[ti % 2]
            for h in range(6):
                P.transpose(ps[:, h, :], r_[:, h, :], idb[:, :])
            yield
            o = qkT[ti % 2]
            P.act(o[:, :, :], ps[:, :, :], AF.Copy)
            P.dma("sp", V(k.gaQT.h[:, :, t0:t0 + 128].rearrange("h d t -> d h t"), None), o[:, 0:4, :])
            P.dma("sp", V(k.gaKT.h[:, :, t0:t0 + 128].rearrange("h d t -> d h t"), None), o[:, 4:6, :])
            v_ = vb[ti % 2]
            P.act(v_[:, :], x_[:, 768:1024], AF.Copy)
            P.dma("sp", k.gaV[t0:t0 + 128, :], v_[:, :])

        prefetch_pipelined(NT, load, body, 2, 2)
        P.flush()


def attn_core(P, es, QTsrc, KT, Vaug, dh, qblocks, kblocks_fn, nsub_heads, finalize, psS, psO, scale_mask=None):
    raise NotImplementedError


def emit_skewed(units, skew):
    n = len(units)
    for i in range(n + skew):
        if i < n:
            units[i][0]()
        if i >= skew:
            units[i - skew][1]()


def phase_ga_attn(P, k, l, with_ctx_out):
    SK = 2
    with ExitStack() as es:
        KTs = P.sbpool(es, "aKT", 2, [128, T], BF16)
        Vts = P.sbpool(es, "aVt", 2, [128, NT, 128], BF16)
        ones = P.sb(es, "aones", [128, 128], F32)
        P.memset(ones[:, :], 1.0)
        QT = P.sbpool(es, "aQT", 3, [128, 512], BF16)
        PT = P.sbpool(es, "aPT", SK + 3, [128, 512], BF16)
        NACC = 3
        pairs = P.sbpool(es, "apair", 3, [128, 512], BF16)
        pprev = [None]
        accs = [P.sbpool(es, f"aacc{i}", NACC, [128, 512], F32) for i in range(2)]
        psS = [P.ps(es, f"apS{i}") for i in range(SK + 1)]
        psO = [P.ps(es, f"apO{i}") for i in range(2)]
        psD = P.ps(es, "apD")
        rec = P.sbpool(es, "arec", 2, [128, 512], F32)
        ob = P.sbpool(es, "aob", 2, [128, 512], BF16)
        units = []
        qloads = []
        nq = 0
        nu = 0
        for g in range(2):
            KT, Vt = KTs[g], Vts[g]
            P.dma("sp", KT[:, :], k.gaKT[g, :, :])
            P.dma("sp", Vt[:, :, :], V(k.gaV.h[:, g * 128:(g + 1) * 128].rearrange("(n p) c -> p n c", p=128), None))
            for hh in range(2):
                h = g * 2 + hh
                qbl = [(i * 512, 512, list(range(NT))) for i in range(L // 512)]
                if with_ctx_out:
                    qbl.append((L, LC, list(range(NTL, NT))))
                for (q0, qn, kbs) in qbl:
                    q_ = QT[nq % 3]
                    qloads.append(lambda q_=q_, qn=qn, h=h, q0=q0: P.dma("sp", q_[:, 0:qn], k.gaQT[h, :, q0:q0 + qn]))
                    pO = psO[nq % 2]
                    ac = accs[nq % 2]
                    r_, o_ = rec[nq % 2], ob[nq % 2]
                    nkb = len(kbs)
                    for ki, kb in enumerate(kbs):
                        pS = psS[nu % (SK + 1)]
                        p_ = PT[nu % (SK + 3)]
                        nu += 1

                        def front(ki=ki, kb=kb, pS=pS, p_=p_, q_=q_, q0=q0, qn=qn, h=h, KT=KT, nq=nq):
                            if ki == 0:
                                if nq == 0:
                                    qloads[0]()
                                if nq + 1 < len(qloads):
                                    qloads[nq + 1]()
                            P.mm(pS[:, 0:qn], KT[:, kb * 128:(kb + 1) * 128], q_[:, 0:qn])
                            P.act(p_[:, 0:qn], pS[:, 0:qn], AF.Exp)

                        def back(ki=ki, kb=kb, p_=p_, qn=qn, pO=pO, ac=ac, nkb=nkb, Vt=Vt, r_=r_, o_=o_, h=h, q0=q0):
                            P.mm(pO[:, 0:qn], Vt[:, kb, :], p_[:, 0:qn], start=(ki == 0), stop=(ki == nkb - 1))
                            if ki % 2 == 1:
                                pr = pairs[(ki // 2) % 3]
                                P.tt(pr[:, 0:qn], pprev[0][:, 0:qn], p_[:, 0:qn], ALU.add)
                                a_ = ac[(ki // 2) % NACC]
                                if ki // 2 < NACC:
                                    P.copy(a_[:, 0:qn], pr[:, 0:qn])
                                else:
                                    P.tt(a_[:, 0:qn], a_[:, 0:qn], pr[:, 0:qn], ALU.add)
                            pprev[0] = p_
                            if ki == nkb - 1:
                                na = min(NACC, nkb // 2)
                                for i in range(na):
                                    P.mm(psD[:, 0:qn], ones[:, :], ac[i][:, 0:qn], start=(i == 0), stop=(i == na - 1))
                                P.recip(r_[:, 0:qn], psD[:, 0:qn])
                                P.tt(o_[:, 0:qn], pO[:, 0:qn], r_[:, 0:qn], ALU.mult)
                                P.dma("sp", k.yT[1, h * 128:(h + 1) * 128, q0:q0 + qn], o_[:, 0:qn])

                        units.append((front, back))
                    nq += 1
        emit_skewed(units, SK)
        P.flush()


def phase_wa_prep(P, k, l):
    with ExitStack() as es:
        idf, idb = load_ident(P, es, k)
        xt = P.sbpool(es, "wx", 4, [128, 768], F32)
        xs_ = P.sbpool(es, "wxs", 2, [128, 10, 64], F32)
        xr = P.sbpool(es, "wxr", 2, [128, 10, 64], BF16)
        vb = P.sbpool(es, "wvb", 2, [128, 128], BF16)
        Ct = P.sbpool(es, "wC", 4, [128, 2, 16], F32)
        St = P.sbpool(es, "wS", 4, [128, 2, 16], F32)
        tmp2 = [[P.sb(es, f"wt{j}_{i}", [128, 10, 2, 16], F32) for i in range(4)] for j in range(2)]
        pst = [P.ps(es, f"wpt{i}", [128, 5, 128], BF16) for i in range(2)]
        qkT = P.sbpool(es, "wqkT", 2, [128, 5, 128], BF16)
        def load(ti):
            t0 = ti * 128
            P.dma("sp", xt[ti % 4][:, :], k.wa[t0:t0 + 128, :])
            if ti < NTL:
                P.dma("sp", Ct[ti % 4][:, :, :], V(k.rc64.h[t0:t0 + 128, :].rearrange("p (b j) -> p b j", b=2), None))
                P.dma("sp", St[ti % 4][:, :, :], V(k.rs64.h[t0:t0 + 128, :].rearrange("p (b j) -> p b j", b=2), None))

        def body(ti):
            t0 = ti * 128
            x_ = xt[ti % 4]
            qk = V(x_.h[:, 0:640].rearrange("p (h d) -> p h d", d=64), x_.key)
            s_ = xs_[ti % 2]
            P.ts(V(s_.h[:, 0:8, :].rearrange("p (j g) d -> p g j d", g=2), s_.key),
                 V(x_.h[:, 0:512].rearrange("p (g j d) -> p g j d", g=2, j=4), x_.key), 64 ** -0.5, None, ALU.mult)
            P.act(s_[:, 8:10, :], V(qk.ap[:, 8:10, :], x_.key), AF.Copy)
            yield
            r_ = xr[ti % 2]
            if ti < NTL:
                rope_apply(P, r_, s_, Ct[ti % 4], St[ti % 4], 10, 16, tmp2[ti % 2])
            else:
                P.copy(r_[:, :, :], s_[:, :, :])
            yield
            ps = pst[ti % 2]
            for j in range(4):
                src = V(r_.h[:, 2 * j:2 * j + 2, :].rearrange("p h d -> p (h d)"), r_.key)
                P.transpose(ps[:, j, :], src, idb[:, :])
            P.transpose(ps[:, 4, :], V(r_.h[:, 8:10, :].rearrange("p h d -> p (h d)"), r_.key), idb[:, :])
            yield
            o = qkT[ti % 2]
            P.act(o[:, :, :], ps[:, :, :], AF.Copy)
            P.dma("sp", k.waQT[:, :, t0:t0 + 128], o[:, 0:4, :])
            P.dma("sp", k.waKT[:, t0:t0 + 128], o[:, 4, :])
            v_ = vb[ti % 2]
            P.act(v_[:, :], x_[:, 640:768], AF.Copy)
            P.dma("sp", k.waV[t0:t0 + 128, :], v_[:, :])

        prefetch_pipelined(NT, load, body, 2, 2)
        P.flush()


def phase_wa_attn(P, k, l, with_ctx_out):
    SK = 2
    with ExitStack() as es:
        idf, idb = load_ident(P, es, k)
        KT = P.sb(es, "bKT", [128, T], BF16)
        Va = P.sb(es, "bVa", [128, NT, 2, 65], BF16)
        P.dma("sp", KT[:, :], k.waKT[:, :])
        for g in range(2):
            P.dma("sp", Va[:, :, g, 0:64], V(k.waV.h[:, g * 64:(g + 1) * 64].rearrange("(n p) c -> p n c", p=128), None))
        P.memset(Va[:, :, :, 64:65], 1.0)
        mf = P.sb(es, "bmf", [128, 2, 128], F32)
        mb = P.sb(es, "bmb", [128, 2, 128], BF16)
        P.dma("sp", mf[:, 0, :], k.mlo[:, :])
        P.dma("sp", mf[:, 1, :], k.mhi[:, :])
        P.copy(mb[:, :, :], mf[:, :, :])
        esk = P.sb(es, "besk", [128, 8], F32)
        P.dma("sp", esk[:, :], V(k.wa_sink.h[l:l + 1, :].to_broadcast([128, 8]), None))
        P.act(esk[:, :], esk[:, :], AF.Exp)
        QT = P.sbpool(es, "bQT", 3, [128, 4, 128], BF16)
        PT = P.sbpool(es, "bPT", SK + 3, [128, 4, 128], BF16)
        psS = [P.ps(es, f"bpS{i}") for i in range(SK + 1)]
        psO = [P.ps(es, f"bpO{i}") for i in range(4)]
        psT = P.ps(es, "bpT", [128, 4, 128], BF16)
        den = P.sbpool(es, "bden", 8, [128, 1], F32)
        ob = P.sbpool(es, "bob", 2, [128, 8, 64], BF16)
        oT = P.sbpool(es, "boT", 2, [128, 4, 128], BF16)
        qtiles = list(range(NTL)) + (list(range(NTL, NT)) if with_ctx_out else [])
        units = []
        nu = 0
        for qi, ti in enumerate(qtiles):
            t0 = ti * 128
            q_ = QT[qi % 3]
            if ti < NTL:
                kbs = [(ti - 1, 0)] if ti > 0 else []
                kbs.append((ti, None))
                if ti < NTL - 1:
                    kbs.append((ti + 1, 1))
                kbs += [(NTL, None), (NTL + 1, None)]
            else:
                kbs = [(NTL, None), (NTL + 1, None)]
            o_ = ob[qi % 2]
            ot = oT[qi % 2]
            nkb = len(kbs)
            for g in range(2):
                for ki, (kb, mk) in enumerate(kbs):
                    pS = psS[nu % (SK + 1)]
                    p_ = PT[nu % (SK + 3)]
                    nu += 1

                    def front(g=g, ki=ki, kb=kb, mk=mk, pS=pS, p_=p_, q_=q_, t0=t0, qi=qi):
                        if g == 0 and ki == 0:
                            if qi == 0:
                                P.dma("sp", QT[0][:, :, :], k.waQT[:, :, qtiles[0] * 128:qtiles[0] * 128 + 128])
                            if qi + 1 < len(qtiles):
                                tn_ = qtiles[qi + 1] * 128
                                P.dma("sp", QT[(qi + 1) % 3][:, :, :], k.waQT[:, :, tn_:tn_ + 128])
                        P.mm(pS[:, :], KT[g * 64:(g + 1) * 64, kb * 128:(kb + 1) * 128],
                             V(q_.h[g * 64:(g + 1) * 64, :, :].rearrange("p j q -> p (j q)"), q_.key))
                        pv = V(p_.h[:, :, :].rearrange("p j q -> p (j q)"), p_.key)
                        P.act(pv, pS[:, :], AF.Exp)
                        if mk is not None:
                            P.tt(p_[:, :, :], p_[:, :, :], bcast(mb[:, mk, :], [128, 4, 128], 1), ALU.mult)

                    def back(g=g, ki=ki, kb=kb, p_=p_, nkb=nkb, o_=o_, ot=ot, t0=t0):
                        for j in range(4):
                            P.mm(psO[j][:, 0:65], p_[:, j, :], Va[:, kb, g, :], start=(ki == 0), stop=(ki == nkb - 1))
                        if ki == nkb - 1:
                            for j in range(4):
                                h = g * 4 + j
                                d_ = den[h]
                                P.tt(d_[:, :], psO[j][:, 64:65], esk[:, h:h + 1], ALU.add)
                                P.recip(d_[:, :], d_[:, :])
                                P.ts(o_[:, h, :], psO[j][:, 0:64], d_[:, :], None, ALU.mult)
                            if g == 1:
                                for c in range(4):
                                    P.transpose(psT[:, c, :], V(o_.h[:, 2 * c:2 * c + 2, :].rearrange("p h d -> p (h d)"), o_.key), idb[:, :])
                                P.act(ot[:, :, :], psT[:, :, :], AF.Copy)
                                P.dma("sp", V(k.yT.h[3, :, t0:t0 + 128].rearrange("(c p) t -> p c t", p=128), None), ot[:, :, :])

                    units.append((front, back))
        emit_skewed(units, SK)
        P.flush()

SEGS = [(0, L), (L, T)]


def conv3_fm(P, u, z, w, b, segs):
    P.act(u[:, 0:T], z[:, 0:T], AF.Identity, bias=b[:, 0:1], scale=w[:, 1:2])
    for (s, e) in segs:
        P.stt(u[:, s + 1:e], z[:, s:e - 1], w[:, 0:1], u[:, s + 1:e], ALU.mult, ALU.add)
        P.stt(u[:, s:e - 1], z[:, s + 1:e], w[:, 2:3], u[:, s:e - 1], ALU.mult, ALU.add)


def conv3_fm_gen(P, u, z, w, b, segs):
    P.act(u[:, 0:T], z[:, 0:T], AF.Identity, bias=b[:, 0:1], scale=w[:, 1:2])
    yield
    for (s, e) in segs:
        P.stt(u[:, s + 1:e], z[:, s:e - 1], w[:, 0:1], u[:, s + 1:e], ALU.mult, ALU.add)
        yield
        P.stt(u[:, s:e - 1], z[:, s + 1:e], w[:, 2:3], u[:, s:e - 1], ALU.mult, ALU.add)
        yield


def declare_ml(nc, k):
    inp = lambda n, s: dram(nc, n, s, F32, "ExternalInput")
    k.ml_conv_w = inp("ml_conv_w", [DEPTH, 3, 1024])
    k.ml_conv_b = inp("ml_conv_b", [DEPTH, 1024])
    k.ml_gate_b = inp("ml_gate_b", [DEPTH, 16])
    k.ml_norm_g = inp("ml_norm_g", [DEPTH, 512])
    k.mlQKT = dram(nc, "mlQKT", [1024, T], BF16)
    k.mlKtm = dram(nc, "mlKtm", [T, 512], BF16)
    k.mlH = dram(nc, "mlH", [2, T, 512])
    k.mlVa = dram(nc, "mlVa", [T, 4, 129], BF16)


def phase_ml_prep(P, k, l):
    with ExitStack() as es:
        idf, idb = load_ident(P, es, k)
        zt = P.sbpool(es, "mz", 2, [128, T], F32)
        ut = P.sbpool(es, "mu", 2, [128, T], F32)
        ab = P.sbpool(es, "ma", 2, [128, T], BF16)
        wt = P.sbpool(es, "mw", 2, [128, 3], F32)
        bt = P.sbpool(es, "mb", 2, [128, 1], F32)
        pst = [P.ps(es, f"mpt{i}", [128, 4, 128], BF16) for i in range(2)]
        kt = P.sbpool(es, "mkt", 3, [128, 4, 128], BF16)
        npt = [0]

        def load(c):
            P.dma("sp", zt[c % 2][:, :], k.mlqkT[c * 128:(c + 1) * 128, :])
            P.dma("sp", wt[c % 2][:, :], V(k.ml_conv_w.h[l, :, c * 128:(c + 1) * 128].rearrange("w c -> c w"), None),
                  allow_slow_non_contiguous=True)
            P.dma("sp", bt[c % 2][:, :], V(k.ml_conv_b.h[l:l + 1, c * 128:(c + 1) * 128].rearrange("o c -> c o"), None),
                  allow_slow_non_contiguous=True)

        def body(c):
            z, u, a, w, b = zt[c % 2], ut[c % 2], ab[c % 2], wt[c % 2], bt[c % 2]
            for _ in conv3_fm_gen(P, u, z, w, b, SEGS):
                yield
            if c < 4:
                P.act(a[:, :], u[:, :], AF.Silu)
            else:
                P.act(u[:, :], u[:, :], AF.Silu)
                yield
                P.act(a[:, :], u[:, :], AF.Copy, scale=128 ** -0.5)
            P.dma("sp", k.mlQKT[c * 128:(c + 1) * 128, :], a[:, :])
            yield
            if c >= 4:
                hk = c - 4
                for n0 in range(0, NT, 4):
                    nn = min(4, NT - n0)
                    ps = pst[npt[0] % 2]
                    for j in range(nn):
                        P.transpose(ps[:, j, :], a[:, (n0 + j) * 128:(n0 + j + 1) * 128], idb[:, :])
                    o = kt[npt[0] % 3]
                    P.copy(o[:, 0:nn, :], ps[:, 0:nn, :])
                    P.dma("sp", V(k.mlKtm.h[n0 * 128:(n0 + nn) * 128, hk * 128:(hk + 1) * 128].rearrange("(n p) d -> p n d", p=128), None),
                          o[:, 0:nn, :])
                    npt[0] += 1
                    if n0 % 16 == 12:
                        yield

        prefetch_pipelined(8, load, body, 1, 1)
        vf = P.sbpool(es, "mvf", 3, [128, 512], F32)
        va = P.sbpool(es, "mva", 3, [128, 4, 129], BF16)
        for b_ in va:
            P.memset(b_[:, :, 128:129], 1.0)

        def vload(ti):
            P.dma("sp", vf[ti % 3][:, :], k.mlvo[ti * 128:(ti + 1) * 128, 0:512])

        def vbody(ti):
            P.act(va[ti % 3][:, :, 0:128], V(vf[ti % 3].h[:, :].rearrange("p (h c) -> p h c", h=4), vf[ti % 3].key), AF.Copy)
            P.dma("sp", k.mlVa[ti * 128:(ti + 1) * 128, :, :], va[ti % 3][:, :, :])

        prefetch_loop(NT, vload, vbody, 2)
        P.flush()


def phase_ml_scan(P, k, l):
    with ExitStack() as es:
        G = P.sb(es, "sG", [128, NT, 16], F32)
        gb = P.sb(es, "sgb", [128, 16], F32)
        P.dma("sp", G[:, :, :], V(k.mlvo.h[:, 1024:1040].rearrange("(n p) c -> p n c", p=128), None))
        P.dma("sp", gb[:, :], V(k.ml_gate_b.h[l:l + 1, :].to_broadcast([128, 16]), None))
        P.tt(G[:, :, :], G[:, :, :], bcast(gb[:, :], [128, NT, 16], 1), ALU.add)
        Gv = lambda d, kind: V(G.h[:, :, :].rearrange("p n (d k h) -> p d k n h", d=2, k=2)[:, d, kind, :, :], G.key)
        I_ = P.sb(es, "sI", [128, 2, NT, 4], F32)
        LF = P.sb(es, "sLF", [128, 2, NT, 4], F32)
        for d in range(2):
            P.copy(I_[:, d, :, :], Gv(d, 0))
            P.act(LF[:, d, :, :], Gv(d, 1), AF.Exp, scale=-1.0)
        P.ts(LF[:, :, :, :], LF[:, :, :, :], 1.0, None, ALU.add)
        P.act(LF[:, :, :, :], LF[:, :, :, :], AF.Ln)
        P.ts(LF[:, :, :, :], LF[:, :, :, :], -1.0, None, ALU.mult)
        tri = P.sb(es, "stri", [128, 3, 128], F32)
        P.dma("sp", tri[:, 0, :], k.mhi[:, :])
        P.dma("sp", tri[:, 1, :], k.mlo[:, :])
        P.memset(tri[:, 2, :], 1.0)
        trib = P.sb(es, "strib", [128, 2, 128], BF16)
        P.copy(trib[:, :, :], tri[:, 0:2, :])
        NC4 = NT * 4
        Bc = P.sb(es, "sB", [128, 2, NT, 4], F32)
        Be = P.sb(es, "sBe", [128, 2, NT, 4], F32)
        es_g = ExitStack()
        psg = [P.ps(es_g, f"spg{i}") for i in range(2)]
        for d in range(2):
            lfv = V(LF.h[:, d, :, :].rearrange("p n h -> p (n h)"), LF.key)
            P.mm(psg[0][:, 0:NC4], tri[:, d, :], lfv)
            P.copy(V(Bc.h[:, d, :, :].rearrange("p n h -> p (n h)"), Bc.key), psg[0][:, 0:NC4])
            P.mm(psg[1][:, 0:NC4], tri[:, 2, :], lfv)
            P.copy(V(Be.h[:, d, :, :].rearrange("p n h -> p (n h)"), Be.key), psg[1][:, 0:NC4])
        U_ = P.sb(es, "sU", [128, 2, NT, 4], F32)
        EB = P.sb(es, "sEB", [128, 2, NT, 4], F32)
        W_ = P.sb(es, "sW", [128, 2, NT, 4], F32)
        EE = P.sb(es, "sEE", [128, 2, NT, 4], F32)
        P.tt(U_[:, :, :, :], I_[:, :, :, :], Bc[:, :, :, :], ALU.subtract)
        P.tt(W_[:, :, :, :], U_[:, :, :, :], Be[:, :, :, :], ALU.add)
        P.act(U_[:, :, :, :], U_[:, :, :, :], AF.Exp)
        P.act(W_[:, :, :, :], W_[:, :, :, :], AF.Exp)
        P.act(EB[:, :, :, :], Bc[:, :, :, :], AF.Exp)
        P.act(EE[:, :, :, :], Be[:, :, :, :], AF.Exp)
        P.flush("phase_ml_gates")
        es_g.close()
        NB = 3
        pool2 = lambda nm, shp, dt: [[P.sb(es, f"{nm}{c}_{i}", shp, dt) for i in range(NB)] for c in range(2)]
        QKc = pool2("sQK", [128, 2, 4, 128], BF16)
        Kmc = pool2("sKm", [128, 512], BF16)
        Vac = pool2("sVa", [128, 4, 129], BF16)
        C32 = [P.sb(es, f"sC{i}", [128, 129], F32) for i in range(8)]
        Cbf = [P.sb(es, f"sCb{i}", [128, 129], BF16) for i in range(8)]
        Sm = P.sbpool(es, "sSm", 8, [128, 128], BF16)
        Vw = P.sbpool(es, "sVw", 8, [128, 129], BF16)
        dn = P.sbpool(es, "sdn", 8, [128, 1], F32)
        ho = P.sbpool(es, "sho", 4, [128, 4, 128], F32)
        bS = [P.ps(es, f"spS{i}", [128, 4, 128], F32) for i in range(2)]
        bO = [P.ps(es, f"spO{i}", [128, 3, 129], F32) for i in range(3)]
        bU = [P.ps(es, f"spU{i}", [128, 3, 129], F32) for i in range(3)]
        pSv = lambda ci, par: bS[ci // 4][:, ci % 4, :]
        pOv = lambda ci, a=0, b=129: bO[ci // 3][:, ci % 3, a:b]
        pUv = lambda ci: bU[ci // 3][:, ci % 3, :]

        def excl(op, v):
            op.writes.append(v.key)
        order = [[NTL, NTL + 1] + list(range(NTL)), [NTL + 1, NTL] + list(range(NTL - 1, -1, -1))]
        onec = P.sb(es, "sonec", [128, 1], F32)
        P.memset(onec[:, :], 1.0)

        def loads(j):
            for d in range(2):
                n = order[d][j]
                tsl = slice(n * 128, (n + 1) * 128)
                qkv = k.mlQKT.h[:, tsl].rearrange("(w hh d) t -> d w hh t", w=2, hh=4)
                for w in range(2):
                    P.dma("sp", QKc[d][j % NB][:, w, :, :], V(qkv[:, w, :, :], None))
                P.dma("sp", Kmc[d][j % NB][:, :], k.mlKtm[tsl, :])
                P.dma("sp", Vac[d][j % NB][:, :, :], k.mlVa[tsl, :, :])

        def round_(j):
            ch = [(h, d, h * 2 + d, order[d][j]) for h in range(4) for d in range(2)]
            par = j % 2
            for (h, d, ci, n) in ch:
                qk_ = QKc[d][j % NB]
                P.mm(pSv(ci, par), qk_[:, 1, h, :], qk_[:, 0, h, :])
            for (h, d, ci, n) in ch:
                excl(P.stt(Sm[ci][:, :], pSv(ci, par), U_[:, d, n, h:h + 1], trib[:, d, :], ALU.mult, ALU.mult), pSv(ci, par))
                P.act(Vw[ci][:, :], Vac[d][j % NB][:, h, :], AF.Identity, scale=W_[:, d, n, h:h + 1])
            for (h, d, ci, n) in ch:
                qk_, km_, vv_ = QKc[d][j % NB], Kmc[d][j % NB], Vac[d][j % NB]
                P.mm(pOv(ci), Sm[ci][:, :], vv_[:, h, :], start=True, stop=(j == 0))
                if j > 0:
                    P.mm(pOv(ci), qk_[:, 0, h, :], Cbf[ci][:, :], start=False, stop=True)
                P.mm(pUv(ci), km_[:, h * 128:(h + 1) * 128], Vw[ci][:, :])
            for (h, d, ci, n) in ch:
                excl(P.tt(dn[ci][:, :], pOv(ci, 128, 129), EB[:, d, n, h:h + 1], ALU.mult), pOv(ci))
            for (h, d, ci, n) in ch:
                P.act(dn[ci][:, :], dn[ci][:, :], AF.Abs)
            for (h, d, ci, n) in ch:
                P.ts(dn[ci][:, :], dn[ci][:, :], 1.0, None, ALU.max)
            for (h, d, ci, n) in ch:
                P.recip(dn[ci][:, :], dn[ci][:, :])
            for (h, d, ci, n) in ch:
                P.tt(dn[ci][:, :], dn[ci][:, :], EB[:, d, n, h:h + 1], ALU.mult)
            for (h, d, ci, n) in ch:
                ho_ = ho[(2 * j + d) % 4]
                excl(P.act(ho_[:, h, :], pOv(ci, 0, 128), AF.Identity, scale=dn[ci][:, :]), pOv(ci))
            for d in range(2):
                n = order[d][j]
                ho_ = ho[(2 * j + d) % 4]
                P.dma("act", k.mlH[d, n * 128:(n + 1) * 128, :], V(ho_.h[:, :, :].rearrange("p h e -> p (h e)"), ho_.key))
            for (h, d, ci, n) in ch:
                if j == 0:
                    excl(P.copy(C32[ci][:, :], pUv(ci)), pUv(ci))
                else:
                    excl(P.stt(C32[ci][:, :], C32[ci][:, :], EE[:, d, n, h:h + 1], pUv(ci), ALU.mult, ALU.add), pUv(ci))
            for (h, d, ci, n) in ch:
                P.act(Cbf[ci][:, :], C32[ci][:, :], AF.Copy)

        prefetch_loop(NT, loads, round_, 1)
        P.flush()


def phase_ml_finish(P, k, l, with_ctx_out):
    with ExitStack() as es:
        idf, idb = load_ident(P, es, k)
        g = P.sb(es, "fg", [128, 4, 128], F32)
        P.dma("sp", V(g.h[:, :, :].rearrange("p h d -> p (h d)"), g.key), V(k.ml_norm_g.h[l:l + 1, :].to_broadcast([128, 512]), None))
        hf = P.sbpool(es, "fhf", 4, [128, 4, 128], F32)
        hb = P.sbpool(es, "fhb", 4, [128, 4, 128], F32)
        og = P.sbpool(es, "fog", 4, [128, 4, 128], F32)
        sq2 = P.sbpool(es, "fsq", 2, [128, 4, 128], F32)
        ss = P.sbpool(es, "fss", 2, [128, 4], F32)
        yb = P.sbpool(es, "fyb", 2, [128, 4, 128], BF16)
        psT = [P.ps(es, f"fpT{i}", [128, 4, 128], BF16) for i in range(2)]
        oT = P.sbpool(es, "foT", 2, [128, 4, 128], BF16)
        flat = lambda b: V(b.h[:, :, :].rearrange("p h d -> p (h d)"), b.key)
        tiles = list(range(NTL)) + (list(range(NTL, NT)) if with_ctx_out else [])

        def load(i):
            t0 = tiles[i] * 128
            P.dma("sp", flat(hf[i % 4]), k.mlH[0, t0:t0 + 128, :])
            P.dma("sp", flat(hb[i % 4]), k.mlH[1, t0:t0 + 128, :])
            P.dma("sp", flat(og[i % 4]), k.mlvo[t0:t0 + 128, 512:1024])

        def body(i):
            t0 = tiles[i] * 128
            a, b, o = hf[i % 4], hb[i % 4], og[i % 4]
            P.tt(a[:, :, :], a[:, :, :], b[:, :, :], ALU.add)
            P.act(o[:, :, :], o[:, :, :], AF.Sigmoid)
            yield
            sq_ = sq2[i % 2]
            P.tt(sq_[:, :, :], a[:, :, :], a[:, :, :], ALU.mult)
            yield
            s_ = ss[i % 2]
            P.reduce(s_[:, :], sq_[:, :, :])
            yield
            P.act(s_[:, :], s_[:, :], AF.Sqrt, bias=EPSB[0][:, :], scale=1.0 / 128)
            yield
            P.recip(s_[:, :], s_[:, :])
            yield
            P.tt(a[:, :, :], a[:, :, :], bcast(s_[:, :], [128, 4, 128], 2), ALU.mult)
            yield
            P.tt(a[:, :, :], a[:, :, :], g[:, :, :], ALU.mult)
            yield
            y_ = yb[i % 2]
            P.tt(y_[:, :, :], a[:, :, :], o[:, :, :], ALU.mult)
            yield
            ps = psT[i % 2]
            for c in range(4):
                P.transpose(ps[:, c, :], y_[:, c, :], idb[:, :])
            yield
            ot = oT[i % 2]
            P.act(ot[:, :, :], ps[:, :, :], AF.Copy)
            P.dma("sp", V(k.yT.h[2, :, t0:t0 + 128].rearrange("(c p) t -> p c t", p=128), None), ot[:, :, :])

        prefetch_pipelined(len(tiles), load, body, 2, 2)
        P.flush()

NFFT = 2 * L
TWO_PI = 2.0 * np.pi


def _hy_consts():
    c = {}
    for nm, Lf in (("lat", L), ("ctx", LC)):
        t = np.arange(Lf, dtype=np.float64)
        tn = t / (Lf - 1)
        w = TWO_PI * t / Lf
        bands = np.linspace(1e-4, 15.0, 16)
        ang = w[:, None] * bands[None, :]
        z = np.concatenate([tn[:, None], np.cos(ang), -np.sin(ang)], axis=-1)
        c["hz_" + nm] = np.ascontiguousarray(z.T).astype(np.float32)
        c["htn_" + nm] = tn[None, :].astype(np.float32)
    c["hntn_ctx"] = (-(np.arange(LC, dtype=np.float64) / (LC - 1)))[:, None].astype(np.float32)
    a = np.arange(128, dtype=np.float64)
    ang = TWO_PI * np.outer(a, a) / 128.0
    C2, S2 = np.cos(ang), np.sin(ang)
    c["hFA"] = np.concatenate([C2[:64], -S2[:64]], axis=1).astype(np.float32)
    c["hC2"] = C2.astype(np.float32)
    c["hS2"] = S2.astype(np.float32)
    c["hIAre"] = np.concatenate([C2, S2], axis=1).astype(np.float32)
    c["hIAim"] = np.concatenate([-S2, C2], axis=1).astype(np.float32)
    angt = TWO_PI * np.outer(a, a) / NFFT
    c["hTc"] = np.cos(angt).astype(np.float32)
    c["hTs"] = np.sin(angt).astype(np.float32)
    r = np.arange(512, dtype=np.float64)
    a5 = TWO_PI * np.outer(r, r) / 512.0
    c["hW5c"] = np.cos(a5).astype(np.float32)
    c["hW5s"] = np.sin(a5).astype(np.float32)
    return c


CONSTS.update(_hy_consts())
HY_IN = ["hy_conv_w", "hy_conv_b", "hy_pe_w1", "hy_pe_b1", "hy_freq", "hy_pe_w2", "hy_pe_b2", "hy_pe_w3", "hy_decay", "hy_skip"]


def declare_hy(nc, k):
    inp = lambda n, s: dram(nc, n, s, F32, "ExternalInput")
    k.hy_conv_w = inp("hy_conv_w", [DEPTH, 3, 1536])
    k.hy_conv_b = inp("hy_conv_b", [DEPTH, 1536])
    k.hy_pe_w1 = inp("hy_pe_w1", [DEPTH, 33, 64])
    k.hy_pe_b1 = inp("hy_pe_b1", [DEPTH, 64])
    k.hy_freq = inp("hy_freq", [DEPTH, 2, 64])
    k.hy_pe_w2 = inp("hy_pe_w2", [DEPTH, 64, 64])
    k.hy_pe_b2 = inp("hy_pe_b2", [DEPTH, 64])
    k.hy_pe_w3 = inp("hy_pe_w3", [DEPTH, 64, 2048])
    k.hy_decay = inp("hy_decay", [DEPTH, 2048])
    k.hy_skip = inp("hy_skip", [DEPTH, 2, 512])
    for n, v in _hy_consts().items():
        setattr(k, n, inp(n, list(v.shape)))
    k.hyU = dram(nc, "hyU", [1536, T])
    k.hyVb = dram(nc, "hyVb", [512, L], BF16)
    k.hyK = dram(nc, "hyK", [2048, L], BF16)
    k.hyNrm = dram(nc, "hyNrm", [2, 512])
    k.hySpec = dram(nc, "hySpec", [2, 2, 128, 512, 128], BF16)


def phase_hy_conv(P, k, l):
    with ExitStack() as es:
        zt = P.sbpool(es, "hz", 2, [128, T], F32)
        ut = P.sbpool(es, "hu", 2, [128, T], F32)
        vb = P.sbpool(es, "hvb", 2, [128, L], BF16)
        wt = P.sbpool(es, "hw", 2, [128, 3], F32)
        bt = P.sbpool(es, "hb", 2, [128, 1], F32)
        def load(c):
            P.dma("sp", zt[c % 2][:, :], k.hyT[c * 128:(c + 1) * 128, :])
            P.dma("sp", wt[c % 2][:, :], V(k.hy_conv_w.h[l, :, c * 128:(c + 1) * 128].rearrange("w c -> c w"), None),
                  allow_slow_non_contiguous=True)
            P.dma("sp", bt[c % 2][:, :], V(k.hy_conv_b.h[l:l + 1, c * 128:(c + 1) * 128].rearrange("o c -> c o"), None),
                  allow_slow_non_contiguous=True)

        def body(c):
            z, u, w, b = zt[c % 2], ut[c % 2], wt[c % 2], bt[c % 2]
            for _ in conv3_fm_gen(P, u, z, w, b, SEGS):
                yield
            P.dma("sp", k.hyU[c * 128:(c + 1) * 128, :], u[:, :])
            if c < 4:
                v_ = vb[c % 2]
                P.act(v_[:, :], u[:, 0:L], AF.Copy)
                P.dma("sp", k.hyVb[c * 128:(c + 1) * 128, :], v_[:, :])
            yield

        prefetch_pipelined(12, load, body, 1, 1)
        P.flush()


def _sin_layer(P, dst, ps, scl, bia, tmp, m, n):
    a = tmp[0]
    P.act(a[0:64, 0:n], ps[0:64, 0:n], AF.Identity, bias=bia, scale=scl)
    for _ in range(2):
        P.ts(m[0:64, 0:n], a[0:64, 0:n], float(np.pi), -TWO_PI, ALU.is_gt, ALU.mult)
        P.tt(a[0:64, 0:n], a[0:64, 0:n], m[0:64, 0:n], ALU.add)
        P.ts(m[0:64, 0:n], a[0:64, 0:n], -float(np.pi), TWO_PI, ALU.is_lt, ALU.mult)
        P.tt(a[0:64, 0:n], a[0:64, 0:n], m[0:64, 0:n], ALU.add)
    P.act(dst, a[0:64, 0:n], AF.Sin)


def _filter_mlp(P, es, k, l, zsrc, Lf, h2T):
    w1 = P.sb(es, "hw1", [33, 64], F32)
    w2 = P.sb(es, "hw2", [64, 64], F32)
    col = P.sb(es, "hcol", [64, 8], F32)
    P.dma("sp", w1[:, :], k.hy_pe_w1[l, :, :])
    P.dma("sp", w2[:, :], k.hy_pe_w2[l, :, :])
    cl = lambda src: V(src, None)
    P.dma("sp", col[:, 0:1], cl(k.hy_pe_b1.h[l:l + 1, :].rearrange("o c -> c o")), allow_slow_non_contiguous=True)
    P.dma("sp", col[:, 1:2], cl(k.hy_freq.h[l, 0:1, :].rearrange("o c -> c o")), allow_slow_non_contiguous=True)
    P.dma("sp", col[:, 2:3], cl(k.hy_pe_b2.h[l:l + 1, :].rearrange("o c -> c o")), allow_slow_non_contiguous=True)
    P.dma("sp", col[:, 3:4], cl(k.hy_freq.h[l, 1:2, :].rearrange("o c -> c o")), allow_slow_non_contiguous=True)
    P.tt(col[:, 4:5], col[:, 0:1], col[:, 1:2], ALU.mult)
    P.tt(col[:, 5:6], col[:, 2:3], col[:, 3:4], ALU.mult)
    zT = P.sbpool(es, "hzT", 2, [33, 512], F32)
    h1 = P.sbpool(es, "hh1", 2, [64, 512], F32)
    tmp = P.sbpool(es, "hta", 2, [64, 512], F32)
    m = P.sb(es, "htm", [64, 512], F32)
    ps = [P.ps(es, f"hpm{i}") for i in range(2)]
    nb = 0
    for t0 in range(0, Lf, 512):
        n = min(512, Lf - t0)
        z_ = zT[nb % 2]
        P.dma("sp", z_[:, 0:n], zsrc[:, t0:t0 + n])
        P.mm(ps[0][0:64, 0:n], w1[:, :], z_[:, 0:n])
        h1_ = h1[nb % 2]
        _sin_layer(P, h1_[:, 0:n], ps[0], col[:, 1:2], col[:, 4:5], [tmp[0]], m, n)
        P.mm(ps[1][0:64, 0:n], w2[:, :], h1_[:, 0:n])
        _sin_layer(P, h2T[0:64, t0:t0 + n], ps[1], col[:, 3:4], col[:, 5:6], [tmp[1]], m, n)
        nb += 1


def phase_hy_filt(P, k, l):
    with ExitStack() as es:
        h2T = P.sb(es, "hh2T", [64, L], F32)
        _filter_mlp(P, es, k, l, k.hz_lat, L, h2T)
        w3 = P.sb(es, "hw3", [64, 2048], F32)
        P.dma("sp", w3[:, :], k.hy_pe_w3[l, :, :])
        tnb = P.sb(es, "htnb", [128, L], F32)
        P.dma("sp", tnb[:, :], V(k.htn_lat.h[0:1, :].to_broadcast([128, L]), None))
        nad = P.sb(es, "hnad", [128, 16], F32)
        P.dma("sp", nad[:, :], V(k.hy_decay.h[l:l + 1, :].rearrange("o (g p) -> p (o g)", p=128), None),
              allow_slow_non_contiguous=True)
        P.act(nad[:, :], nad[:, :], AF.Abs)
        P.ts(nad[:, :], nad[:, :], -1.0, None, ALU.mult)
        ssq = P.sb(es, "hssq", [128, 16, 16], F32)
        sst = P.sb(es, "hsst", [128, 16], F32)
        E = P.sbpool(es, "hE", 4, [128, 512], F32)
        kf = P.sbpool(es, "hkf", 2, [128, 512], F32)
        kb = P.sbpool(es, "hkb", 2, [128, L], BF16)
        junk = P.sb(es, "hjk", [128, 512], F32)
        ps = [P.ps(es, f"hpf{i}") for i in range(3)]
        nb = 0
        w3b = P.sb(es, "hw3b", [64, 2048], BF16)
        h2b = P.sb(es, "hh2b", [64, L], BF16)
        P.copy(w3b[:, :], w3[:, :])
        for q in range(4):
            P.act(h2b[:, q * 2048:(q + 1) * 2048], h2T[0:64, q * 2048:(q + 1) * 2048], AF.Copy)
        units = []
        for g in range(16):
            is_bwd = (g // 4) % 2 == 1
            kb_ = kb[g % 2]
            for bi in range(L // 512):
                t0 = bi * 512
                p_ = ps[nb % 3]
                e_ = E[nb % 4]
                nb += 1

                def front(g=g, t0=t0, p_=p_, e_=e_):
                    P.mm(p_[:, :], w3b[:, g * 128:(g + 1) * 128], h2b[0:64, t0:t0 + 512])
                    P.act(e_[:, :], tnb[:, t0:t0 + 512], AF.Exp, scale=nad[:, g:g + 1])

                def back(g=g, bi=bi, t0=t0, p_=p_, e_=e_, kb_=kb_, is_bwd=is_bwd):
                    P.tt(kb_[:, t0:t0 + 512], p_[:, :], e_[:, :], ALU.mult)
                    if is_bwd and bi == 0:
                        P.memset(kb_[:, 0:1], 0.0)
                    P.act(junk[:, :], kb_[:, t0:t0 + 512], AF.Square, accum_out=ssq[:, g, bi:bi + 1])
                    if bi == L // 512 - 1:
                        P.dma("sp", k.hyK[g * 128:(g + 1) * 128, :], kb_[:, :])

                units.append((front, back))
        emit_skewed(units, 2)
        P.reduce(sst[:, :], ssq[:, :, :])
        rn = P.sb(es, "hrn", [128, 8], F32)
        for n in range(2):
            P.tt(rn[:, n * 4:(n + 1) * 4], sst[:, n * 8:n * 8 + 4], sst[:, n * 8 + 4:n * 8 + 8], ALU.add)
        P.act(rn[:, :], rn[:, :], AF.Sqrt, bias=EPSB[0][:, :], scale=1.0)
        P.recip(rn[:, :], rn[:, :])
        P.ts(rn[:, :], rn[:, :], 1.0 / NFFT, None, ALU.mult)
        P.dma("sp", V(k.hyNrm.h[:, :].rearrange("n (cb p) -> p n cb", p=128), None),
              V(rn.h[:, :].rearrange("p (n cb) -> p n cb", n=2), rn.key), allow_slow_non_contiguous=True)
        P.flush()


SBQ = 4
GQ = 8
NH = GQ // SBQ


class HyTab:
    pass


def hy_tables(P, es, k):
    tb = HyTab()
    st = P.sb(es, "htst", [128, 256], F32)

    def ld(name, src, rows, cols):
        b = P.sb(es, name, [128, cols], BF16)
        P.dma("sp", st[0:rows, 0:cols], src[:, :])
        P.copy(b[0:rows, :], st[0:rows, 0:cols])
        return b
    tb.FA = ld("hFAb", k.hFA, 64, 256)
    tb.C2 = ld("hC2b", k.hC2, 128, 128)
    tb.S2 = ld("hS2b", k.hS2, 128, 128)
    tb.IAre = ld("hIAreb", k.hIAre, 128, 256)
    tb.IAim = ld("hIAimb", k.hIAim, 128, 256)
    tb.nS2 = P.sb(es, "hnS2b", [128, 128], BF16)
    tb.nC2 = P.sb(es, "hnC2b", [128, 128], BF16)
    P.ts(tb.nS2[:, :], tb.S2[:, :], -1.0, None, ALU.mult)
    P.ts(tb.nC2[:, :], tb.C2[:, :], -1.0, None, ALU.mult)
    tb.Tc = P.sb(es, "hTcf", [128, 128], F32)
    tb.Ts = P.sb(es, "hTsf", [128, 128], F32)
    P.dma("sp", tb.Tc[:, :], k.hTc[:, :])
    P.dma("sp", tb.Ts[:, :], k.hTs[:, :])
    tb.TcB = P.sb(es, "hTcB", [128, GQ, 128], BF16)
    tb.TsB = P.sb(es, "hTsB", [128, GQ, 128], BF16)
    P.copy(tb.TcB[:, :, :], bcast(tb.Tc[:, :], [128, GQ, 128], 1))
    P.copy(tb.TsB[:, :, :], bcast(tb.Ts[:, :], [128, GQ, 128], 1))
    return tb


def hy_evacA(P, pA, Ab, s0):
    P.act(Ab[:, 0, s0:s0 + SBQ, :], V(pA.h[:, :, 0:128], pA.key), AF.Copy)
    P.act(Ab[:, 1, s0:s0 + SBQ, :], V(pA.h[:, :, 128:256], pA.key), AF.Copy)


def hy_twiddle(P, tb, Bt, tmp, inverse, Ab):
    Are, Aim = Ab[:, 0, :, :], Ab[:, 1, :, :]
    Tc, Ts = tb.TcB[:, :, :], tb.TsB[:, :, :]
    t1, t2, t3, t4 = [t[:, :, :] for t in tmp]
    P.tt(t1, Are, Tc, ALU.mult)
    P.tt(t2, Aim, Ts, ALU.mult)
    P.tt(t3, Aim, Tc, ALU.mult)
    P.tt(t4, Are, Ts, ALU.mult)
    if not inverse:
        P.tt(Bt[:, 0, :, :], t1, t2, ALU.add)
        P.tt(Bt[:, 1, :, :], t3, t4, ALU.subtract)
    else:
        P.tt(Bt[:, 0, :, :], t1, t2, ALU.subtract)
        P.tt(Bt[:, 1, :, :], t3, t4, ALU.add)


def flat2(b, i, s0):
    return V(b.h[:, i, s0:s0 + SBQ, :].rearrange("p s l -> p (s l)"), b.key)


def hy_stageA_fwd(P, tb, X, pA, s0):
    for s in range(SBQ):
        P.mm(pA[:, s, :], X[0:64, s0 + s, :], tb.FA[0:64, :])


def hy_stageB_fwd(P, tb, Bts, pB, s0):
    n = len(Bts)
    for i, (Bt, conj) in enumerate(Bts):
        P.mm(pB[:, 0, :], tb.C2[:, :], flat2(Bt, 0, s0), start=(i == 0), stop=False)
        P.mm(pB[:, 0, :], tb.S2[:, :], flat2(Bt, 1, s0), start=False, stop=(i == n - 1))
    for i, (Bt, conj) in enumerate(Bts):
        if not conj:
            P.mm(pB[:, 1, :], tb.C2[:, :], flat2(Bt, 1, s0), start=(i == 0), stop=False)
            P.mm(pB[:, 1, :], tb.nS2[:, :], flat2(Bt, 0, s0), start=False, stop=(i == n - 1))
        else:
            P.mm(pB[:, 1, :], tb.nC2[:, :], flat2(Bt, 1, s0), start=(i == 0), stop=False)
            P.mm(pB[:, 1, :], tb.S2[:, :], flat2(Bt, 0, s0), start=False, stop=(i == n - 1))


def seqview(dr, row0, nrows):
    return V(dr.h[row0:row0 + nrows, 0:L].rearrange("c (h l) -> h c l", l=128), None)


def run_pipelined(gens_fn, items, npipe):
    for g0 in range(0, len(items), npipe):
        alive = [gens_fn(i, items[g0 + i]) for i in range(npipe) if g0 + i < len(items)]
        while alive:
            nxt = []
            for g in alive:
                try:
                    next(g)
                    nxt.append(g)
                except StopIteration:
                    pass
            alive = nxt


class PsPool:
    def __init__(self, tiles):
        self.t = tiles
        self.c = 0

    def next(self):
        self.c += 1
        return self.t[self.c % len(self.t)]


def phase_hy_spec(P, k, l):
    with ExitStack() as es:
        tb = hy_tables(P, es, k)
        NPIPE = 3
        mk = lambda nm, shp, dt: [P.sb(es, f"{nm}{i}", shp, dt) for i in range(NPIPE)]
        Xf = mk("hXf", [64, GQ, 128], BF16)
        Xb = mk("hXb", [64, GQ, 128], BF16)
        Bf = mk("hBf", [128, 2, GQ, 128], BF16)
        Bb = mk("hBb", [128, 2, GQ, 128], BF16)
        tmp = [mk(f"htw{i}", [128, GQ, 128], BF16) for i in range(4)]
        tmp2 = [mk(f"htx{i}", [128, GQ, 128], BF16) for i in range(4)]
        Ab1 = mk("hAb1", [128, 2, GQ, 128], BF16)
        Ab2 = mk("hAb2", [128, 2, GQ, 128], BF16)
        rnb = P.sb(es, "hrnb", [128, 2, 512], F32)
        P.dma("sp", V(rnb.h[:, :, :].rearrange("p n c -> p (n c)"), rnb.key),
              V(k.hyNrm.h[:, :].rearrange("n c -> (n c)").unsqueeze(0).to_broadcast([128, 1024]), None))
        Sp = mk("hSp", [128, 2, GQ, 128], BF16)
        Ys = mk("hYs", [128, 2, GQ, 128], F32)
        pA = PsPool([P.ps(es, f"hpA{i}", [128, SBQ, 256], F32) for i in range(2)])
        pB = PsPool([P.ps(es, f"hpB{i}", [128, 2, SBQ * 128], F32) for i in range(2)])

        def steps(i, item):
            n, c0 = item
            P.dma("sp", Xf[i][:, :, :], seqview(k.hyK, n * 1024 + c0, GQ))
            P.dma("sp", Xb[i][:, :, :], seqview(k.hyK, n * 1024 + 512 + c0, GQ))
            for hf in range(NH):
                pa = pA.next()
                hy_stageA_fwd(P, tb, Xf[i], pa, hf * SBQ)
                hy_evacA(P, pa, Ab1[i], hf * SBQ)
            yield
            hy_twiddle(P, tb, Bf[i], [t[i] for t in tmp], False, Ab1[i])
            for hf in range(NH):
                pa = pA.next()
                hy_stageA_fwd(P, tb, Xb[i], pa, hf * SBQ)
                hy_evacA(P, pa, Ab2[i], hf * SBQ)
            yield
            hy_twiddle(P, tb, Bb[i], [t[i] for t in tmp2], False, Ab2[i])
            yield
            for hf in range(NH):
                pb = pB.next()
                hy_stageB_fwd(P, tb, [(Bf[i], False), (Bb[i], True)], pb, hf * SBQ)
                P.act(Ys[i][:, :, hf * SBQ:(hf + 1) * SBQ, :],
                      V(pb.h[:, :, :].rearrange("p r (s l) -> p r s l", l=128), pb.key), AF.Copy)
            yield
            rv = V(rnb.h[:, n, c0:c0 + GQ].unsqueeze(1).unsqueeze(3).to_broadcast([128, 2, GQ, 128]), rnb.key)
            P.tt(Sp[i][:, :, :, :], Ys[i][:, :, :, :], rv, ALU.mult)
            for ri in range(2):
                P.dma("act", V(k.hySpec.h[n, ri, :, c0:c0 + GQ, :], None), Sp[i][:, ri, :, :])
            yield

        items = [(n, c0) for n in range(2) for c0 in range(0, 512, GQ)]
        run_pipelined(steps, items, NPIPE)
        P.flush()


def phase_hy_data(P, k, l):
    with ExitStack() as es:
        tb = hy_tables(P, es, k)
        skb = P.sb(es, "hskb", [64, 2, 512], F32)
        P.dma("sp", V(skb.h[:, :, :].rearrange("p n c -> p (n c)"), skb.key),
              V(k.hy_skip.h[l, :, :].rearrange("n c -> (n c)").unsqueeze(0).to_broadcast([64, 1024]), None))
        NPIPE = 3
        mk = lambda nm, shp, dt: [P.sb(es, f"{nm}{i}", shp, dt) for i in range(NPIPE)]
        X = mk("dX", [64, GQ, 128], BF16)
        yp = mk("dyp", [64, GQ, 128], F32)
        gt = [mk("dg0", [64, GQ, 128], F32), mk("dg1", [64, GQ, 128], F32)]
        Bt = mk("dB", [128, 2, GQ, 128], BF16)
        Zt = mk("dZ", [128, 2, GQ, 128], BF16)
        Sp = mk("dSp", [128, 2, GQ, 128], BF16)
        tmp = [mk(f"dtw{i}", [128, GQ, 128], BF16) for i in range(4)]
        Ab = mk("dAb", [128, 2, GQ, 128], BF16)
        Yb = mk("dYb", [128, 2, GQ, 128], BF16)
        cmb = mk("dcmb", [64, GQ, 128], F32)
        yfs = mk("dyfs", [64, GQ, 128], F32)
        yb = mk("dyb", [64, GQ, 128], BF16)
        pA = PsPool([P.ps(es, f"dpA{i}", [128, SBQ, 256], F32) for i in range(2)])
        pB = PsPool([P.ps(es, f"dpB{i}", [128, 2, SBQ * 128], F32) for i in range(2)])

        def steps(i, c0):
            P.dma("sp", X[i][:, :, :], seqview(k.hyVb, c0, GQ))
            P.dma("sp", yp[i][:, :, :], seqview(k.hyU, c0, GQ))
            P.dma("sp", gt[0][i][:, :, :], seqview(k.hyU, 512 + c0, GQ))
            P.dma("sp", gt[1][i][:, :, :], seqview(k.hyU, 1024 + c0, GQ))
            yield
            tw = [t[i] for t in tmp]
            for n in range(2):
                for ri in range(2):
                    P.dma("sp", Sp[i][:, ri, :, :], V(k.hySpec.h[n, ri, :, c0:c0 + GQ, :], None))
                for hf in range(NH):
                    pa = pA.next()
                    hy_stageA_fwd(P, tb, X[i], pa, hf * SBQ)
                    hy_evacA(P, pa, Ab[i], hf * SBQ)
                yield
                hy_twiddle(P, tb, Bt[i], tw, False, Ab[i])
                yield
                for hf in range(NH):
                    pb = pB.next()
                    hy_stageB_fwd(P, tb, [(Bt[i], False)], pb, hf * SBQ)
                    P.act(Yb[i][:, :, hf * SBQ:(hf + 1) * SBQ, :],
                          V(pb.h[:, :, :].rearrange("p r (s l) -> p r s l", l=128), pb.key), AF.Copy)
                yield
                t1, t2, t3, t4 = [t[:, :, :] for t in tw]
                Yre, Yim = Yb[i][:, 0, :, :], Yb[i][:, 1, :, :]
                P.tt(t1, Yre, Sp[i][:, 0, :, :], ALU.mult)
                P.tt(t2, Yim, Sp[i][:, 1, :, :], ALU.mult)
                P.tt(t3, Yre, Sp[i][:, 1, :, :], ALU.mult)
                P.tt(t4, Yim, Sp[i][:, 0, :, :], ALU.mult)
                P.tt(Zt[i][:, 0, :, :], t1, t2, ALU.subtract)
                P.tt(Zt[i][:, 1, :, :], t3, t4, ALU.add)
                yield
                for hf in range(NH):
                    pa = pA.next()
                    for s in range(SBQ):
                        P.mm(pa[:, s, :], Zt[i][:, 0, hf * SBQ + s, :], tb.IAre[:, :], start=True, stop=False)
                        P.mm(pa[:, s, :], Zt[i][:, 1, hf * SBQ + s, :], tb.IAim[:, :], start=False, stop=True)
                    hy_evacA(P, pa, Ab[i], hf * SBQ)
                yield
                hy_twiddle(P, tb, Bt[i], tw, True, Ab[i])
                yield
                for hf in range(NH):
                    pb = pB.next()
                    P.mm(pb[0:64, 0, :], tb.C2[:, 0:64], flat2(Bt[i], 0, hf * SBQ), start=True, stop=False)
                    P.mm(pb[0:64, 0, :], tb.nS2[:, 0:64], flat2(Bt[i], 1, hf * SBQ), start=False, stop=True)
                    P.act(yfs[i][:, hf * SBQ:(hf + 1) * SBQ, :],
                          V(pb.h[0:64, 0, :].rearrange("p (s l) -> p s l", l=128), pb.key), AF.Copy)
                yield
                a = cmb[i][:, :, :]
                sk = V(skb.h[:, n, c0:c0 + GQ].unsqueeze(2).to_broadcast([64, GQ, 128]), skb.key)
                P.tt(a, yp[i][:, :, :], sk, ALU.mult)
                P.tt(a, a, yfs[i][:, :, :], ALU.add)
                if n == 0:
                    P.tt(yp[i][:, :, :], a, gt[0][i][:, :, :], ALU.mult)
                    P.act(X[i][:, :, :], yp[i][:, :, :], AF.Copy)
                else:
                    P.tt(yb[i][:, :, :], a, gt[1][i][:, :, :], ALU.mult)
                    P.dma("act", V(k.yT.h[0, c0:c0 + GQ, 0:L].rearrange("c (h l) -> h c l", l=128), None), yb[i][:, :, :])
                yield

        run_pipelined(steps, list(range(0, 512, GQ)), NPIPE)
        P.flush()


def phase_hy_ctx(P, k, l):
    with ExitStack() as es:
        idf, idb = load_ident(P, es, k)
        h2T = P.sb(es, "ch2T", [64, LC], F32)
        _filter_mlp(P, es, k, l, k.hz_ctx, LC, h2T)
        w3 = P.sb(es, "cw3", [64, 2048], F32)
        P.dma("sp", w3[:, :], k.hy_pe_w3[l, :, :])
        adb = P.sb(es, "cadb", [128, 2048], F32)
        P.dma("sp", adb[:, :], V(k.hy_decay.h[l:l + 1, :].to_broadcast([128, 2048]), None))
        P.act(adb[:, :], adb[:, :], AF.Abs)
        ntn = P.sb(es, "cntn", [128, 2], F32)
        P.dma("sp", ntn[:, :], V(k.hntn_ctx.h[:, :].rearrange("(c p) o -> p (c o)", p=128), None), allow_slow_non_contiguous=True)
        kfc = P.sb(es, "ckfc", [128, 2, 2048], F32)
        E = P.sbpool(es, "cE", 2, [128, 512], F32)
        pf = [P.ps(es, f"cpf{i}") for i in range(2)]
        nb = 0
        for tc in range(2):
            for cb in range(4):
                cs = slice(cb * 512, (cb + 1) * 512)
                p_ = pf[nb % 2]
                P.mm(p_[:, :], h2T[0:64, tc * 128:(tc + 1) * 128], w3[:, cs])
                e_ = E[nb % 2]
                P.act(e_[:, :], adb[:, cs], AF.Exp, scale=ntn[:, tc:tc + 1])
                P.tt(kfc[:, tc, cs], p_[:, :], e_[:, :], ALU.mult)
                nb += 1
        for n in range(2):
            P.memset(kfc[0:1, 0, n * 1024 + 512:(n + 1) * 1024], 0.0)
        sq = P.sb(es, "csq", [128, 2, 2048], F32)
        P.tt(sq[:, :, :], kfc[:, :, :], kfc[:, :, :], ALU.mult)
        ones = P.sb(es, "cones", [128, 128], F32)
        P.memset(ones[:, :], 1.0)
        ssb = P.sb(es, "cssb", [128, 4, 512], F32)
        for cb in range(4):
            p_ = pf[cb % 2]
            for tc in range(2):
                P.mm(p_[:, :], ones[:, :], sq[:, tc, cb * 512:(cb + 1) * 512], start=(tc == 0), stop=(tc == 1))
            P.copy(ssb[:, cb, :], p_[:, :])
        rn = P.sb(es, "crn", [128, 2, 512], F32)
        for n in range(2):
            P.tt(rn[:, n, :], ssb[:, 2 * n, :], ssb[:, 2 * n + 1, :], ALU.add)
        P.act(rn[:, :, :], rn[:, :, :], AF.Sqrt, bias=EPSB[0][:, :], scale=1.0)
        P.recip(rn[:, :, :], rn[:, :, :])
        P.ts(rn[:, :, :], rn[:, :, :], 1.0 / (2 * LC), None, ALU.mult)
        Spl = P.sb(es, "cSpl", [128, 2, 2, 512], BF16)
        Smi = P.sb(es, "cSmi", [128, 2, 2, 512], BF16)
        kv = lambda dr: V(kfc.h[:, :, :].rearrange("p t (n d c) -> p t n d c", n=2, d=2)[:, :, :, dr, :], kfc.key)
        P.tt(Spl[:, :, :, :], kv(0), kv(1), ALU.add)
        P.tt(Smi[:, :, :, :], kv(1), kv(0), ALU.subtract)
        st = P.sb(es, "cst", [128, 4, 512], F32)
        W5c = P.sb(es, "cW5c", [128, 4, 512], BF16)
        W5s = P.sb(es, "cW5s", [128, 4, 512], BF16)
        nW5s = P.sb(es, "cnW5s", [128, 4, 512], BF16)
        P.dma("sp", st[:, :, :], V(k.hW5c.h[:, :].rearrange("(c p) k -> p c k", p=128), None))
        P.copy(W5c[:, :, :], st[:, :, :])
        P.dma("sp", st[:, :, :], V(k.hW5s.h[:, :].rearrange("(c p) k -> p c k", p=128), None))
        P.copy(W5s[:, :, :], st[:, :, :])
        P.ts(nW5s[:, :, :], st[:, :, :], -1.0, None, ALU.mult)
        Spc = P.sb(es, "cSpc", [128, 4, 2, 2, 512], F32)
        for kc in range(4):
            ks = slice(kc * 128, (kc + 1) * 128)
            for n in range(2):
                for ri, (tab, src) in enumerate(((W5c, Spl), (W5s, Smi))):
                    p_ = pf[nb % 2]
                    nb += 1
                    for tc in range(2):
                        P.mm(p_[:, :], tab[:, tc, ks], src[:, tc, n, :], start=(tc == 0), stop=(tc == 1))
                    P.tt(Spc[:, kc, n, ri, :], p_[:, :], rn[:, n, :], ALU.mult)
        uf = P.sb(es, "cuf", [128, 12, LC], F32)
        P.dma("sp", uf[:, :, :], V(k.hyU.h[:, L:T].rearrange("(g p) t -> p g t", p=128), None))
        uc = P.sb(es, "cuc", [128, 2, 1536], F32)
        pT = P.ps(es, "cpT")
        for tc in range(2):
            for g0 in range(0, 12, 4):
                for g in range(4):
                    P.transpose(pT[:, g * 128:(g + 1) * 128], uf[:, g0 + g, tc * 128:(tc + 1) * 128], idf[:, :])
                P.copy(uc[:, tc, g0 * 128:(g0 + 4) * 128], pT[:, :])
        skc = P.sb(es, "cskc", [128, 2, 512], F32)
        P.dma("sp", V(skc.h[:, :, :].rearrange("p n c -> p (n c)"), skc.key),
              V(k.hy_skip.h[l, :, :].rearrange("n c -> (n c)").unsqueeze(0).to_broadcast([128, 1024]), None))
        X = P.sb(es, "cX", [128, 2, 512], BF16)
        yp = P.sb(es, "cyp", [128, 2, 512], F32)
        P.copy(yp[:, :, :], uc[:, :, 0:512])
        P.copy(X[:, :, :], uc[:, :, 0:512])
        Z = P.sb(es, "cZ", [128, 4, 2, 512], BF16)
        tw = [P.sb(es, f"ctw{i}", [128, 512], F32) for i in range(4)]
        pY = [P.ps(es, f"cpY{i}") for i in range(2)]
        yb = P.sb(es, "cyb", [128, 2, 512], BF16)
        for n in range(2):
            for kc in range(4):
                ks = slice(kc * 128, (kc + 1) * 128)
                for ri, tab in enumerate((W5c, nW5s)):
                    for tc in range(2):
                        P.mm(pY[ri][:, :], tab[:, tc, ks], X[:, tc, :], start=(tc == 0), stop=(tc == 1))
                t1, t2, t3, t4 = [t[:, :] for t in tw]
                P.tt(t1, pY[0][:, :], Spc[:, kc, n, 0, :], ALU.mult)
                P.tt(t2, pY[1][:, :], Spc[:, kc, n, 1, :], ALU.mult)
                P.tt(t3, pY[0][:, :], Spc[:, kc, n, 1, :], ALU.mult)
                P.tt(t4, pY[1][:, :], Spc[:, kc, n, 0, :], ALU.mult)
                P.tt(Z[:, kc, 0, :], t1, t2, ALU.subtract, eng="pool")
                P.tt(Z[:, kc, 1, :], t3, t4, ALU.add, eng="pool")
            for tc in range(2):
                ts_ = slice(tc * 128, (tc + 1) * 128)
                p_ = pf[tc]
                for kc in range(4):
                    P.mm(p_[:, :], W5c[:, kc, ts_], Z[:, kc, 0, :], start=(kc == 0), stop=False)
                    P.mm(p_[:, :], nW5s[:, kc, ts_], Z[:, kc, 1, :], start=False, stop=(kc == 3))
                a = tw[tc][:, :]
                P.tt(a, yp[:, tc, :], skc[:, n, :], ALU.mult, eng="pool")
                P.tt(a, a, p_[:, :], ALU.add)
                if n == 0:
                    P.tt(yp[:, tc, :], a, uc[:, tc, 512:1024], ALU.mult, eng="pool")
                    P.copy(X[:, tc, :], yp[:, tc, :])
                else:
                    P.tt(yb[:, tc, :], a, uc[:, tc, 1024:1536], ALU.mult, eng="pool")
        pTb = P.ps(es, "cpTb", [128, 8, 128], BF16)
        oT = P.sb(es, "coT", [128, 4, 2, 128], BF16)
        for tc in range(2):
            for c in range(4):
                P.transpose(pTb[:, tc * 4 + c, :], yb[:, tc, c * 128:(c + 1) * 128], idb[:, :])
        P.copy(V(oT.h[:, :, :, :].rearrange("p c t q -> p t c q"), oT.key), V(pTb.h[:, :, :].rearrange("p (t c) q -> p t c q", t=2), pTb.key))
        P.dma("act", V(k.yT.h[0, :, L:T].rearrange("(c p) (t q) -> p c t q", p=128, q=128), None), oT[:, :, :, :])
        P.flush()

def phase_merge(P, k, l, with_ctx):
    with ExitStack() as es:
        idf, idb = load_ident(P, es, k)
        stg = mk_stage(P, es, 2, 2048)
        Wup = P.sb(es, "eWup", [128, 4, 4, 1024], BF16)
        Wg = P.sb(es, "eWg", [128, 8, 4096], BF16)
        Wout = P.sb(es, "eWout", [128, 8, 1024], BF16)
        for n in range(4):
            for c in range(4):
                load_cast(P, stg, Wup[:, n, c, :], k.w_up[l, n, c * 128:(c + 1) * 128, :], 1024)
        for c in range(8):
            for hh in range(2):
                load_cast(P, stg, Wg[:, c, hh * 2048:(hh + 1) * 2048],
                          k.w_in[l, c * 128:(c + 1) * 128, NPROJ + hh * 2048:NPROJ + (hh + 1) * 2048], 2048)
            load_cast(P, stg, Wout[:, c, :], k.w_out[l, c * 128:(c + 1) * 128, :], 1024)
        g1 = P.sbpool(es, "eg1", 2, [128, D], F32)
        for j in range(2):
            P.dma("sp", g1[j][:, :], V(k.mod.h[l, j:j + 1, 2 * D:3 * D].to_broadcast([128, D]), None))
        hts = P.sbpool(es, "eht", 3, [128, 8, 128], BF16)
        yts = P.sbpool(es, "eyt", 3, [128, 4, 4, 128], BF16)
        xts = P.sbpool(es, "ext", 3, [128, D], F32)
        acc = P.sbpool(es, "eacc", 2, [128, D], F32)
        accb = P.sbpool(es, "eaccb", 2, [128, D], BF16)
        accT = P.sbpool(es, "eaccT", 2, [128, 8, 128], BF16)
        sg = P.sbpool(es, "esg", 3, [128, 512], F32)
        tm = P.sbpool(es, "etm", 3, [128, 512], F32)
        psA = [P.ps(es, f"epA{i}") for i in range(3)]
        psB = [P.ps(es, f"epB{i}") for i in range(3)]
        psT = P.ps(es, "epT", [128, 8, 128], BF16)
        hTv = k.hT.h.rearrange("(c p) t -> p c t", p=128)
        tiles = list(range(NTL)) + (list(range(NTL, NT)) if with_ctx else [])
        na = nb = 0
        cn = [0, 0]

        def load(i):
            t0 = tiles[i] * 128
            P.dma("sp", hts[i % 3][:, :, :], V(hTv[:, :, t0:t0 + 128], None))
            for n in range(4):
                P.dma("sp", yts[i % 3][:, n, :, :], V(k.yT.h[n, :, t0:t0 + 128].rearrange("(c p) t -> p c t", p=128), None))
            P.dma("sp", xts[i % 3][:, :], k.xs[t0:t0 + 128, :])

        def body(i):
            ti = tiles[i]
            na, nb = cn
            t0 = ti * 128
            j = 0 if ti < NTL else 1
            ht, yt, xt, ac = hts[i % 3], yts[i % 3], xts[i % 3], acc[i % 2]
            for n in range(4):
                for hf in range(2):
                    fs = slice(hf * 512, (hf + 1) * 512)
                    pu = psA[na % 3]
                    na += 1
                    pg = psB[nb % 3]
                    nb += 1
                    for c in range(4):
                        P.mm(pu[:, :], yt[:, n, c, :], Wup[:, n, c, fs], start=(c == 0), stop=(c == 3))
                    for c in range(8):
                        P.mm(pg[:, :], ht[:, c, :], Wg[:, c, n * 1024 + hf * 512:n * 1024 + (hf + 1) * 512],
                             start=(c == 0), stop=(c == 7))
                    s_ = sg[nb % 3]
                    P.act(s_[:, :], pg[:, :], AF.Sigmoid)
                    if n == 0:
                        P.tt(ac[:, fs], s_[:, :], pu[:, :], ALU.mult)
                    else:
                        t_ = tm[nb % 3]
                        P.tt(t_[:, :], s_[:, :], pu[:, :], ALU.mult)
                        P.tt(ac[:, fs], ac[:, fs], t_[:, :], ALU.add)
            ab = accb[i % 2]
            P.act(ab[:, :], ac[:, :], AF.Copy)
            for c in range(8):
                P.transpose(psT[:, c, :], ab[:, c * 128:(c + 1) * 128], idb[:, :])
            at = accT[i % 2]
            P.act(at[:, :, :], psT[:, :, :], AF.Copy)
            for hf in range(2):
                fs = slice(hf * 512, (hf + 1) * 512)
                po = psA[na % 3]
                na += 1
                for c in range(8):
                    P.mm(po[:, :], at[:, c, :], Wout[:, c, fs], start=(c == 0), stop=(c == 7))
                t_ = tm[(nb + hf) % 3]
                P.tt(t_[:, :], po[:, :], g1[j][:, fs], ALU.mult)
                P.tt(xt[:, fs], xt[:, fs], t_[:, :], ALU.add)
            P.dma("sp", k.xs[t0:t0 + 128, :], xt[:, :])
            cn[0], cn[1] = na, nb

        prefetch_loop(len(tiles), load, body, 1)
        P.flush()


def phase_mlp(P, k, l, with_ctx):
    TBK = 256
    with ExitStack() as es:
        stg = mk_stage(P, es, 2, 1024)
        W1 = P.sb(es, "fW1", [128, 8, 4096], BF16)
        W2 = P.sb(es, "fW2", [128, 32, 1024], BF16)
        for c in range(8):
            for q in range(4):
                load_cast(P, stg, W1[:, c, q * 1024:(q + 1) * 1024], k.mlp_w1[l, c * 128:(c + 1) * 128, q * 1024:(q + 1) * 1024], 1024)
        for c in range(32):
            load_cast(P, stg, W2[:, c, :], k.mlp_w2[l, c * 128:(c + 1) * 128, :], 1024)
        b1 = P.sb(es, "fb1", [128, 32], F32)
        P.dma("sp", b1[:, :], V(k.mlp_b1.h[l:l + 1, :].rearrange("o (c p) -> p (o c)", p=128), None), allow_slow_non_contiguous=True)
        b2 = P.sb(es, "fb2", [128, D], F32)
        P.dma("sp", b2[:, :], V(k.mlp_b2.h[l:l + 1, :].to_broadcast([128, D]), None))
        g2 = P.sbpool(es, "fg2", 2, [128, D], F32)
        for j in range(2):
            P.dma("sp", g2[j][:, :], V(k.mod.h[l, j:j + 1, 5 * D:6 * D].to_broadcast([128, D]), None))
        hts = P.sbpool(es, "fht", 2, [128, 8, TBK], BF16)
        aT = P.sb(es, "faT", [128, 32, TBK], BF16)
        rt = P.sbpool(es, "frt", 2, [128, TBK], F32)
        xts = P.sbpool(es, "fxt", 4, [128, D], F32)
        tm = P.sbpool(es, "ftm", 2, [128, 512], F32)
        psA = [P.ps(es, f"fpA{i}") for i in range(3)]
        psB = [P.ps(es, f"fpB{i}") for i in range(2)]
        hTv = k.hT.h.rearrange("(c p) t -> p c t", p=128)
        nblk = (T if with_ctx else L) // TBK
        na = nb = nx = 0
        def load_blk(bi):
            t0 = bi * TBK
            P.dma("sp", hts[bi % 2][:, :, :], V(hTv[:, :, t0:t0 + TBK], None))

        load_blk(0)
        for bi in range(nblk):
            t0 = bi * TBK
            j = 0 if t0 < L else 1
            ht = hts[bi % 2]
            nsub = TBK // 128
            for sub in range(nsub):
                P.dma("sp", xts[(bi * nsub + sub) % 4][:, :], k.xs[t0 + sub * 128:t0 + (sub + 1) * 128, :])
            for fc in range(32):
                ps = psA[na % 3]
                na += 1
                for c in range(8):
                    P.mm(ps[:, 0:TBK], W1[:, c, fc * 128:(fc + 1) * 128], ht[:, c, :], start=(c == 0), stop=(c == 7))
                r_ = rt[fc % 2]
                P.act(r_[:, :], ps[:, 0:TBK], AF.Relu, bias=b1[:, fc:fc + 1])
                P.act(aT[:, fc, :], r_[:, :], AF.Square)
            if bi + 1 < nblk:
                load_blk(bi + 1)
            for sub in range(nsub):
                xt = xts[(bi * nsub + sub) % 4]
                r0 = t0 + sub * 128
                for hf in range(2):
                    fs = slice(hf * 512, (hf + 1) * 512)
                    po = psB[nb % 2]
                    nb += 1
                    for fc in range(32):
                        P.mm(po[:, :], aT[:, fc, sub * 128:(sub + 1) * 128], W2[:, fc, fs], start=(fc == 0), stop=(fc == 31))
                    t_ = tm[nb % 2]
                    P.tt(t_[:, :], po[:, :], b2[:, fs], ALU.add)
                    P.tt(t_[:, :], t_[:, :], g2[j][:, fs], ALU.mult)
                    P.tt(xt[:, fs], xt[:, fs], t_[:, :], ALU.add)
                P.dma("sp", k.xs[r0:r0 + 128, :], xt[:, :])
        P.flush()


def phase_final(P, k):
    with ExitStack() as es:
        g = P.sb(es, "zg", [128, D], F32)
        P.dma("sp", g[:, :], V(k.final_g.h[0:1, :].to_broadcast([128, D]), None))
        xt = P.sbpool(es, "zx", 4, [128, D], F32)
        junk2 = P.sbpool(es, "zj", 2, [128, D], F32)
        ss = P.sbpool(es, "zs", 3, [128, 1], F32)
        yo = P.sbpool(es, "zy", 3, [128, D], F32)
        def load(ti):
            P.dma("sp", xt[ti % 4][:, :], k.xs[ti * 128:(ti + 1) * 128, :])

        def body(ti):
            x_, s_, y_ = xt[ti % 4], ss[ti % 3], yo[ti % 3]
            P.act(junk2[ti % 2][:, :], x_[:, :], AF.Square, accum_out=s_[:, :])
            yield
            P.act(s_[:, :], s_[:, :], AF.Sqrt, bias=EPSB[0][:, :], scale=1.0 / D)
            yield
            P.recip(s_[:, :], s_[:, :])
            yield
            P.stt(y_[:, :], x_[:, :], s_[:, :], g[:, :], ALU.mult, ALU.mult)
            P.dma("sp", k.y[ti * 128:(ti + 1) * 128, :], y_[:, :])

        prefetch_pipelined(NTL, load, body, 2, 2)
        P.flush()
def build(stages=99, layers=DEPTH):
    nc = bass.Bass("TRN2", target_bir_lowering=False)
    k = declare(nc)
    declare_ml(nc, k)
    declare_hy(nc, k)
    P = Prog(nc)
    with ExitStack() as ges:
        make_consts(P, ges)
        phase_init(P, k)
        if stages >= 0.5:
            phase_mod(P, k)
        for l in range(layers if stages >= 1 else 0):
            wco = l < DEPTH - 1
            phase_norm(P, k, l, k.xs, k.ln1_g, 1, 0)
            if stages >= 2:
                phase_inproj(P, k, l)
            if stages >= 3 and 'noattn' not in DEBUG:
                phase_ga_prep(P, k, l)
                phase_ga_attn(P, k, l, wco)
            if stages >= 4 and 'noattn' not in DEBUG:
                phase_wa_prep(P, k, l)
                phase_wa_attn(P, k, l, wco)
            if stages >= 5 and 'noml' not in DEBUG:
                phase_ml_prep(P, k, l)
                phase_ml_scan(P, k, l)
                phase_ml_finish(P, k, l, wco)
            if stages >= 6 and 'nohy' not in DEBUG:
                phase_hy_conv(P, k, l)
                phase_hy_filt(P, k, l)
                phase_hy_spec(P, k, l)
                phase_hy_data(P, k, l)
                if wco:
                    phase_hy_ctx(P, k, l)
            if 'ya_in' in DEBUG:
                ya_dbg = dram(nc, f"ya_dbg{l}", [512, T], BF16, "ExternalInput")
                P.dma("sp", k.yT[0, :, :], ya_dbg[:, :])
                P.flush()
            if stages >= 7:
                phase_merge(P, k, l, wco)
                phase_norm(P, k, l, k.xs, k.ln2_g, 4, 3, do_ctx=wco)
                phase_mlp(P, k, l, wco)
        if stages >= 8:
            phase_final(P, k)
        else:
            P.dma("sp", k.y[0:128, :], k.xs[0:128, :])
            P.flush()
    P.es.close()
    return nc


def make_inputs(inputs, b):
    f = lambda a: np.ascontiguousarray(np.asarray(a, dtype=np.float32))
    m = {
        "x": f(inputs["x"][b]),
        "ctx": f(inputs["ctx"][b]),
        "cvec": f(np.stack([inputs["c"][b], inputs["c_ctx"]])),
        "final_g": f(inputs["final_g"]).reshape(1, D),
        "ident": np.eye(128, dtype=np.float32),
    }
    m.update(CONSTS)
    for n in ["w_mod", "b_mod", "ln1_g", "ln2_g", "w_in", "w_up", "w_out", "mlp_w1", "mlp_b1", "mlp_w2", "mlp_b2",
              "ga_q_g", "ga_k_g", "wa_sink", "ml_conv_w", "ml_conv_b", "ml_gate_b", "ml_norm_g"] + HY_IN:
        m[n] = f(inputs[n])
    return m


_NC = [None]


def kernel(**inputs):
    if _NC[0] is None:
        _NC[0] = build()
    nc = _NC[0]
    in_maps = [make_inputs(inputs, b) for b in range(8)]
    res = run_bass_kernel_spmd(nc, in_maps, core_ids=list(range(8)))
    return np.stack([r["y"] for r in res.results], axis=0)
```

```python
from concourse.bass_utils import run_bass_kernel_spmd
import sys
import numpy as np
import concourse.bass as bass
import concourse.mybir as mybir
from contextlib import ExitStack

F32 = mybir.dt.float32
BF16 = mybir.dt.bfloat16
AF = mybir.ActivationFunctionType
ALU = mybir.AluOpType
AX = mybir.AxisListType


class V:
    __slots__ = ("ap", "key")

    def __init__(self, ap, key):
        self.ap = ap
        self.key = key


class Buf:
    def __init__(self, handle, key, track=True):
        self.h = handle
        self.key = key
        self.track = track

    def __getitem__(self, idx):
        return V(self.h[idx], self.key if self.track else None)

    def v(self, idx, sub):
        return V(self.h[idx], (self.key, sub))

    def ap(self, ap, sub=None):
        return V(ap, (self.key, sub) if sub is not None else (self.key if self.track else None))


class Op:
    __slots__ = ("eng", "fn", "reads", "writes", "dma", "deps", "tok", "waits", "marked")


ENGS = ["pe", "act", "dve", "pool", "sp"]
BLOCKNAME = {"pe": "tensor", "act": "scalar", "dve": "vector", "pool": "gpsimd", "sp": "sync"}
NDMASEM = 12


class Prog:
    def __init__(self, nc):
        self.nc = nc
        self.ops = []
        self.es = ExitStack()
        self.sems = {e: self.es.enter_context(nc.semaphore("cs_" + e)) for e in ENGS}
        self.dsems = {q: [self.es.enter_context(nc.semaphore(f"ds_{q}{i}")) for i in range(NDMASEM)]
                      for q in ("sp", "pool", "act")}
        self.allsems = list(self.sems.values()) + [s for l in self.dsems.values() for s in l]
        self.nphase = 0
        self.phase_names = []
        self.uid = 0

    def sb(self, es, name, shape, dtype):
        self.uid += 1
        nm = f"{name}_{self.uid}"
        h = es.enter_context(self.nc.sbuf_tensor(nm, list(shape), dtype))
        return Buf(h, nm)

    def sbpool(self, es, name, n, shape, dtype):
        return [self.sb(es, f"{name}{i}", shape, dtype) for i in range(n)]

    def ps(self, es, name, shape=(128, 512), dtype=F32):
        self.uid += 1
        nm = f"{name}_{self.uid}"
        h = es.enter_context(self.nc.psum_tensor(nm, list(shape), dtype))
        return Buf(h, nm)

    def op(self, eng, fn, reads, writes, dma=False):
        o = Op()
        o.eng = eng
        o.fn = fn
        o.reads = [k for k in reads if k is not None]
        o.writes = [k for k in writes if k is not None]
        o.dma = dma
        self.ops.append(o)
        return o

    @staticmethod
    def _ks(vs):
        return [v.key for v in vs if isinstance(v, V)]

    @staticmethod
    def _a(v):
        return v.ap if isinstance(v, V) else v

    def dma(self, q, out, in_, **kw):
        return self.op(q, lambda e: e.dma_start(out=out.ap, in_=in_.ap, **kw), [in_.key], [out.key], dma=True)

    def mm(self, out, lhsT, rhs, start=True, stop=True, **kw):
        return self.op("pe", lambda e: e.matmul(out.ap, lhsT.ap, rhs.ap, start=start, stop=stop, **kw),
                       [lhsT.key, rhs.key], [out.key])

    def transpose(self, out, in_, ident):
        return self.op("pe", lambda e: e.transpose(out.ap, in_.ap, ident.ap), [in_.key, ident.key], [out.key])

    def act(self, out, in_, func, bias=0.0, scale=1.0, accum_out=None, eng="act"):
        a = self._a
        rd = self._ks([in_, bias, scale])
        wr = self._ks([out, accum_out])
        kw = {}
        if accum_out is not None:
            kw["accum_out"] = accum_out.ap
        return self.op(eng, lambda e: e.activation(out.ap, in_.ap, func, bias=a(bias), scale=a(scale), **kw), rd, wr)

    def tt(self, out, in0, in1, op, eng="dve"):
        return self.op(eng, lambda e: e.tensor_tensor(out.ap, in0.ap, in1.ap, op), [in0.key, in1.key], [out.key])

    def ts(self, out, in0, s1, s2=None, op0=ALU.mult, op1=None, accum_out=None, eng="dve"):
        a = self._a
        rd = self._ks([in0, s1, s2])
        wr = self._ks([out, accum_out])
        kw = {}
        if op1 is not None:
            kw["op1"] = op1
        if accum_out is not None:
            kw["accum_out"] = accum_out.ap
        return self.op(eng, lambda e: e.tensor_scalar(out.ap, in0.ap, a(s1), a(s2) if s2 is not None else None, op0, **kw), rd, wr)

    def stt(self, out, in0, scalar, in1, op0, op1, eng="dve"):
        a = self._a
        rd = self._ks([in0, scalar, in1])
        return self.op(eng, lambda e: e.scalar_tensor_tensor(out.ap, in0.ap, a(scalar), in1.ap, op0, op1), rd, [out.key])

    def copy(self, out, in_, eng="dve"):
        return self.op(eng, lambda e: e.tensor_copy(out.ap, in_.ap), [in_.key], [out.key])

    def memset(self, out, val, eng="dve"):
        return self.op(eng, lambda e: e.memset(out.ap, val), [], [out.key])

    def recip(self, out, in_):
        return self.op("dve", lambda e: e.reciprocal(out.ap, in_.ap), [in_.key], [out.key])

    def reduce(self, out, in_, op=ALU.add, axis=AX.X, eng="dve"):
        return self.op(eng, lambda e: e.tensor_reduce(out.ap, in_.ap, axis, op), [in_.key], [out.key])

    def flush(self, name=None):
        ops = self.ops
        self.ops = []
        if not ops:
            return
        self.phase_names.append(name or sys._getframe(1).f_code.co_name)
        nc = self.nc
        last_w = {}
        readers = {}
        for i, o in enumerate(ops):
            deps = set()
            for k in o.reads:
                if k in last_w:
                    deps.add(last_w[k])
            for k in o.writes:
                if k in last_w:
                    deps.add(last_w[k])
                for r in readers.get(k, ()):
                    deps.add(r)
            deps.discard(i)
            o.deps = [d for d in deps if not (o.eng == "pe" and ops[d].eng == "pe" and not ops[d].dma and not o.dma)]
            for k in o.reads:
                readers.setdefault(k, []).append(i)
            for k in o.writes:
                last_w[k] = i
                readers[k] = []
            o.marked = False
            o.tok = None
            o.waits = []
        for o in ops:
            for d in o.deps:
                ops[d].marked = True
        cnt = {e: 0 for e in ENGS}
        dcum = {}
        dnext = {q: 0 for q in self.dsems}
        dprev = {}
        for o in ops:
            if o.dma:
                q = o.eng
                pool = self.dsems[q]
                s = pool[dnext[q] % NDMASEM]
                dnext[q] += 1
                if s in dprev:
                    o.waits.append(dprev[s])
                dcum[s] = dcum.get(s, 0) + 16
                o.tok = (s, dcum[s], 16)
                dprev[s] = (s, dcum[s])
            elif o.marked:
                cnt[o.eng] += 1
                o.tok = (self.sems[o.eng], cnt[o.eng], 1)
        for o in ops:
            for d in o.deps:
                t = ops[d].tok
                o.waits.append((t[0], t[1]))
        per = {e: [o for o in ops if o.eng == e] for e in ENGS}
        with nc.Block() as block:
            for e in ENGS:
                lst = per[e]
                if not lst:
                    continue

                def body(eng, lst=lst, e=e):
                    known = {}
                    for o in lst:
                        for (s, v) in o.waits:
                            if known.get(s, 0) < v:
                                eng.wait_ge(s, v)
                                known[s] = v
                        ins = o.fn(eng)
                        if o.tok is not None:
                            ins.then_inc(o.tok[0], o.tok[2])
                    if e in self.dsems:
                        for s in self.dsems[e]:
                            if s in dprev and known.get(s, 0) < dprev[s][1]:
                                eng.wait_ge(s, dprev[s][1])

                getattr(block, BLOCKNAME[e])(body)
        with nc.Block() as block:
            @block.sync
            def _(eng):
                for s in self.allsems:
                    eng.sem_clear(s)
        self.nphase += 1


def prefetch_loop(n, load_fn, body_fn, pf=2):
    for i in range(n + pf):
        if i < n:
            load_fn(i)
        if i >= pf:
            body_fn(i - pf)


def prefetch_pipelined(n, load_fn, body_gen, pf=2, G=2):
    nl = 0
    for g0 in range(0, n, G):
        while nl < min(n, g0 + G + pf):
            load_fn(nl)
            nl += 1
        alive = [body_gen(i) for i in range(g0, min(n, g0 + G))]
        while alive:
            nxt = []
            for g in alive:
                try:
                    next(g)
                    nxt.append(g)
                except StopIteration:
                    pass
            alive = nxt

L = 8192
LC = 256
T = L + LC
D = 1024
NIN = 9488
NPROJ = 5392
DEPTH = 2
EPS = 1e-6
TB = [(i * 512, 512) for i in range(16)] + [(8192, 256)]

DEBUG = {}


def dram(nc, name, shape, dtype=F32, kind=None):
    if kind is None and name in DEBUG:
        kind = "ExternalOutput"
    if kind is None:
        t = nc.dram_tensor(name, list(shape), dtype)
    else:
        t = nc.dram_tensor(name, list(shape), dtype, kind=kind)
    return Buf(t.ap(), name, track=False)


class K:
    pass


def declare(nc):
    k = K()
    inp = lambda n, s: dram(nc, n, s, F32, "ExternalInput")
    k.x = inp("x", [L, D])
    k.ctx = inp("ctx", [LC, D])
    k.cvec = inp("cvec", [2, D])
    k.w_mod = inp("w_mod", [DEPTH, D, 6 * D])
    k.b_mod = inp("b_mod", [DEPTH, 6 * D])
    k.ln1_g = inp("ln1_g", [DEPTH, D])
    k.ln2_g = inp("ln2_g", [DEPTH, D])
    k.w_in = inp("w_in", [DEPTH, D, NIN])
    k.w_up = inp("w_up", [DEPTH, 4, 512, D])
    k.w_out = inp("w_out", [DEPTH, D, D])
    k.mlp_w1 = inp("mlp_w1", [DEPTH, D, 4 * D])
    k.mlp_b1 = inp("mlp_b1", [DEPTH, 4 * D])
    k.mlp_w2 = inp("mlp_w2", [DEPTH, 4 * D, D])
    k.mlp_b2 = inp("mlp_b2", [DEPTH, D])
    k.final_g = inp("final_g", [1, D])
    k.ident = inp("ident", [128, 128])
    k.y = dram(nc, "y", [L, D], F32, "ExternalOutput")
    k.xs = dram(nc, "xs", [T, D])
    k.mod = dram(nc, "mod", [DEPTH, 2, 6 * D])
    k.hT = dram(nc, "hT", [D, T], BF16)
    k.hyT = dram(nc, "hyT", [1536, T])
    k.ga = dram(nc, "ga", [T, 1024])
    k.mlqkT = dram(nc, "mlqkT", [1024, T])
    k.mlvo = dram(nc, "mlvo", [T, 1024 + 16])
    k.wa = dram(nc, "wa", [T, 768])
    k.ga_q_g = inp("ga_q_g", [DEPTH, 128])
    k.ga_k_g = inp("ga_k_g", [DEPTH, 128])
    k.wa_sink = inp("wa_sink", [DEPTH, 8])
    k.rc128 = inp("rc128", [L, 64])
    k.rs128 = inp("rs128", [L, 64])
    k.rc64 = inp("rc64", [L, 32])
    k.rs64 = inp("rs64", [L, 32])
    k.mlo = inp("mlo", [128, 128])
    k.mhi = inp("mhi", [128, 128])
    k.gaQT = dram(nc, "gaQT", [4, 128, T], BF16)
    k.gaKT = dram(nc, "gaKT", [2, 128, T], BF16)
    k.gaV = dram(nc, "gaV", [T, 256], BF16)
    k.waQT = dram(nc, "waQT", [128, 4, T], BF16)
    k.waKT = dram(nc, "waKT", [128, T], BF16)
    k.waV = dram(nc, "waV", [T, 128], BF16)
    k.yT = dram(nc, "yT", [4, 512, T], BF16)
    return k


def load_ident(P, es, k):
    idf = P.sb(es, "idf", [128, 128], F32)
    idb = P.sb(es, "idb", [128, 128], BF16)
    P.dma("sp", idf[:, :], k.ident[:, :])
    P.copy(idb[:, :], idf[:, :])
    return idf, idb


_STG = [0]


def mk_stage(P, es, n=3, w=2048):
    return P.sbpool(es, "stg", n, [128, w], F32)


def load_cast(P, stg, dst, src, ncol, q="sp"):
    i = _STG[0]
    _STG[0] += 1
    st = stg[i % len(stg)]
    P.dma(q, st[:, 0:ncol], src)
    if i % 2 == 0:
        P.act(dst, st[:, 0:ncol], AF.Copy)
    else:
        P.copy(dst, st[:, 0:ncol])

def phase_mod(P, k):
    nc = P.nc
    with ExitStack() as es:
        cv = P.sb(es, "cv", [128, 2, 8], F32)
        sT = P.sb(es, "sT", [128, 8, 2], BF16)
        P.dma("sp", cv[:, :, :], V(k.cvec.h.rearrange("j (p c) -> p j c", c=8), None))
        sg = P.sb(es, "sg", [128, 2, 8], F32)
        P.act(sg[:, :, :], cv[:, :, :], AF.Sigmoid)
        for j in range(2):
            P.tt(sT[:, :, j], cv[:, j, :], sg[:, j, :], ALU.mult)
        wts = P.sbpool(es, "wm", 2, [128, 8, 2048], BF16)
        stg = mk_stage(P, es)
        bm = P.sb(es, "bm", [2, 6 * D], F32)
        ot = P.sb(es, "ot", [2, 6 * D], F32)
        pss = [P.ps(es, f"pm{i}") for i in range(2)]
        n = 0
        for l in range(DEPTH):
            P.dma("sp", bm[:, :], V(k.b_mod.h[l:l + 1, :].broadcast(0, 2) if False else k.b_mod.h[l:l + 1, :].to_broadcast([2, 6 * D]), None))
            for fb in range(3):
                w = wts[n % 2]
                wv = k.w_mod.h[l].rearrange("(p c) f -> p c f", c=8)
                for c in range(8):
                    load_cast(P, stg, w[:, c, :], V(wv[:, c, fb * 2048:(fb + 1) * 2048], None), 2048)
                for s in range(4):
                    ps = pss[s % 2]
                    for c in range(8):
                        P.mm(ps[0:2, :], sT[:, c, :], w[:, c, s * 512:(s + 1) * 512], start=(c == 0), stop=(c == 7))
                    f0 = fb * 2048 + s * 512
                    P.tt(ot[:, f0:f0 + 512], ps[0:2, :], bm[:, f0:f0 + 512], ALU.add)
                n += 1
            P.dma("sp", k.mod[l, :, :], ot[:, :])
            P.flush()


def phase_norm(P, k, l, src, g_dram, sc_idx, sh_idx, ntiles_lat=64, do_ctx=True):
    with ExitStack() as es:
        idf, idb = load_ident(P, es, k)
        A = P.sbpool(es, "A", 2, [128, D], F32)
        Bt = P.sbpool(es, "B", 2, [128, D], F32)
        gt = P.sb(es, "gt", [128, D], F32)
        P.dma("sp", gt[:, :], V(g_dram.h[l:l + 1, :].to_broadcast([128, D]), None))
        for j in range(2):
            P.dma("sp", A[j][:, :], V(k.mod.h[l, j:j + 1, sc_idx * D:(sc_idx + 1) * D].to_broadcast([128, D]), None))
            P.dma("sp", Bt[j][:, :], V(k.mod.h[l, j:j + 1, sh_idx * D:(sh_idx + 1) * D].to_broadcast([128, D]), None))
            P.stt(A[j][:, :], A[j][:, :], 1.0, gt[:, :], ALU.add, ALU.mult)
        xt = P.sbpool(es, "xt", 4, [128, D], F32)
        junk2 = P.sbpool(es, "junk", 2, [128, D], F32)
        ssq = P.sbpool(es, "ssq", 3, [128, 1], F32)
        rstd = P.sbpool(es, "rstd", 3, [128, 1], F32)
        t1 = P.sbpool(es, "t1", 2, [128, D], F32)
        hb = P.sbpool(es, "hb", 2, [128, D], BF16)
        hTs = P.sbpool(es, "hTs", 3, [128, 8, 128], BF16)
        pst = [P.ps(es, f"pt{i}", [128, 8, 128], BF16) for i in range(2)]
        tiles = list(range(ntiles_lat)) + ([64, 65] if do_ctx else [])

        def load(n):
            ti = tiles[n]
            P.dma("sp", xt[n % 4][:, :], src[ti * 128:(ti + 1) * 128, :])

        def body(n):
            ti = tiles[n]
            j = 0 if ti < 64 else 1
            x_ = xt[n % 4]
            P.act(junk2[n % 2][:, :], x_[:, :], AF.Square, accum_out=ssq[n % 3][:, :])
            yield
            P.act(rstd[n % 3][:, :], ssq[n % 3][:, :], AF.Sqrt, bias=EPSB[0][:, :], scale=1.0 / D)
            yield
            P.recip(rstd[n % 3][:, :], rstd[n % 3][:, :])
            yield
            P.stt(t1[n % 2][:, :], x_[:, :], rstd[n % 3][:, :], A[j][:, :], ALU.mult, ALU.mult)
            yield
            P.tt(hb[n % 2][:, :], t1[n % 2][:, :], Bt[j][:, :], ALU.add)
            yield
            ps = pst[n % 2]
            for c in range(8):
                P.transpose(ps[:, c, :], hb[n % 2][:, c * 128:(c + 1) * 128], idb[:, :])
            yield
            hs = hTs[n % 3]
            P.copy(hs[:, :, :], ps[:, :, :])
            P.dma("sp", V(k.hT.h.rearrange("(c p) t -> p c t", p=128)[:, :, ti * 128:(ti + 1) * 128], None), hs[:, :, :])

        prefetch_pipelined(len(tiles), load, body, 2, 2)
        P.flush()


EPSB = [None]


def make_consts(P, es):
    e = P.sb(es, "epsb", [128, 1], F32)
    P.memset(e[:, :], EPS)
    EPSB[0] = e
    P.flush()


def phase_inproj(P, k, l):
    groups = [
        (0, 1536, "fm", k.hyT, 0),
        (1536, 1024, "tm", k.ga, 0),
        (2560, 1024, "fm", k.mlqkT, 0),
        (3584, 1024 + 16, "tm", k.mlvo, 0),
        (4624, 768, "tm", k.wa, 0),
    ]
    with ExitStack() as es:
        W = P.sb(es, "Win", [128, 8, NPROJ], BF16)
        wv = k.w_in.h[l].rearrange("(c p) n -> p c n", p=128)
        stg = mk_stage(P, es)
        for c in range(8):
            for h in range(4):
                load_cast(P, stg, W[:, c, h * 1348:(h + 1) * 1348], V(wv[:, c, h * 1348:(h + 1) * 1348], None), 1348)
        hts = P.sbpool(es, "ht", 2, [128, 8, 512], BF16)
        pss = [P.ps(es, f"pi{i}") for i in range(8)]
        osb = P.sbpool(es, "osb", 8, [128, 512], F32)
        hTv = k.hT.h.rearrange("(c p) t -> p c t", p=128)
        n = 0
        P.dma("sp", hts[0][:, :, 0:TB[0][1]], V(hTv[:, :, TB[0][0]:TB[0][0] + TB[0][1]], None))
        for bi, (t0, tn) in enumerate(TB):
            ht = hts[bi % 2]
            if bi + 1 < len(TB):
                t0n, tnn = TB[bi + 1]
                P.dma("sp", hts[(bi + 1) % 2][:, :, 0:tnn], V(hTv[:, :, t0n:t0n + tnn], None))
            for (c0, ncol, mode, dst, doff) in groups:
                if mode == "fm":
                    for cc in range(0, ncol, 128):
                        ps = pss[n % 8]
                        for c in range(8):
                            P.mm(ps[:, 0:tn], W[:, c, c0 + cc:c0 + cc + 128], ht[:, c, 0:tn], start=(c == 0), stop=(c == 7))
                        o = osb[n % 8]
                        P.act(o[:, 0:tn], ps[:, 0:tn], AF.Copy)
                        P.dma("sp", dst[doff + cc:doff + cc + 128, t0:t0 + tn], o[:, 0:tn])
                        n += 1
                else:
                    for ts_ in range(0, tn, 128):
                        for cc in range(0, ncol, 512):
                            w_ = min(512, ncol - cc)
                            ps = pss[n % 8]
                            for c in range(8):
                                P.mm(ps[:, 0:w_], ht[:, c, ts_:ts_ + 128], W[:, c, c0 + cc:c0 + cc + w_], start=(c == 0), stop=(c == 7))
                            o = osb[n % 8]
                            P.act(o[:, 0:w_], ps[:, 0:w_], AF.Copy)
                            P.dma("sp", dst[t0 + ts_:t0 + ts_ + 128, doff + cc:doff + cc + w_], o[:, 0:w_])
                            n += 1
        P.flush()


def phase_init(P, k):
    for i in range(8):
        P.dma("sp", k.xs[i * 1024:(i + 1) * 1024, :], k.x[i * 1024:(i + 1) * 1024, :])
    P.dma("sp", k.xs[L:T, :], k.ctx[:, :])
    P.flush()


def _rope_tables(d):
    quarter = d // 4
    t = np.arange(L)
    row = (t // 64).astype(np.float64)
    col = (t % 64).astype(np.float64)
    inv = 10000.0 ** (-np.arange(quarter, dtype=np.float64) / quarter)
    ang = np.stack([row[:, None] * inv[None, :], col[:, None] * inv[None, :]], axis=1)
    return (np.cos(ang).reshape(L, 2 * quarter).astype(np.float32),
            np.sin(ang).reshape(L, 2 * quarter).astype(np.float32))


def _make_consts():
    c = {}
    c["rc128"], c["rs128"] = _rope_tables(128)
    c["rc64"], c["rs64"] = _rope_tables(64)
    p = np.arange(128)[:, None]
    f = np.arange(128)[None, :]
    c["mlo"] = (p >= f).astype(np.float32)
    c["mhi"] = (p <= f).astype(np.float32)
    return c


CONSTS = _make_consts()


NT = T // 128
NTL = L // 128


def bcast(v, shape, axis):
    return V(v.ap.unsqueeze(axis).to_broadcast(list(shape)), v.key)


def rope_apply(P, out, xin, Ct, St, nh, q, tmp):
    xv = lambda b, half: V(b.h[:, :, :].rearrange("p h (b f j) -> p h b f j", b=2, f=2)[:, :, :, half, :], b.key)
    Cb = V(Ct.h[:, :, :].unsqueeze(1).to_broadcast([128, nh, 2, q]), Ct.key)
    Sb = V(St.h[:, :, :].unsqueeze(1).to_broadcast([128, nh, 2, q]), St.key)
    a, b_ = xv(xin, 0), xv(xin, 1)
    t1, t2, t3, t4 = [t[:, :, :, :] for t in tmp]
    P.tt(t1, a, Cb, ALU.mult)
    P.tt(t2, b_, Sb, ALU.mult)
    P.tt(t3, a, Sb, ALU.mult)
    P.tt(t4, b_, Cb, ALU.mult)
    P.tt(xv(out, 0), t1, t2, ALU.subtract)
    P.tt(xv(out, 1), t3, t4, ALU.add)


def phase_ga_prep(P, k, l):
    with ExitStack() as es:
        idf, idb = load_ident(P, es, k)
        gq = P.sb(es, "gq", [128, 6, 128], F32)
        for h in range(4):
            P.dma("sp", gq[:, h, :], V(k.ga_q_g.h[l:l + 1, :].to_broadcast([128, 128]), None))
        for h in range(2):
            P.dma("sp", gq[:, 4 + h, :], V(k.ga_k_g.h[l:l + 1, :].to_broadcast([128, 128]), None))
        P.ts(gq[:, 0:4, :], gq[:, 0:4, :], 128 ** -0.5, None, ALU.mult)
        xt = P.sbpool(es, "gx", 4, [128, 1024], F32)
        sq2 = P.sbpool(es, "gsq", 2, [128, 6, 128], F32)
        ssq = P.sbpool(es, "gss", 2, [128, 6], F32)
        rstd = P.sbpool(es, "grs", 2, [128, 6], F32)
        xn = P.sbpool(es, "gxn", 2, [128, 6, 128], F32)
        xr = P.sbpool(es, "gxr", 2, [128, 6, 128], BF16)
        vb = P.sbpool(es, "gvb", 2, [128, 256], BF16)
        Ct = P.sbpool(es, "gC", 4, [128, 2, 32], F32)
        St = P.sbpool(es, "gS", 4, [128, 2, 32], F32)
        tmp2 = [[P.sb(es, f"gt{j}_{i}", [128, 6, 2, 32], F32) for i in range(4)] for j in range(2)]
        pst = [P.ps(es, f"gpt{i}", [128, 6, 128], BF16) for i in range(2)]
        qkT = P.sbpool(es, "gqkT", 2, [128, 6, 128], BF16)
        def load(ti):
            t0 = ti * 128
            P.dma("sp", xt[ti % 4][:, :], k.ga[t0:t0 + 128, :])
            if ti < NTL:
                P.dma("sp", Ct[ti % 4][:, :, :], V(k.rc128.h[t0:t0 + 128, :].rearrange("p (b j) -> p b j", b=2), None))
                P.dma("sp", St[ti % 4][:, :, :], V(k.rs128.h[t0:t0 + 128, :].rearrange("p (b j) -> p b j", b=2), None))

        def body(ti):
            t0 = ti * 128
            x_ = xt[ti % 4]
            qk = V(x_.h[:, 0:768].rearrange("p (h d) -> p h d", d=128), x_.key)
            sq_ = sq2[ti % 2]
            P.tt(sq_[:, :, :], qk, qk, ALU.mult)
            yield
            ss, rs = ssq[ti % 2], rstd[ti % 2]
            P.reduce(ss[:, :], sq_[:, :, :])
            yield
            P.act(rs[:, :], ss[:, :], AF.Sqrt, bias=EPSB[0][:, :], scale=1.0 / 128)
            yield
            P.recip(rs[:, :], rs[:, :])
            yield
            n_ = xn[ti % 2]
            P.tt(n_[:, :, :], qk, bcast(rs[:, :], [128, 6, 128], 2), ALU.mult)
            yield
            P.tt(n_[:, :, :], n_[:, :, :], gq[:, :, :], ALU.mult)
            yield
            r_ = xr[ti % 2]
            if ti < NTL:
                rope_apply(P, r_, n_, Ct[ti % 4], St[ti % 4], 6, 32, tmp2[ti % 2])
            else:
                P.copy(r_[:, :, :], n_[:, :, :])
            yield
            ps = pst[ti % 2]
            for h in range(6):
                P.transpose(ps[:, h, :], r_[:, h, :], idb[:, :])
            yield
            o = qkT[ti % 2]
            P.act(o[:, :, :], ps[:, :, :], AF.Copy)
            P.dma("sp", V(k.gaQT.h[:, :, t0:t0 + 128].rearrange("h d t -> d h t"), None), o[:, 0:4, :])
            P.dma("sp", V(k.gaKT.h[:, :, t0:t0 + 128].rearrange("h d t -> d h t"), None), o[:, 4:6, :])
            v_ = vb[ti % 2]
            P.act(v_[:, :], x_[:, 768:1024], AF.Copy)
            P.dma("sp", k.gaV[t0:t0 + 128, :], v_[:, :])

        prefetch_pipelined(NT, load, body, 2, 2)
        P.flush()


def attn_core(P, es, QTsrc, KT, Vaug, dh, qblocks, kblocks_fn, nsub_heads, finalize, psS, psO, scale_mask=None):
    raise NotImplementedError


def emit_skewed(units, skew):
    n = len(units)
    for i in range(n + skew):
        if i < n:
            units[i][0]()
        if i >= skew:
            units[i - skew][1]()


def phase_ga_attn(P, k, l, with_ctx_out):
    SK = 2
    with ExitStack() as es:
        KTs = P.sbpool(es, "aKT", 2, [128, T], BF16)
        Vts = P.sbpool(es, "aVt", 2, [128, NT, 128], BF16)
        ones = P.sb(es, "aones", [128, 128], F32)
        P.memset(ones[:, :], 1.0)
        QT = P.sbpool(es, "aQT", 3, [128, 512], BF16)
        PT = P.sbpool(es, "aPT", SK + 3, [128, 512], BF16)
        NACC = 3
        pairs = P.sbpool(es, "apair", 3, [128, 512], BF16)
        pprev = [None]
        accs = [P.sbpool(es, f"aacc{i}", NACC, [128, 512], F32) for i in range(2)]
        psS = [P.ps(es, f"apS{i}") for i in range(SK + 1)]
        psO = [P.ps(es, f"apO{i}") for i in range(2)]
        psD = P.ps(es, "apD")
        rec = P.sbpool(es, "arec", 2, [128, 512], F32)
        ob = P.sbpool(es, "aob", 2, [128, 512], BF16)
        units = []
        qloads = []
        nq = 0
        nu = 0
        for g in range(2):
            KT, Vt = KTs[g], Vts[g]
            P.dma("sp", KT[:, :], k.gaKT[g, :, :])
            P.dma("sp", Vt[:, :, :], V(k.gaV.h[:, g * 128:(g + 1) * 128].rearrange("(n p) c -> p n c", p=128), None))
            for hh in range(2):
                h = g * 2 + hh
                qbl = [(i * 512, 512, list(range(NT))) for i in range(L // 512)]
                if with_ctx_out:
                    qbl.append((L, LC, list(range(NTL, NT))))
                for (q0, qn, kbs) in qbl:
                    q_ = QT[nq % 3]
                    qloads.append(lambda q_=q_, qn=qn, h=h, q0=q0: P.dma("sp", q_[:, 0:qn], k.gaQT[h, :, q0:q0 + qn]))
                    pO = psO[nq % 2]
                    ac = accs[nq % 2]
                    r_, o_ = rec[nq % 2], ob[nq % 2]
                    nkb = len(kbs)
                    for ki, kb in enumerate(kbs):
                        pS = psS[nu % (SK + 1)]
                        p_ = PT[nu % (SK + 3)]
                        nu += 1

                        def front(ki=ki, kb=kb, pS=pS, p_=p_, q_=q_, q0=q0, qn=qn, h=h, KT=KT, nq=nq):
                            if ki == 0:
                                if nq == 0:
                                    qloads[0]()
                                if nq + 1 < len(qloads):
                                    qloads[nq + 1]()
                            P.mm(pS[:, 0:qn], KT[:, kb * 128:(kb + 1) * 128], q_[:, 0:qn])
                            P.act(p_[:, 0:qn], pS[:, 0:qn], AF.Exp)

                        def back(ki=ki, kb=kb, p_=p_, qn=qn, pO=pO, ac=ac, nkb=nkb, Vt=Vt, r_=r_, o_=o_, h=h, q0=q0):
                            P.mm(pO[:, 0:qn], Vt[:, kb, :], p_[:, 0:qn], start=(ki == 0), stop=(ki == nkb - 1))
                            if ki % 2 == 1:
                                pr = pairs[(ki // 2) % 3]
                                P.tt(pr[:, 0:qn], pprev[0][:, 0:qn], p_[:, 0:qn], ALU.add)
                                a_ = ac[(ki // 2) % NACC]
                                if ki // 2 < NACC:
                                    P.copy(a_[:, 0:qn], pr[:, 0:qn])
                                else:
                                    P.tt(a_[:, 0:qn], a_[:, 0:qn], pr[:, 0:qn], ALU.add)
                            pprev[0] = p_
                            if ki == nkb - 1:
                                na = min(NACC, nkb // 2)
                                for i in range(na):
                                    P.mm(psD[:, 0:qn], ones[:, :], ac[i][:, 0:qn], start=(i == 0), stop=(i == na - 1))
                                P.recip(r_[:, 0:qn], psD[:, 0:qn])
                                P.tt(o_[:, 0:qn], pO[:, 0:qn], r_[:, 0:qn], ALU.mult)
                                P.dma("sp", k.yT[1, h * 128:(h + 1) * 128, q0:q0 + qn], o_[:, 0:qn])

                        units.append((front, back))
                    nq += 1
        emit_skewed(units, SK)
        P.flush()


def phase_wa_prep(P, k, l):
    with ExitStack() as es:
        idf, idb = load_ident(P, es, k)
        xt = P.sbpool(es, "wx", 4, [128, 768], F32)
        xs_ = P.sbpool(es, "wxs", 2, [128, 10, 64], F32)
        xr = P.sbpool(es, "wxr", 2, [128, 10, 64], BF16)
        vb = P.sbpool(es, "wvb", 2, [128, 128], BF16)
        Ct = P.sbpool(es, "wC", 4, [128, 2, 16], F32)
        St = P.sbpool(es, "wS", 4, [128, 2, 16], F32)
        tmp2 = [[P.sb(es, f"wt{j}_{i}", [128, 10, 2, 16], F32) for i in range(4)] for j in range(2)]
        pst = [P.ps(es, f"wpt{i}", [128, 5, 128], BF16) for i in range(2)]
        qkT = P.sbpool(es, "wqkT", 2, [128, 5, 128], BF16)
        def load(ti):
            t0 = ti * 128
            P.dma("sp", xt[ti % 4][:, :], k.wa[t0:t0 + 128, :])
            if ti < NTL:
                P.dma("sp", Ct[ti % 4][:, :, :], V(k.rc64.h[t0:t0 + 128, :].rearrange("p (b j) -> p b j", b=2), None))
                P.dma("sp", St[ti % 4][:, :, :], V(k.rs64.h[t0:t0 + 128, :].rearrange("p (b j) -> p b j", b=2), None))

        def body(ti):
            t0 = ti * 128
            x_ = xt[ti % 4]
            qk = V(x_.h[:, 0:640].rearrange("p (h d) -> p h d", d=64), x_.key)
            s_ = xs_[ti % 2]
            P.ts(V(s_.h[:, 0:8, :].rearrange("p (j g) d -> p g j d", g=2), s_.key),
                 V(x_.h[:, 0:512].rearrange("p (g j d) -> p g j d", g=2, j=4), x_.key), 64 ** -0.5, None, ALU.mult)
            P.act(s_[:, 8:10, :], V(qk.ap[:, 8:10, :], x_.key), AF.Copy)
            yield
            r_ = xr[ti % 2]
            if ti < NTL:
                rope_apply(P, r_, s_, Ct[ti % 4], St[ti % 4], 10, 16, tmp2[ti % 2])
            else:
                P.copy(r_[:, :, :], s_[:, :, :])
            yield
            ps = pst[ti % 2]
            for j in range(4):
                src = V(r_.h[:, 2 * j:2 * j + 2, :].rearrange("p h d -> p (h d)"), r_.key)
                P.transpose(ps[:, j, :], src, idb[:, :])
            P.transpose(ps[:, 4, :], V(r_.h[:, 8:10, :].rearrange("p h d -> p (h d)"), r_.key), idb[:, :])
            yield
            o = qkT[ti % 2]
            P.act(o[:, :, :], ps[:, :, :], AF.Copy)
            P.dma("sp", k.waQT[:, :, t0:t0 + 128], o[:, 0:4, :])
            P.dma("sp", k.waKT[:, t0:t0 + 128], o[:, 4, :])
            v_ = vb[ti % 2]
            P.act(v_[:, :], x_[:, 640:768], AF.Copy)
            P.dma("sp", k.waV[t0:t0 + 128, :], v_[:, :])

        prefetch_pipelined(NT, load, body, 2, 2)
        P.flush()


def phase_wa_attn(P, k, l, with_ctx_out):
    SK = 2
    with ExitStack() as es:
        idf, idb = load_ident(P, es, k)
        KT = P.sb(es, "bKT", [128, T], BF16)
        Va = P.sb(es, "bVa", [128, NT, 2, 65], BF16)
        P.dma("sp", KT[:, :], k.waKT[:, :])
        for g in range(2):
            P.dma("sp", Va[:, :, g, 0:64], V(k.waV.h[:, g * 64:(g + 1) * 64].rearrange("(n p) c -> p n c", p=128), None))
        P.memset(Va[:, :, :, 64:65], 1.0)
        mf = P.sb(es, "bmf", [128, 2, 128], F32)
        mb = P.sb(es, "bmb", [128, 2, 128], BF16)
        P.dma("sp", mf[:, 0, :], k.mlo[:, :])
        P.dma("sp", mf[:, 1, :], k.mhi[:, :])
        P.copy(mb[:, :, :], mf[:, :, :])
        esk = P.sb(es, "besk", [128, 8], F32)
        P.dma("sp", esk[:, :], V(k.wa_sink.h[l:l + 1, :].to_broadcast([128, 8]), None))
        P.act(esk[:, :], esk[:, :], AF.Exp)
        QT = P.sbpool(es, "bQT", 3, [128, 4, 128], BF16)
        PT = P.sbpool(es, "bPT", SK + 3, [128, 4, 128], BF16)
        psS = [P.ps(es, f"bpS{i}") for i in range(SK + 1)]
        psO = [P.ps(es, f"bpO{i}") for i in range(4)]
        psT = P.ps(es, "bpT", [128, 4, 128], BF16)
        den = P.sbpool(es, "bden", 8, [128, 1], F32)
        ob = P.sbpool(es, "bob", 2, [128, 8, 64], BF16)
        oT = P.sbpool(es, "boT", 2, [128, 4, 128], BF16)
        qtiles = list(range(NTL)) + (list(range(NTL, NT)) if with_ctx_out else [])
        units = []
        nu = 0
        for qi, ti in enumerate(qtiles):
            t0 = ti * 128
            q_ = QT[qi % 3]
            if ti < NTL:
                kbs = [(ti - 1, 0)] if ti > 0 else []
                kbs.append((ti, None))
                if ti < NTL - 1:
                    kbs.append((ti + 1, 1))
                kbs += [(NTL, None), (NTL + 1, None)]
            else:
                kbs = [(NTL, None), (NTL + 1, None)]
            o_ = ob[qi % 2]
            ot = oT[qi % 2]
            nkb = len(kbs)
            for g in range(2):
                for ki, (kb, mk) in enumerate(kbs):
                    pS = psS[nu % (SK + 1)]
                    p_ = PT[nu % (SK + 3)]
                    nu += 1

                    def front(g=g, ki=ki, kb=kb, mk=mk, pS=pS, p_=p_, q_=q_, t0=t0, qi=qi):
                        if g == 0 and ki == 0:
                            if qi == 0:
                                P.dma("sp", QT[0][:, :, :], k.waQT[:, :, qtiles[0] * 128:qtiles[0] * 128 + 128])
                            if qi + 1 < len(qtiles):
                                tn_ = qtiles[qi + 1] * 128
                                P.dma("sp", QT[(qi + 1) % 3][:, :, :], k.waQT[:, :, tn_:tn_ + 128])
                        P.mm(pS[:, :], KT[g * 64:(g + 1) * 64, kb * 128:(kb + 1) * 128],
                             V(q_.h[g * 64:(g + 1) * 64, :, :].rearrange("p j q -> p (j q)"), q_.key))
                        pv = V(p_.h[:, :, :].rearrange("p j q -> p (j q)"), p_.key)
                        P.act(pv, pS[:, :], AF.Exp)
                        if mk is not None:
                            P.tt(p_[:, :, :], p_[:, :, :], bcast(mb[:, mk, :], [128, 4, 128], 1), ALU.mult)

                    def back(g=g, ki=ki, kb=kb, p_=p_, nkb=nkb, o_=o_, ot=ot, t0=t0):
                        for j in range(4):
                            P.mm(psO[j][:, 0:65], p_[:, j, :], Va[:, kb, g, :], start=(ki == 0), stop=(ki == nkb - 1))
                        if ki == nkb - 1:
                            for j in range(4):
                                h = g * 4 + j
                                d_ = den[h]
                                P.tt(d_[:, :], psO[j][:, 64:65], esk[:, h:h + 1], ALU.add)
                                P.recip(d_[:, :], d_[:, :])
                                P.ts(o_[:, h, :], psO[j][:, 0:64], d_[:, :], None, ALU.mult)
                            if g == 1:
                                for c in range(4):
                                    P.transpose(psT[:, c, :], V(o_.h[:, 2 * c:2 * c + 2, :].rearrange("p h d -> p (h d)"), o_.key), idb[:, :])
                                P.act(ot[:, :, :], psT[:, :, :], AF.Copy)
                                P.dma("sp", V(k.yT.h[3, :, t0:t0 + 128].rearrange("(c p) t -> p c t", p=128), None), ot[:, :, :])

                    units.append((front, back))
        emit_skewed(units, SK)
        P.flush()

SEGS = [(0, L), (L, T)]


def conv3_fm(P, u, z, w, b, segs):
    P.act(u[:, 0:T], z[:, 0:T], AF.Identity, bias=b[:, 0:1], scale=w[:, 1:2])
    for (s, e) in segs:
        P.stt(u[:, s + 1:e], z[:, s:e - 1], w[:, 0:1], u[:, s + 1:e], ALU.mult, ALU.add)
        P.stt(u[:, s:e - 1], z[:, s + 1:e], w[:, 2:3], u[:, s:e - 1], ALU.mult, ALU.add)


def conv3_fm_gen(P, u, z, w, b, segs):
    P.act(u[:, 0:T], z[:, 0:T], AF.Identity, bias=b[:, 0:1], scale=w[:, 1:2])
    yield
    for (s, e) in segs:
        P.stt(u[:, s + 1:e], z[:, s:e - 1], w[:, 0:1], u[:, s + 1:e], ALU.mult, ALU.add)
        yield
        P.stt(u[:, s:e - 1], z[:, s + 1:e], w[:, 2:3], u[:, s:e - 1], ALU.mult, ALU.add)
        yield


def declare_ml(nc, k):
    inp = lambda n, s: dram(nc, n, s, F32, "ExternalInput")
    k.ml_conv_w = inp("ml_conv_w", [DEPTH, 3, 1024])
    k.ml_conv_b = inp("ml_conv_b", [DEPTH, 1024])
    k.ml_gate_b = inp("ml_gate_b", [DEPTH, 16])
    k.ml_norm_g = inp("ml_norm_g", [DEPTH, 512])
    k.mlQKT = dram(nc, "mlQKT", [1024, T], BF16)
    k.mlKtm = dram(nc, "mlKtm", [T, 512], BF16)
    k.mlH = dram(nc, "mlH", [2, T, 512])
    k.mlVa = dram(nc, "mlVa", [T, 4, 129], BF16)


def phase_ml_prep(P, k, l):
    with ExitStack() as es:
        idf, idb = load_ident(P, es, k)
        zt = P.sbpool(es, "mz", 2, [128, T], F32)
        ut = P.sbpool(es, "mu", 2, [128, T], F32)
        ab = P.sbpool(es, "ma", 2, [128, T], BF16)
        wt = P.sbpool(es, "mw", 2, [128, 3], F32)
        bt = P.sbpool(es, "mb", 2, [128, 1], F32)
        pst = [P.ps(es, f"mpt{i}", [128, 4, 128], BF16) for i in range(2)]
        kt = P.sbpool(es, "mkt", 3, [128, 4, 128], BF16)
        npt = [0]

        def load(c):
            P.dma("sp", zt[c % 2][:, :], k.mlqkT[c * 128:(c + 1) * 128, :])
            P.dma("sp", wt[c % 2][:, :], V(k.ml_conv_w.h[l, :, c * 128:(c + 1) * 128].rearrange("w c -> c w"), None),
                  allow_slow_non_contiguous=True)
            P.dma("sp", bt[c % 2][:, :], V(k.ml_conv_b.h[l:l + 1, c * 128:(c + 1) * 128].rearrange("o c -> c o"), None),
                  allow_slow_non_contiguous=True)

        def body(c):
            z, u, a, w, b = zt[c % 2], ut[c % 2], ab[c % 2], wt[c % 2], bt[c % 2]
            for _ in conv3_fm_gen(P, u, z, w, b, SEGS):
                yield
            if c < 4:
                P.act(a[:, :], u[:, :], AF.Silu)
            else:
                P.act(u[:, :], u[:, :], AF.Silu)
                yield
                P.act(a[:, :], u[:, :], AF.Copy, scale=128 ** -0.5)
            P.dma("sp", k.mlQKT[c * 128:(c + 1) * 128, :], a[:, :])
            yield
            if c >= 4:
                hk = c - 4
                for n0 in range(0, NT, 4):
                    nn = min(4, NT - n0)
                    ps = pst[npt[0] % 2]
                    for j in range(nn):
                        P.transpose(ps[:, j, :], a[:, (n0 + j) * 128:(n0 + j + 1) * 128], idb[:, :])
                    o = kt[npt[0] % 3]
                    P.copy(o[:, 0:nn, :], ps[:, 0:nn, :])
                    P.dma("sp", V(k.mlKtm.h[n0 * 128:(n0 + nn) * 128, hk * 128:(hk + 1) * 128].rearrange("(n p) d -> p n d", p=128), None),
                          o[:, 0:nn, :])
                    npt[0] += 1
                    if n0 % 16 == 12:
                        yield

        prefetch_pipelined(8, load, body, 1, 1)
        vf = P.sbpool(es, "mvf", 3, [128, 512], F32)
        va = P.sbpool(es, "mva", 3, [128, 4, 129], BF16)
        for b_ in va:
            P.memset(b_[:, :, 128:129], 1.0)

        def vload(ti):
            P.dma("sp", vf[ti % 3][:, :], k.mlvo[ti * 128:(ti + 1) * 128, 0:512])

        def vbody(ti):
            P.act(va[ti % 3][:, :, 0:128], V(vf[ti % 3].h[:, :].rearrange("p (h c) -> p h c", h=4), vf[ti % 3].key), AF.Copy)
            P.dma("sp", k.mlVa[ti * 128:(ti + 1) * 128, :, :], va[ti % 3][:, :, :])

        prefetch_loop(NT, vload, vbody, 2)
        P.flush()


def phase_ml_scan(P, k, l):
    with ExitStack() as es:
        G = P.sb(es, "sG", [128, NT, 16], F32)
        gb = P.sb(es, "sgb", [128, 16], F32)
        P.dma("sp", G[:, :, :], V(k.mlvo.h[:, 1024:1040].rearrange("(n p) c -> p n c", p=128), None))
        P.dma("sp", gb[:, :], V(k.ml_gate_b.h[l:l + 1, :].to_broadcast([128, 16]), None))
        P.tt(G[:, :, :], G[:, :, :], bcast(gb[:, :], [128, NT, 16], 1), ALU.add)
        Gv = lambda d, kind: V(G.h[:, :, :].rearrange("p n (d k h) -> p d k n h", d=2, k=2)[:, d, kind, :, :], G.key)
        I_ = P.sb(es, "sI", [128, 2, NT, 4], F32)
        LF = P.sb(es, "sLF", [128, 2, NT, 4], F32)
        for d in range(2):
            P.copy(I_[:, d, :, :], Gv(d, 0))
            P.act(LF[:, d, :, :], Gv(d, 1), AF.Exp, scale=-1.0)
        P.ts(LF[:, :, :, :], LF[:, :, :, :], 1.0, None, ALU.add)
        P.act(LF[:, :, :, :], LF[:, :, :, :], AF.Ln)
        P.ts(LF[:, :, :, :], LF[:, :, :, :], -1.0, None, ALU.mult)
        tri = P.sb(es, "stri", [128, 3, 128], F32)
        P.dma("sp", tri[:, 0, :], k.mhi[:, :])
        P.dma("sp", tri[:, 1, :], k.mlo[:, :])
        P.memset(tri[:, 2, :], 1.0)
        trib = P.sb(es, "strib", [128, 2, 128], BF16)
        P.copy(trib[:, :, :], tri[:, 0:2, :])
        NC4 = NT * 4
        Bc = P.sb(es, "sB", [128, 2, NT, 4], F32)
        Be = P.sb(es, "sBe", [128, 2, NT, 4], F32)
        es_g = ExitStack()
        psg = [P.ps(es_g, f"spg{i}") for i in range(2)]
        for d in range(2):
            lfv = V(LF.h[:, d, :, :].rearrange("p n h -> p (n h)"), LF.key)
            P.mm(psg[0][:, 0:NC4], tri[:, d, :], lfv)
            P.copy(V(Bc.h[:, d, :, :].rearrange("p n h -> p (n h)"), Bc.key), psg[0][:, 0:NC4])
            P.mm(psg[1][:, 0:NC4], tri[:, 2, :], lfv)
            P.copy(V(Be.h[:, d, :, :].rearrange("p n h -> p (n h)"), Be.key), psg[1][:, 0:NC4])
        U_ = P.sb(es, "sU", [128, 2, NT, 4], F32)
        EB = P.sb(es, "sEB", [128, 2, NT, 4], F32)
        W_ = P.sb(es, "sW", [128, 2, NT, 4], F32)
        EE = P.sb(es, "sEE", [128, 2, NT, 4], F32)
        P.tt(U_[:, :, :, :], I_[:, :, :, :], Bc[:, :, :, :], ALU.subtract)
        P.tt(W_[:, :, :, :], U_[:, :, :, :], Be[:, :, :, :], ALU.add)
        P.act(U_[:, :, :, :], U_[:, :, :, :], AF.Exp)
        P.act(W_[:, :, :, :], W_[:, :, :, :], AF.Exp)
        P.act(EB[:, :, :, :], Bc[:, :, :, :], AF.Exp)
        P.act(EE[:, :, :, :], Be[:, :, :, :], AF.Exp)
        P.flush("phase_ml_gates")
        es_g.close()
        NB = 3
        pool2 = lambda nm, shp, dt: [[P.sb(es, f"{nm}{c}_{i}", shp, dt) for i in range(NB)] for c in range(2)]
        QKc = pool2("sQK", [128, 2, 4, 128], BF16)
        Kmc = pool2("sKm", [128, 512], BF16)
        Vac = pool2("sVa", [128, 4, 129], BF16)
        C32 = [P.sb(es, f"sC{i}", [128, 129], F32) for i in range(8)]
        Cbf = [P.sb(es, f"sCb{i}", [128, 129], BF16) for i in range(8)]
        Sm = P.sbpool(es, "sSm", 8, [128, 128], BF16)
        Vw = P.sbpool(es, "sVw", 8, [128, 129], BF16)
        dn = P.sbpool(es, "sdn", 8, [128, 1], F32)
        ho = P.sbpool(es, "sho", 4, [128, 4, 128], F32)
        bS = [P.ps(es, f"spS{i}", [128, 4, 128], F32) for i in range(2)]
        bO = [P.ps(es, f"spO{i}", [128, 3, 129], F32) for i in range(3)]
        bU = [P.ps(es, f"spU{i}", [128, 3, 129], F32) for i in range(3)]
        pSv = lambda ci, par: bS[ci // 4][:, ci % 4, :]
        pOv = lambda ci, a=0, b=129: bO[ci // 3][:, ci % 3, a:b]
        pUv = lambda ci: bU[ci // 3][:, ci % 3, :]

        def excl(op, v):
            op.writes.append(v.key)
        order = [[NTL, NTL + 1] + list(range(NTL)), [NTL + 1, NTL] + list(range(NTL - 1, -1, -1))]
        onec = P.sb(es, "sonec", [128, 1], F32)
        P.memset(onec[:, :], 1.0)

        def loads(j):
            for d in range(2):
                n = order[d][j]
                tsl = slice(n * 128, (n + 1) * 128)
                qkv = k.mlQKT.h[:, tsl].rearrange("(w hh d) t -> d w hh t", w=2, hh=4)
                for w in range(2):
                    P.dma("sp", QKc[d][j % NB][:, w, :, :], V(qkv[:, w, :, :], None))
                P.dma("sp", Kmc[d][j % NB][:, :], k.mlKtm[tsl, :])
                P.dma("sp", Vac[d][j % NB][:, :, :], k.mlVa[tsl, :, :])

        def round_(j):
            ch = [(h, d, h * 2 + d, order[d][j]) for h in range(4) for d in range(2)]
            par = j % 2
            for (h, d, ci, n) in ch:
                qk_ = QKc[d][j % NB]
                P.mm(pSv(ci, par), qk_[:, 1, h, :], qk_[:, 0, h, :])
            for (h, d, ci, n) in ch:
                excl(P.stt(Sm[ci][:, :], pSv(ci, par), U_[:, d, n, h:h + 1], trib[:, d, :], ALU.mult, ALU.mult), pSv(ci, par))
                P.act(Vw[ci][:, :], Vac[d][j % NB][:, h, :], AF.Identity, scale=W_[:, d, n, h:h + 1])
            for (h, d, ci, n) in ch:
                qk_, km_, vv_ = QKc[d][j % NB], Kmc[d][j % NB], Vac[d][j % NB]
                P.mm(pOv(ci), Sm[ci][:, :], vv_[:, h, :], start=True, stop=(j == 0))
                if j > 0:
                    P.mm(pOv(ci), qk_[:, 0, h, :], Cbf[ci][:, :], start=False, stop=True)
                P.mm(pUv(ci), km_[:, h * 128:(h + 1) * 128], Vw[ci][:, :])
            for (h, d, ci, n) in ch:
                excl(P.tt(dn[ci][:, :], pOv(ci, 128, 129), EB[:, d, n, h:h + 1], ALU.mult), pOv(ci))
            for (h, d, ci, n) in ch:
                P.act(dn[ci][:, :], dn[ci][:, :], AF.Abs)
            for (h, d, ci, n) in ch:
                P.ts(dn[ci][:, :], dn[ci][:, :], 1.0, None, ALU.max)
            for (h, d, ci, n) in ch:
                P.recip(dn[ci][:, :], dn[ci][:, :])
            for (h, d, ci, n) in ch:
                P.tt(dn[ci][:, :], dn[ci][:, :], EB[:, d, n, h:h + 1], ALU.mult)
            for (h, d, ci, n) in ch:
                ho_ = ho[(2 * j + d) % 4]
                excl(P.act(ho_[:, h, :], pOv(ci, 0, 128), AF.Identity, scale=dn[ci][:, :]), pOv(ci))
            for d in range(2):
                n = order[d][j]
                ho_ = ho[(2 * j + d) % 4]
                P.dma("act", k.mlH[d, n * 128:(n + 1) * 128, :], V(ho_.h[:, :, :].rearrange("p h e -> p (h e)"), ho_.key))
            for (h, d, ci, n) in ch:
                if j == 0:
                    excl(P.copy(C32[ci][:, :], pUv(ci)), pUv(ci))
                else:
                    excl(P.stt(C32[ci][:, :], C32[ci][:, :], EE[:, d, n, h:h + 1], pUv(ci), ALU.mult, ALU.add), pUv(ci))
            for (h, d, ci, n) in ch:
                P.act(Cbf[ci][:, :], C32[ci][:, :], AF.Copy)

        prefetch_loop(NT, loads, round_, 1)
        P.flush()


def phase_ml_finish(P, k, l, with_ctx_out):
    with ExitStack() as es:
        idf, idb = load_ident(P, es, k)
        g = P.sb(es, "fg", [128, 4, 128], F32)
        P.dma("sp", V(g.h[:, :, :].rearrange("p h d -> p (h d)"), g.key), V(k.ml_norm_g.h[l:l + 1, :].to_broadcast([128, 512]), None))
        hf = P.sbpool(es, "fhf", 4, [128, 4, 128], F32)
        hb = P.sbpool(es, "fhb", 4, [128, 4, 128], F32)
        og = P.sbpool(es, "fog", 4, [128, 4, 128], F32)
        sq2 = P.sbpool(es, "fsq", 2, [128, 4, 128], F32)
        ss = P.sbpool(es, "fss", 2, [128, 4], F32)
        yb = P.sbpool(es, "fyb", 2, [128, 4, 128], BF16)
        psT = [P.ps(es, f"fpT{i}", [128, 4, 128], BF16) for i in range(2)]
        oT = P.sbpool(es, "foT", 2, [128, 4, 128], BF16)
        flat = lambda b: V(b.h[:, :, :].rearrange("p h d -> p (h d)"), b.key)
        tiles = list(range(NTL)) + (list(range(NTL, NT)) if with_ctx_out else [])

        def load(i):
            t0 = tiles[i] * 128
            P.dma("sp", flat(hf[i % 4]), k.mlH[0, t0:t0 + 128, :])
            P.dma("sp", flat(hb[i % 4]), k.mlH[1, t0:t0 + 128, :])
            P.dma("sp", flat(og[i % 4]), k.mlvo[t0:t0 + 128, 512:1024])

        def body(i):
            t0 = tiles[i] * 128
            a, b, o = hf[i % 4], hb[i % 4], og[i % 4]
            P.tt(a[:, :, :], a[:, :, :], b[:, :, :], ALU.add)
            P.act(o[:, :, :], o[:, :, :], AF.Sigmoid)
            yield
            sq_ = sq2[i % 2]
            P.tt(sq_[:, :, :], a[:, :, :], a[:, :, :], ALU.mult)
            yield
            s_ = ss[i % 2]
            P.reduce(s_[:, :], sq_[:, :, :])
            yield
            P.act(s_[:, :], s_[:, :], AF.Sqrt, bias=EPSB[0][:, :], scale=1.0 / 128)
            yield
            P.recip(s_[:, :], s_[:, :])
            yield
            P.tt(a[:, :, :], a[:, :, :], bcast(s_[:, :], [128, 4, 128], 2), ALU.mult)
            yield
            P.tt(a[:, :, :], a[:, :, :], g[:, :, :], ALU.mult)
            yield
            y_ = yb[i % 2]
            P.tt(y_[:, :, :], a[:, :, :], o[:, :, :], ALU.mult)
            yield
            ps = psT[i % 2]
            for c in range(4):
                P.transpose(ps[:, c, :], y_[:, c, :], idb[:, :])
            yield
            ot = oT[i % 2]
            P.act(ot[:, :, :], ps[:, :, :], AF.Copy)
            P.dma("sp", V(k.yT.h[2, :, t0:t0 + 128].rearrange("(c p) t -> p c t", p=128), None), ot[:, :, :])

        prefetch_pipelined(len(tiles), load, body, 2, 2)
        P.flush()

NFFT = 2 * L
TWO_PI = 2.0 * np.pi


def _hy_consts():
    c = {}
    for nm, Lf in (("lat", L), ("ctx", LC)):
        t = np.arange(Lf, dtype=np.float64)
        tn = t / (Lf - 1)
        w = TWO_PI * t / Lf
        bands = np.linspace(1e-4, 15.0, 16)
        ang = w[:, None] * bands[None, :]
        z = np.concatenate([tn[:, None], np.cos(ang), -np.sin(ang)], axis=-1)
        c["hz_" + nm] = np.ascontiguousarray(z.T).astype(np.float32)
        c["htn_" + nm] = tn[None, :].astype(np.float32)
    c["hntn_ctx"] = (-(np.arange(LC, dtype=np.float64) / (LC - 1)))[:, None].astype(np.float32)
    a = np.arange(128, dtype=np.float64)
    ang = TWO_PI * np.outer(a, a) / 128.0
    C2, S2 = np.cos(ang), np.sin(ang)
    c["hFA"] = np.concatenate([C2[:64], -S2[:64]], axis=1).astype(np.float32)
    c["hC2"] = C2.astype(np.float32)
    c["hS2"] = S2.astype(np.float32)
    c["hIAre"] = np.concatenate([C2, S2], axis=1).astype(np.float32)
    c["hIAim"] = np.concatenate([-S2, C2], axis=1).astype(np.float32)
    angt = TWO_PI * np.outer(a, a) / NFFT
    c["hTc"] = np.cos(angt).astype(np.float32)
    c["hTs"] = np.sin(angt).astype(np.float32)
    r = np.arange(512, dtype=np.float64)
    a5 = TWO_PI * np.outer(r, r) / 512.0
    c["hW5c"] = np.cos(a5).astype(np.float32)
    c["hW5s"] = np.sin(a5).astype(np.float32)
    return c


CONSTS.update(_hy_consts())
HY_IN = ["hy_conv_w", "hy_conv_b", "hy_pe_w1", "hy_pe_b1", "hy_freq", "hy_pe_w2", "hy_pe_b2", "hy_pe_w3", "hy_decay", "hy_skip"]


def declare_hy(nc, k):
    inp = lambda n, s: dram(nc, n, s, F32, "ExternalInput")
    k.hy_conv_w = inp("hy_conv_w", [DEPTH, 3, 1536])
    k.hy_conv_b = inp("hy_conv_b", [DEPTH, 1536])
    k.hy_pe_w1 = inp("hy_pe_w1", [DEPTH, 33, 64])
    k.hy_pe_b1 = inp("hy_pe_b1", [DEPTH, 64])
    k.hy_freq = inp("hy_freq", [DEPTH, 2, 64])
    k.hy_pe_w2 = inp("hy_pe_w2", [DEPTH, 64, 64])
    k.hy_pe_b2 = inp("hy_pe_b2", [DEPTH, 64])
    k.hy_pe_w3 = inp("hy_pe_w3", [DEPTH, 64, 2048])
    k.hy_decay = inp("hy_decay", [DEPTH, 2048])
    k.hy_skip = inp("hy_skip", [DEPTH, 2, 512])
    for n, v in _hy_consts().items():
        setattr(k, n, inp(n, list(v.shape)))
    k.hyU = dram(nc, "hyU", [1536, T])
    k.hyVb = dram(nc, "hyVb", [512, L], BF16)
    k.hyK = dram(nc, "hyK", [2048, L], BF16)
    k.hyNrm = dram(nc, "hyNrm", [2, 512])
    k.hySpec = dram(nc, "hySpec", [2, 2, 128, 512, 128], BF16)


def phase_hy_conv(P, k, l):
    with ExitStack() as es:
        zt = P.sbpool(es, "hz", 2, [128, T], F32)
        ut = P.sbpool(es, "hu", 2, [128, T], F32)
        vb = P.sbpool(es, "hvb", 2, [128, L], BF16)
        wt = P.sbpool(es, "hw", 2, [128, 3], F32)
        bt = P.sbpool(es, "hb", 2, [128, 1], F32)
        def load(c):
            P.dma("sp", zt[c % 2][:, :], k.hyT[c * 128:(c + 1) * 128, :])
            P.dma("sp", wt[c % 2][:, :], V(k.hy_conv_w.h[l, :, c * 128:(c + 1) * 128].rearrange("w c -> c w"), None),
                  allow_slow_non_contiguous=True)
            P.dma("sp", bt[c % 2][:, :], V(k.hy_conv_b.h[l:l + 1, c * 128:(c + 1) * 128].rearrange("o c -> c o"), None),
                  allow_slow_non_contiguous=True)

        def body(c):
            z, u, w, b = zt[c % 2], ut[c % 2], wt[c % 2], bt[c % 2]
            for _ in conv3_fm_gen(P, u, z, w, b, SEGS):
                yield
            P.dma("sp", k.hyU[c * 128:(c + 1) * 128, :], u[:, :])
            if c < 4:
                v_ = vb[c % 2]
                P.act(v_[:, :], u[:, 0:L], AF.Copy)
                P.dma("sp", k.hyVb[c * 128:(c + 1) * 128, :], v_[:, :])
            yield

        prefetch_pipelined(12, load, body, 1, 1)
        P.flush()


def _sin_layer(P, dst, ps, scl, bia, tmp, m, n):
    a = tmp[0]
    P.act(a[0:64, 0:n], ps[0:64, 0:n], AF.Identity, bias=bia, scale=scl)
    for _ in range(2):
        P.ts(m[0:64, 0:n], a[0:64, 0:n], float(np.pi), -TWO_PI, ALU.is_gt, ALU.mult)
        P.tt(a[0:64, 0:n], a[0:64, 0:n], m[0:64, 0:n], ALU.add)
        P.ts(m[0:64, 0:n], a[0:64, 0:n], -float(np.pi), TWO_PI, ALU.is_lt, ALU.mult)
        P.tt(a[0:64, 0:n], a[0:64, 0:n], m[0:64, 0:n], ALU.add)
    P.act(dst, a[0:64, 0:n], AF.Sin)


def _filter_mlp(P, es, k, l, zsrc, Lf, h2T):
    w1 = P.sb(es, "hw1", [33, 64], F32)
    w2 = P.sb(es, "hw2", [64, 64], F32)
    col = P.sb(es, "hcol", [64, 8], F32)
    P.dma("sp", w1[:, :], k.hy_pe_w1[l, :, :])
    P.dma("sp", w2[:, :], k.hy_pe_w2[l, :, :])
    cl = lambda src: V(src, None)
    P.dma("sp", col[:, 0:1], cl(k.hy_pe_b1.h[l:l + 1, :].rearrange("o c -> c o")), allow_slow_non_contiguous=True)
    P.dma("sp", col[:, 1:2], cl(k.hy_freq.h[l, 0:1, :].rearrange("o c -> c o")), allow_slow_non_contiguous=True)
    P.dma("sp", col[:, 2:3], cl(k.hy_pe_b2.h[l:l + 1, :].rearrange("o c -> c o")), allow_slow_non_contiguous=True)
    P.dma("sp", col[:, 3:4], cl(k.hy_freq.h[l, 1:2, :].rearrange("o c -> c o")), allow_slow_non_contiguous=True)
    P.tt(col[:, 4:5], col[:, 0:1], col[:, 1:2], ALU.mult)
    P.tt(col[:, 5:6], col[:, 2:3], col[:, 3:4], ALU.mult)
    zT = P.sbpool(es, "hzT", 2, [33, 512], F32)
    h1 = P.sbpool(es, "hh1", 2, [64, 512], F32)
    tmp = P.sbpool(es, "hta", 2, [64, 512], F32)
    m = P.sb(es, "htm", [64, 512], F32)
    ps = [P.ps(es, f"hpm{i}") for i in range(2)]
    nb = 0
    for t0 in range(0, Lf, 512):
        n = min(512, Lf - t0)
        z_ = zT[nb % 2]
        P.dma("sp", z_[:, 0:n], zsrc[:, t0:t0 + n])
        P.mm(ps[0][0:64, 0:n], w1[:, :], z_[:, 0:n])
        h1_ = h1[nb % 2]
        _sin_layer(P, h1_[:, 0:n], ps[0], col[:, 1:2], col[:, 4:5], [tmp[0]], m, n)
        P.mm(ps[1][0:64, 0:n], w2[:, :], h1_[:, 0:n])
        _sin_layer(P, h2T[0:64, t0:t0 + n], ps[1], col[:, 3:4], col[:, 5:6], [tmp[1]], m, n)
        nb += 1


def phase_hy_filt(P, k, l):
    with ExitStack() as es:
        h2T = P.sb(es, "hh2T", [64, L], F32)
        _filter_mlp(P, es, k, l, k.hz_lat, L, h2T)
        w3 = P.sb(es, "hw3", [64, 2048], F32)
        P.dma("sp", w3[:, :], k.hy_pe_w3[l, :, :])
        tnb = P.sb(es, "htnb", [128, L], F32)
        P.dma("sp", tnb[:, :], V(k.htn_lat.h[0:1, :].to_broadcast([128, L]), None))
        nad = P.sb(es, "hnad", [128, 16], F32)
        P.dma("sp", nad[:, :], V(k.hy_decay.h[l:l + 1, :].rearrange("o (g p) -> p (o g)", p=128), None),
              allow_slow_non_contiguous=True)
        P.act(nad[:, :], nad[:, :], AF.Abs)
        P.ts(nad[:, :], nad[:, :], -1.0, None, ALU.mult)
        ssq = P.sb(es, "hssq", [128, 16, 16], F32)
        sst = P.sb(es, "hsst", [128, 16], F32)
        E = P.sbpool(es, "hE", 4, [128, 512], F32)
        kf = P.sbpool(es, "hkf", 2, [128, 512], F32)
        kb = P.sbpool(es, "hkb", 2, [128, L], BF16)
        junk = P.sb(es, "hjk", [128, 512], F32)
        ps = [P.ps(es, f"hpf{i}") for i in range(3)]
        nb = 0
        w3b = P.sb(es, "hw3b", [64, 2048], BF16)
        h2b = P.sb(es, "hh2b", [64, L], BF16)
        P.copy(w3b[:, :], w3[:, :])
        for q in range(4):
            P.act(h2b[:, q * 2048:(q + 1) * 2048], h2T[0:64, q * 2048:(q + 1) * 2048], AF.Copy)
        units = []
        for g in range(16):
            is_bwd = (g // 4) % 2 == 1
            kb_ = kb[g % 2]
            for bi in range(L // 512):
                t0 = bi * 512
                p_ = ps[nb % 3]
                e_ = E[nb % 4]
                nb += 1

                def front(g=g, t0=t0, p_=p_, e_=e_):
                    P.mm(p_[:, :], w3b[:, g * 128:(g + 1) * 128], h2b[0:64, t0:t0 + 512])
                    P.act(e_[:, :], tnb[:, t0:t0 + 512], AF.Exp, scale=nad[:, g:g + 1])

                def back(g=g, bi=bi, t0=t0, p_=p_, e_=e_, kb_=kb_, is_bwd=is_bwd):
                    P.tt(kb_[:, t0:t0 + 512], p_[:, :], e_[:, :], ALU.mult)
                    if is_bwd and bi == 0:
                        P.memset(kb_[:, 0:1], 0.0)
                    P.act(junk[:, :], kb_[:, t0:t0 + 512], AF.Square, accum_out=ssq[:, g, bi:bi + 1])
                    if bi == L // 512 - 1:
                        P.dma("sp", k.hyK[g * 128:(g + 1) * 128, :], kb_[:, :])

                units.append((front, back))
        emit_skewed(units, 2)
        P.reduce(sst[:, :], ssq[:, :, :])
        rn = P.sb(es, "hrn", [128, 8], F32)
        for n in range(2):
            P.tt(rn[:, n * 4:(n + 1) * 4], sst[:, n * 8:n * 8 + 4], sst[:, n * 8 + 4:n * 8 + 8], ALU.add)
        P.act(rn[:, :], rn[:, :], AF.Sqrt, bias=EPSB[0][:, :], scale=1.0)
        P.recip(rn[:, :], rn[:, :])
        P.ts(rn[:, :], rn[:, :], 1.0 / NFFT, None, ALU.mult)
        P.dma("sp", V(k.hyNrm.h[:, :].rearrange("n (cb p) -> p n cb", p=128), None),
              V(rn.h[:, :].rearrange("p (n cb) -> p n cb", n=2), rn.key), allow_slow_non_contiguous=True)
        P.flush()


SBQ = 4
GQ = 8
NH = GQ // SBQ


class HyTab:
    pass


def hy_tables(P, es, k):
    tb = HyTab()
    st = P.sb(es, "htst", [128, 256], F32)

    def ld(name, src, rows, cols):
        b = P.sb(es, name, [128, cols], BF16)
        P.dma("sp", st[0:rows, 0:cols], src[:, :])
        P.copy(b[0:rows, :], st[0:rows, 0:cols])
        return b
    tb.FA = ld("hFAb", k.hFA, 64, 256)
    tb.C2 = ld("hC2b", k.hC2, 128, 128)
    tb.S2 = ld("hS2b", k.hS2, 128, 128)
    tb.IAre = ld("hIAreb", k.hIAre, 128, 256)
    tb.IAim = ld("hIAimb", k.hIAim, 128, 256)
    tb.nS2 = P.sb(es, "hnS2b", [128, 128], BF16)
    tb.nC2 = P.sb(es, "hnC2b", [128, 128], BF16)
    P.ts(tb.nS2[:, :], tb.S2[:, :], -1.0, None, ALU.mult)
    P.ts(tb.nC2[:, :], tb.C2[:, :], -1.0, None, ALU.mult)
    tb.Tc = P.sb(es, "hTcf", [128, 128], F32)
    tb.Ts = P.sb(es, "hTsf", [128, 128], F32)
    P.dma("sp", tb.Tc[:, :], k.hTc[:, :])
    P.dma("sp", tb.Ts[:, :], k.hTs[:, :])
    tb.TcB = P.sb(es, "hTcB", [128, GQ, 128], BF16)
    tb.TsB = P.sb(es, "hTsB", [128, GQ, 128], BF16)
    P.copy(tb.TcB[:, :, :], bcast(tb.Tc[:, :], [128, GQ, 128], 1))
    P.copy(tb.TsB[:, :, :], bcast(tb.Ts[:, :], [128, GQ, 128], 1))
    return tb


def hy_evacA(P, pA, Ab, s0):
    P.act(Ab[:, 0, s0:s0 + SBQ, :], V(pA.h[:, :, 0:128], pA.key), AF.Copy)
    P.act(Ab[:, 1, s0:s0 + SBQ, :], V(pA.h[:, :, 128:256], pA.key), AF.Copy)


def hy_twiddle(P, tb, Bt, tmp, inverse, Ab):
    Are, Aim = Ab[:, 0, :, :], Ab[:, 1, :, :]
    Tc, Ts = tb.TcB[:, :, :], tb.TsB[:, :, :]
    t1, t2, t3, t4 = [t[:, :, :] for t in tmp]
    P.tt(t1, Are, Tc, ALU.mult)
    P.tt(t2, Aim, Ts, ALU.mult)
    P.tt(t3, Aim, Tc, ALU.mult)
    P.tt(t4, Are, Ts, ALU.mult)
    if not inverse:
        P.tt(Bt[:, 0, :, :], t1, t2, ALU.add)
        P.tt(Bt[:, 1, :, :], t3, t4, ALU.subtract)
    else:
        P.tt(Bt[:, 0, :, :], t1, t2, ALU.subtract)
        P.tt(Bt[:, 1, :, :], t3, t4, ALU.add)


def flat2(b, i, s0):
    return V(b.h[:, i, s0:s0 + SBQ, :].rearrange("p s l -> p (s l)"), b.key)


def hy_stageA_fwd(P, tb, X, pA, s0):
    for s in range(SBQ):
        P.mm(pA[:, s, :], X[0:64, s0 + s, :], tb.FA[0:64, :])


def hy_stageB_fwd(P, tb, Bts, pB, s0):
    n = len(Bts)
    for i, (Bt, conj) in enumerate(Bts):
        P.mm(pB[:, 0, :], tb.C2[:, :], flat2(Bt, 0, s0), start=(i == 0), stop=False)
        P.mm(pB[:, 0, :], tb.S2[:, :], flat2(Bt, 1, s0), start=False, stop=(i == n - 1))
    for i, (Bt, conj) in enumerate(Bts):
        if not conj:
            P.mm(pB[:, 1, :], tb.C2[:, :], flat2(Bt, 1, s0), start=(i == 0), stop=False)
            P.mm(pB[:, 1, :], tb.nS2[:, :], flat2(Bt, 0, s0), start=False, stop=(i == n - 1))
        else:
            P.mm(pB[:, 1, :], tb.nC2[:, :], flat2(Bt, 1, s0), start=(i == 0), stop=False)
            P.mm(pB[:, 1, :], tb.S2[:, :], flat2(Bt, 0, s0), start=False, stop=(i == n - 1))


def seqview(dr, row0, nrows):
    return V(dr.h[row0:row0 + nrows, 0:L].rearrange("c (h l) -> h c l", l=128), None)


def run_pipelined(gens_fn, items, npipe):
    for g0 in range(0, len(items), npipe):
        alive = [gens_fn(i, items[g0 + i]) for i in range(npipe) if g0 + i < len(items)]
        while alive:
            nxt = []
            for g in alive:
                try:
                    next(g)
                    nxt.append(g)
                except StopIteration:
                    pass
            alive = nxt


class PsPool:
    def __init__(self, tiles):
        self.t = tiles
        self.c = 0

    def next(self):
        self.c += 1
        return self.t[self.c % len(self.t)]


def phase_hy_spec(P, k, l):
    with ExitStack() as es:
        tb = hy_tables(P, es, k)
        NPIPE = 3
        mk = lambda nm, shp, dt: [P.sb(es, f"{nm}{i}", shp, dt) for i in range(NPIPE)]
        Xf = mk("hXf", [64, GQ, 128], BF16)
        Xb = mk("hXb", [64, GQ, 128], BF16)
        Bf = mk("hBf", [128, 2, GQ, 128], BF16)
        Bb = mk("hBb", [128, 2, GQ, 128], BF16)
        tmp = [mk(f"htw{i}", [128, GQ, 128], BF16) for i in range(4)]
        tmp2 = [mk(f"htx{i}", [128, GQ, 128], BF16) for i in range(4)]
        Ab1 = mk("hAb1", [128, 2, GQ, 128], BF16)
        Ab2 = mk("hAb2", [128, 2, GQ, 128], BF16)
        rnb = P.sb(es, "hrnb", [128, 2, 512], F32)
        P.dma("sp", V(rnb.h[:, :, :].rearrange("p n c -> p (n c)"), rnb.key),
              V(k.hyNrm.h[:, :].rearrange("n c -> (n c)").unsqueeze(0).to_broadcast([128, 1024]), None))
        Sp = mk("hSp", [128, 2, GQ, 128], BF16)
        Ys = mk("hYs", [128, 2, GQ, 128], F32)
        pA = PsPool([P.ps(es, f"hpA{i}", [128, SBQ, 256], F32) for i in range(2)])
        pB = PsPool([P.ps(es, f"hpB{i}", [128, 2, SBQ * 128], F32) for i in range(2)])

        def steps(i, item):
            n, c0 = item
            P.dma("sp", Xf[i][:, :, :], seqview(k.hyK, n * 1024 + c0, GQ))
            P.dma("sp", Xb[i][:, :, :], seqview(k.hyK, n * 1024 + 512 + c0, GQ))
            for hf in range(NH):
                pa = pA.next()
                hy_stageA_fwd(P, tb, Xf[i], pa, hf * SBQ)
                hy_evacA(P, pa, Ab1[i], hf * SBQ)
            yield
            hy_twiddle(P, tb, Bf[i], [t[i] for t in tmp], False, Ab1[i])
            for hf in range(NH):
                pa = pA.next()
                hy_stageA_fwd(P, tb, Xb[i], pa, hf * SBQ)
                hy_evacA(P, pa, Ab2[i], hf * SBQ)
            yield
            hy_twiddle(P, tb, Bb[i], [t[i] for t in tmp2], False, Ab2[i])
            yield
            for hf in range(NH):
                pb = pB.next()
                hy_stageB_fwd(P, tb, [(Bf[i], False), (Bb[i], True)], pb, hf * SBQ)
                P.act(Ys[i][:, :, hf * SBQ:(hf + 1) * SBQ, :],
                      V(pb.h[:, :, :].rearrange("p r (s l) -> p r s l", l=128), pb.key), AF.Copy)
            yield
            rv = V(rnb.h[:, n, c0:c0 + GQ].unsqueeze(1).unsqueeze(3).to_broadcast([128, 2, GQ, 128]), rnb.key)
            P.tt(Sp[i][:, :, :, :], Ys[i][:, :, :, :], rv, ALU.mult)
            for ri in range(2):
                P.dma("act", V(k.hySpec.h[n, ri, :, c0:c0 + GQ, :], None), Sp[i][:, ri, :, :])
            yield

        items = [(n, c0) for n in range(2) for c0 in range(0, 512, GQ)]
        run_pipelined(steps, items, NPIPE)
        P.flush()


def phase_hy_data(P, k, l):
    with ExitStack() as es:
        tb = hy_tables(P, es, k)
        skb = P.sb(es, "hskb", [64, 2, 512], F32)
        P.dma("sp", V(skb.h[:, :, :].rearrange("p n c -> p (n c)"), skb.key),
              V(k.hy_skip.h[l, :, :].rearrange("n c -> (n c)").unsqueeze(0).to_broadcast([64, 1024]), None))
        NPIPE = 3
        mk = lambda nm, shp, dt: [P.sb(es, f"{nm}{i}", shp, dt) for i in range(NPIPE)]
        X = mk("dX", [64, GQ, 128], BF16)
        yp = mk("dyp", [64, GQ, 128], F32)
        gt = [mk("dg0", [64, GQ, 128], F32), mk("dg1", [64, GQ, 128], F32)]
        Bt = mk("dB", [128, 2, GQ, 128], BF16)
        Zt = mk("dZ", [128, 2, GQ, 128], BF16)
        Sp = mk("dSp", [128, 2, GQ, 128], BF16)
        tmp = [mk(f"dtw{i}", [128, GQ, 128], BF16) for i in range(4)]
        Ab = mk("dAb", [128, 2, GQ, 128], BF16)
        Yb = mk("dYb", [128, 2, GQ, 128], BF16)
        cmb = mk("dcmb", [64, GQ, 128], F32)
        yfs = mk("dyfs", [64, GQ, 128], F32)
        yb = mk("dyb", [64, GQ, 128], BF16)
        pA = PsPool([P.ps(es, f"dpA{i}", [128, SBQ, 256], F32) for i in range(2)])
        pB = PsPool([P.ps(es, f"dpB{i}", [128, 2, SBQ * 128], F32) for i in range(2)])

        def steps(i, c0):
            P.dma("sp", X[i][:, :, :], seqview(k.hyVb, c0, GQ))
            P.dma("sp", yp[i][:, :, :], seqview(k.hyU, c0, GQ))
            P.dma("sp", gt[0][i][:, :, :], seqview(k.hyU, 512 + c0, GQ))
            P.dma("sp", gt[1][i][:, :, :], seqview(k.hyU, 1024 + c0, GQ))
            yield
            tw = [t[i] for t in tmp]
            for n in range(2):
                for ri in range(2):
                    P.dma("sp", Sp[i][:, ri, :, :], V(k.hySpec.h[n, ri, :, c0:c0 + GQ, :], None))
                for hf in range(NH):
                    pa = pA.next()
                    hy_stageA_fwd(P, tb, X[i], pa, hf * SBQ)
                    hy_evacA(P, pa, Ab[i], hf * SBQ)
                yield
                hy_twiddle(P, tb, Bt[i], tw, False, Ab[i])
                yield
                for hf in range(NH):
                    pb = pB.next()
                    hy_stageB_fwd(P, tb, [(Bt[i], False)], pb, hf * SBQ)
                    P.act(Yb[i][:, :, hf * SBQ:(hf + 1) * SBQ, :],
                          V(pb.h[:, :, :].rearrange("p r (s l) -> p r s l", l=128), pb.key), AF.Copy)
                yield
                t1, t2, t3, t4 = [t[:, :, :] for t in tw]
                Yre, Yim = Yb[i][:, 0, :, :], Yb[i][:, 1, :, :]
                P.tt(t1, Yre, Sp[i][:, 0, :, :], ALU.mult)
                P.tt(t2, Yim, Sp[i][:, 1, :, :], ALU.mult)
                P.tt(t3, Yre, Sp[i][:, 1, :, :], ALU.mult)
                P.tt(t4, Yim, Sp[i][:, 0, :, :], ALU.mult)
                P.tt(Zt[i][:, 0, :, :], t1, t2, ALU.subtract)
                P.tt(Zt[i][:, 1, :, :], t3, t4, ALU.add)
                yield
                for hf in range(NH):
                    pa = pA.next()
                    for s in range(SBQ):
                        P.mm(pa[:, s, :], Zt[i][:, 0, hf * SBQ + s, :], tb.IAre[:, :], start=True, stop=False)
                        P.mm(pa[:, s, :], Zt[i][:, 1, hf * SBQ + s, :], tb.IAim[:, :], start=False, stop=True)
                    hy_evacA(P, pa, Ab[i], hf * SBQ)
                yield
                hy_twiddle(P, tb, Bt[i], tw, True, Ab[i])
                yield
                for hf in range(NH):
                    pb = pB.next()
                    P.mm(pb[0:64, 0, :], tb.C2[:, 0:64], flat2(Bt[i], 0, hf * SBQ), start=True, stop=False)
                    P.mm(pb[0:64, 0, :], tb.nS2[:, 0:64], flat2(Bt[i], 1, hf * SBQ), start=False, stop=True)
                    P.act(yfs[i][:, hf * SBQ:(hf + 1) * SBQ, :],
                          V(pb.h[0:64, 0, :].rearrange("p (s l) -> p s l", l=128), pb.key), AF.Copy)
                yield
                a = cmb[i][:, :, :]
                sk = V(skb.h[:, n, c0:c0 + GQ].unsqueeze(2).to_broadcast([64, GQ, 128]), skb.key)
                P.tt(a, yp[i][:, :, :], sk, ALU.mult)
                P.tt(a, a, yfs[i][:, :, :], ALU.add)
                if n == 0:
                    P.tt(yp[i][:, :, :], a, gt[0][i][:, :, :], ALU.mult)
                    P.act(X[i][:, :, :], yp[i][:, :, :], AF.Copy)
                else:
                    P.tt(yb[i][:, :, :], a, gt[1][i][:, :, :], ALU.mult)
                    P.dma("act", V(k.yT.h[0, c0:c0 + GQ, 0:L].rearrange("c (h l) -> h c l", l=128), None), yb[i][:, :, :])
                yield

        run_pipelined(steps, list(range(0, 512, GQ)), NPIPE)
        P.flush()


def phase_hy_ctx(P, k, l):
    with ExitStack() as es:
        idf, idb = load_ident(P, es, k)
        h2T = P.sb(es, "ch2T", [64, LC], F32)
        _filter_mlp(P, es, k, l, k.hz_ctx, LC, h2T)
        w3 = P.sb(es, "cw3", [64, 2048], F32)
        P.dma("sp", w3[:, :], k.hy_pe_w3[l, :, :])
        adb = P.sb(es, "cadb", [128, 2048], F32)
        P.dma("sp", adb[:, :], V(k.hy_decay.h[l:l + 1, :].to_broadcast([128, 2048]), None))
        P.act(adb[:, :], adb[:, :], AF.Abs)
        ntn = P.sb(es, "cntn", [128, 2], F32)
        P.dma("sp", ntn[:, :], V(k.hntn_ctx.h[:, :].rearrange("(c p) o -> p (c o)", p=128), None), allow_slow_non_contiguous=True)
        kfc = P.sb(es, "ckfc", [128, 2, 2048], F32)
        E = P.sbpool(es, "cE", 2, [128, 512], F32)
        pf = [P.ps(es, f"cpf{i}") for i in range(2)]
        nb = 0
        for tc in range(2):
            for cb in range(4):
                cs = slice(cb * 512, (cb + 1) * 512)
                p_ = pf[nb % 2]
                P.mm(p_[:, :], h2T[0:64, tc * 128:(tc + 1) * 128], w3[:, cs])
                e_ = E[nb % 2]
                P.act(e_[:, :], adb[:, cs], AF.Exp, scale=ntn[:, tc:tc + 1])
                P.tt(kfc[:, tc, cs], p_[:, :], e_[:, :], ALU.mult)
                nb += 1
        for n in range(2):
            P.memset(kfc[0:1, 0, n * 1024 + 512:(n + 1) * 1024], 0.0)
        sq = P.sb(es, "csq", [128, 2, 2048], F32)
        P.tt(sq[:, :, :], kfc[:, :, :], kfc[:, :, :], ALU.mult)
        ones = P.sb(es, "cones", [128, 128], F32)
        P.memset(ones[:, :], 1.0)
        ssb = P.sb(es, "cssb", [128, 4, 512], F32)
        for cb in range(4):
            p_ = pf[cb % 2]
            for tc in range(2):
                P.mm(p_[:, :], ones[:, :], sq[:, tc, cb * 512:(cb + 1) * 512], start=(tc == 0), stop=(tc == 1))
            P.copy(ssb[:, cb, :], p_[:, :])
        rn = P.sb(es, "crn", [128, 2, 512], F32)
        for n in range(2):
            P.tt(rn[:, n, :], ssb[:, 2 * n, :], ssb[:, 2 * n + 1, :], ALU.add)
        P.act(rn[:, :, :], rn[:, :, :], AF.Sqrt, bias=EPSB[0][:, :], scale=1.0)
        P.recip(rn[:, :, :], rn[:, :, :])
        P.ts(rn[:, :, :], rn[:, :, :], 1.0 / (2 * LC), None, ALU.mult)
        Spl = P.sb(es, "cSpl", [128, 2, 2, 512], BF16)
        Smi = P.sb(es, "cSmi", [128, 2, 2, 512], BF16)
        kv = lambda dr: V(kfc.h[:, :, :].rearrange("p t (n d c) -> p t n d c", n=2, d=2)[:, :, :, dr, :], kfc.key)
        P.tt(Spl[:, :, :, :], kv(0), kv(1), ALU.add)
        P.tt(Smi[:, :, :, :], kv(1), kv(0), ALU.subtract)
        st = P.sb(es, "cst", [128, 4, 512], F32)
        W5c = P.sb(es, "cW5c", [128, 4, 512], BF16)
        W5s = P.sb(es, "cW5s", [128, 4, 512], BF16)
        nW5s = P.sb(es, "cnW5s", [128, 4, 512], BF16)
        P.dma("sp", st[:, :, :], V(k.hW5c.h[:, :].rearrange("(c p) k -> p c k", p=128), None))
        P.copy(W5c[:, :, :], st[:, :, :])
        P.dma("sp", st[:, :, :], V(k.hW5s.h[:, :].rearrange("(c p) k -> p c k", p=128), None))
        P.copy(W5s[:, :, :], st[:, :, :])
        P.ts(nW5s[:, :, :], st[:, :, :], -1.0, None, ALU.mult)
        Spc = P.sb(es, "cSpc", [128, 4, 2, 2, 512], F32)
        for kc in range(4):
            ks = slice(kc * 128, (kc + 1) * 128)
            for n in range(2):
                for ri, (tab, src) in enumerate(((W5c, Spl), (W5s, Smi))):
                    p_ = pf[nb % 2]
                    nb += 1
                    for tc in range(2):
                        P.mm(p_[:, :], tab[:, tc, ks], src[:, tc, n, :], start=(tc == 0), stop=(tc == 1))
                    P.tt(Spc[:, kc, n, ri, :], p_[:, :], rn[:, n, :], ALU.mult)
        uf = P.sb(es, "cuf", [128, 12, LC], F32)
        P.dma("sp", uf[:, :, :], V(k.hyU.h[:, L:T].rearrange("(g p) t -> p g t", p=128), None))
        uc = P.sb(es, "cuc", [128, 2, 1536], F32)
        pT = P.ps(es, "cpT")
        for tc in range(2):
            for g0 in range(0, 12, 4):
                for g in range(4):
                    P.transpose(pT[:, g * 128:(g + 1) * 128], uf[:, g0 + g, tc * 128:(tc + 1) * 128], idf[:, :])
                P.copy(uc[:, tc, g0 * 128:(g0 + 4) * 128], pT[:, :])
        skc = P.sb(es, "cskc", [128, 2, 512], F32)
        P.dma("sp", V(skc.h[:, :, :].rearrange("p n c -> p (n c)"), skc.key),
              V(k.hy_skip.h[l, :, :].rearrange("n c -> (n c)").unsqueeze(0).to_broadcast([128, 1024]), None))
        X = P.sb(es, "cX", [128, 2, 512], BF16)
        yp = P.sb(es, "cyp", [128, 2, 512], F32)
        P.copy(yp[:, :, :], uc[:, :, 0:512])
        P.copy(X[:, :, :], uc[:, :, 0:512])
        Z = P.sb(es, "cZ", [128, 4, 2, 512], BF16)
        tw = [P.sb(es, f"ctw{i}", [128, 512], F32) for i in range(4)]
        pY = [P.ps(es, f"cpY{i}") for i in range(2)]
        yb = P.sb(es, "cyb", [128, 2, 512], BF16)
        for n in range(2):
            for kc in range(4):
                ks = slice(kc * 128, (kc + 1) * 128)
                for ri, tab in enumerate((W5c, nW5s)):
                    for tc in range(2):
                        P.mm(pY[ri][:, :], tab[:, tc, ks], X[:, tc, :], start=(tc == 0), stop=(tc == 1))
                t1, t2, t3, t4 = [t[:, :] for t in tw]
                P.tt(t1, pY[0][:, :], Spc[:, kc, n, 0, :], ALU.mult)
                P.tt(t2, pY[1][:, :], Spc[:, kc, n, 1, :], ALU.mult)
                P.tt(t3, pY[0][:, :], Spc[:, kc, n, 1, :], ALU.mult)
                P.tt(t4, pY[1][:, :], Spc[:, kc, n, 0, :], ALU.mult)
                P.tt(Z[:, kc, 0, :], t1, t2, ALU.subtract, eng="pool")
                P.tt(Z[:, kc, 1, :], t3, t4, ALU.add, eng="pool")
            for tc in range(2):
                ts_ = slice(tc * 128, (tc + 1) * 128)
                p_ = pf[tc]
                for kc in range(4):
                    P.mm(p_[:, :], W5c[:, kc, ts_], Z[:, kc, 0, :], start=(kc == 0), stop=False)
                    P.mm(p_[:, :], nW5s[:, kc, ts_], Z[:, kc, 1, :], start=False, stop=(kc == 3))
                a = tw[tc][:, :]
                P.tt(a, yp[:, tc, :], skc[:, n, :], ALU.mult, eng="pool")
                P.tt(a, a, p_[:, :], ALU.add)
                if n == 0:
                    P.tt(yp[:, tc, :], a, uc[:, tc, 512:1024], ALU.mult, eng="pool")
                    P.copy(X[:, tc, :], yp[:, tc, :])
                else:
                    P.tt(yb[:, tc, :], a, uc[:, tc, 1024:1536], ALU.mult, eng="pool")
        pTb = P.ps(es, "cpTb", [128, 8, 128], BF16)
        oT = P.sb(es, "coT", [128, 4, 2, 128], BF16)
        for tc in range(2):
            for c in range(4):
                P.transpose(pTb[:, tc * 4 + c, :], yb[:, tc, c * 128:(c + 1) * 128], idb[:, :])
        P.copy(V(oT.h[:, :, :, :].rearrange("p c t q -> p t c q"), oT.key), V(pTb.h[:, :, :].rearrange("p (t c) q -> p t c q", t=2), pTb.key))
        P.dma("act", V(k.yT.h[0, :, L:T].rearrange("(c p) (t q) -> p c t q", p=128, q=128), None), oT[:, :, :, :])
        P.flush()

def phase_merge(P, k, l, with_ctx):
    with ExitStack() as es:
        idf, idb = load_ident(P, es, k)
        stg = mk_stage(P, es, 2, 2048)
        Wup = P.sb(es, "eWup", [128, 4, 4, 1024], BF16)
        Wg = P.sb(es, "eWg", [128, 8, 4096], BF16)
        Wout = P.sb(es, "eWout", [128, 8, 1024], BF16)
        for n in range(4):
            for c in range(4):
                load_cast(P, stg, Wup[:, n, c, :], k.w_up[l, n, c * 128:(c + 1) * 128, :], 1024)
        for c in range(8):
            for hh in range(2):
                load_cast(P, stg, Wg[:, c, hh * 2048:(hh + 1) * 2048],
                          k.w_in[l, c * 128:(c + 1) * 128, NPROJ + hh * 2048:NPROJ + (hh + 1) * 2048], 2048)
            load_cast(P, stg, Wout[:, c, :], k.w_out[l, c * 128:(c + 1) * 128, :], 1024)
        g1 = P.sbpool(es, "eg1", 2, [128, D], F32)
        for j in range(2):
            P.dma("sp", g1[j][:, :], V(k.mod.h[l, j:j + 1, 2 * D:3 * D].to_broadcast([128, D]), None))
        hts = P.sbpool(es, "eht", 3, [128, 8, 128], BF16)
        yts = P.sbpool(es, "eyt", 3, [128, 4, 4, 128], BF16)
        xts = P.sbpool(es, "ext", 3, [128, D], F32)
        acc = P.sbpool(es, "eacc", 2, [128, D], F32)
        accb = P.sbpool(es, "eaccb", 2, [128, D], BF16)
        accT = P.sbpool(es, "eaccT", 2, [128, 8, 128], BF16)
        sg = P.sbpool(es, "esg", 3, [128, 512], F32)
        tm = P.sbpool(es, "etm", 3, [128, 512], F32)
        psA = [P.ps(es, f"epA{i}") for i in range(3)]
        psB = [P.ps(es, f"epB{i}") for i in range(3)]
        psT = P.ps(es, "epT", [128, 8, 128], BF16)
        hTv = k.hT.h.rearrange("(c p) t -> p c t", p=128)
        tiles = list(range(NTL)) + (list(range(NTL, NT)) if with_ctx else [])
        na = nb = 0
        cn = [0, 0]

        def load(i):
            t0 = tiles[i] * 128
            P.dma("sp", hts[i % 3][:, :, :], V(hTv[:, :, t0:t0 + 128], None))
            for n in range(4):
                P.dma("sp", yts[i % 3][:, n, :, :], V(k.yT.h[n, :, t0:t0 + 128].rearrange("(c p) t -> p c t", p=128), None))
            P.dma("sp", xts[i % 3][:, :], k.xs[t0:t0 + 128, :])

        def body(i):
            ti = tiles[i]
            na, nb = cn
            t0 = ti * 128
            j = 0 if ti < NTL else 1
            ht, yt, xt, ac = hts[i % 3], yts[i % 3], xts[i % 3], acc[i % 2]
            for n in range(4):
                for hf in range(2):
                    fs = slice(hf * 512, (hf + 1) * 512)
                    pu = psA[na % 3]
                    na += 1
                    pg = psB[nb % 3]
                    nb += 1
                    for c in range(4):
                        P.mm(pu[:, :], yt[:, n, c, :], Wup[:, n, c, fs], start=(c == 0), stop=(c == 3))
                    for c in range(8):
                        P.mm(pg[:, :], ht[:, c, :], Wg[:, c, n * 1024 + hf * 512:n * 1024 + (hf + 1) * 512],
                             start=(c == 0), stop=(c == 7))
                    s_ = sg[nb % 3]
                    P.act(s_[:, :], pg[:, :], AF.Sigmoid)
                    if n == 0:
                        P.tt(ac[:, fs], s_[:, :], pu[:, :], ALU.mult)
                    else:
                        t_ = tm[nb % 3]
                        P.tt(t_[:, :], s_[:, :], pu[:, :], ALU.mult)
                        P.tt(ac[:, fs], ac[:, fs], t_[:, :], ALU.add)
            ab = accb[i % 2]
            P.act(ab[:, :], ac[:, :], AF.Copy)
            for c in range(8):
                P.transpose(psT[:, c, :], ab[:, c * 128:(c + 1) * 128], idb[:, :])
            at = accT[i % 2]
            P.act(at[:, :, :], psT[:, :, :], AF.Copy)
            for hf in range(2):
                fs = slice(hf * 512, (hf + 1) * 512)
                po = psA[na % 3]
                na += 1
                for c in range(8):
                    P.mm(po[:, :], at[:, c, :], Wout[:, c, fs], start=(c == 0), stop=(c == 7))
                t_ = tm[(nb + hf) % 3]
                P.tt(t_[:, :], po[:, :], g1[j][:, fs], ALU.mult)
                P.tt(xt[:, fs], xt[:, fs], t_[:, :], ALU.add)
            P.dma("sp", k.xs[t0:t0 + 128, :], xt[:, :])
            cn[0], cn[1] = na, nb

        prefetch_loop(len(tiles), load, body, 1)
        P.flush()


def phase_mlp(P, k, l, with_ctx):
    TBK = 256
    with ExitStack() as es:
        stg = mk_stage(P, es, 2, 1024)
        W1 = P.sb(es, "fW1", [128, 8, 4096], BF16)
        W2 = P.sb(es, "fW2", [128, 32, 1024], BF16)
        for c in range(8):
            for q in range(4):
                load_cast(P, stg, W1[:, c, q * 1024:(q + 1) * 1024], k.mlp_w1[l, c * 128:(c + 1) * 128, q * 1024:(q + 1) * 1024], 1024)
        for c in range(32):
            load_cast(P, stg, W2[:, c, :], k.mlp_w2[l, c * 128:(c + 1) * 128, :], 1024)
        b1 = P.sb(es, "fb1", [128, 32], F32)
        P.dma("sp", b1[:, :], V(k.mlp_b1.h[l:l + 1, :].rearrange("o (c p) -> p (o c)", p=128), None), allow_slow_non_contiguous=True)
        b2 = P.sb(es, "fb2", [128, D], F32)
        P.dma("sp", b2[:, :], V(k.mlp_b2.h[l:l + 1, :].to_broadcast([128, D]), None))
        g2 = P.sbpool(es, "fg2", 2, [128, D], F32)
        for j in range(2):
            P.dma("sp", g2[j][:, :], V(k.mod.h[l, j:j + 1, 5 * D:6 * D].to_broadcast([128, D]), None))
        hts = P.sbpool(es, "fht", 2, [128, 8, TBK], BF16)
        aT = P.sb(es, "faT", [128, 32, TBK], BF16)
        rt = P.sbpool(es, "frt", 2, [128, TBK], F32)
        xts = P.sbpool(es, "fxt", 4, [128, D], F32)
        tm = P.sbpool(es, "ftm", 2, [128, 512], F32)
        psA = [P.ps(es, f"fpA{i}") for i in range(3)]
        psB = [P.ps(es, f"fpB{i}") for i in range(4)]
        hTv = k.hT.h.rearrange("(c p) t -> p c t", p=128)
        nblk = (T if with_ctx else L) // TBK
        na = nb = nx = 0
        def load_blk(bi):
            t0 = bi * TBK
            P.dma("sp", hts[bi % 2][:, :, :], V(hTv[:, :, t0:t0 + TBK], None))

        load_blk(0)
        for bi in range(nblk):
            t0 = bi * TBK
            j = 0 if t0 < L else 1
            ht = hts[bi % 2]
            nsub = TBK // 128
            for sub in range(nsub):
                P.dma("sp", xts[(bi * nsub + sub) % 4][:, :], k.xs[t0 + sub * 128:t0 + (sub + 1) * 128, :])
            for fc in range(32):
                ps = psA[na % 3]
                na += 1
                for c in range(8):
                    P.mm(ps[:, 0:TBK], W1[:, c, fc * 128:(fc + 1) * 128], ht[:, c, :], start=(c == 0), stop=(c == 7))
                r_ = rt[fc % 2]
                P.act(r_[:, :], ps[:, 0:TBK], AF.Relu, bias=b1[:, fc:fc + 1])
                P.act(aT[:, fc, :], r_[:, :], AF.Square)
            if bi + 1 < nblk:
                load_blk(bi + 1)
            for sub in range(nsub):
                xt = xts[(bi * nsub + sub) % 4]
                r0 = t0 + sub * 128
                for hf in range(2):
                    fs = slice(hf * 512, (hf + 1) * 512)
                    po = psB[nb % 4]
                    nb += 1
                    for fc in range(32):
                        P.mm(po[:, :], aT[:, fc, sub * 128:(sub + 1) * 128], W2[:, fc, fs], start=(fc == 0), stop=(fc == 31))
                    t_ = tm[nb % 2]
                    P.tt(t_[:, :], po[:, :], b2[:, fs], ALU.add)
                    P.tt(t_[:, :], t_[:, :], g2[j][:, fs], ALU.mult)
                    P.tt(xt[:, fs], xt[:, fs], t_[:, :], ALU.add)
                P.dma("sp", k.xs[r0:r0 + 128, :], xt[:, :])
        P.flush()


def phase_final(P, k):
    with ExitStack() as es:
        g = P.sb(es, "zg", [128, D], F32)
        P.dma("sp", g[:, :], V(k.final_g.h[0:1, :].to_broadcast([128, D]), None))
        xt = P.sbpool(es, "zx", 4, [128, D], F32)
        junk2 = P.sbpool(es, "zj", 2, [128, D], F32)
        ss = P.sbpool(es, "zs", 3, [128, 1], F32)
        yo = P.sbpool(es, "zy", 3, [128, D], F32)
        def load(ti):
            P.dma("sp", xt[ti % 4][:, :], k.xs[ti * 128:(ti + 1) * 128, :])

        def body(ti):
            x_, s_, y_ = xt[ti % 4], ss[ti % 3], yo[ti % 3]
            P.act(junk2[ti % 2][:, :], x_[:, :], AF.Square, accum_out=s_[:, :])
            yield
            P.act(s_[:, :], s_[:, :], AF.Sqrt, bias=EPSB[0][:, :], scale=1.0 / D)
            yield
            P.recip(s_[:, :], s_[:, :])
            yield
            P.stt(y_[:, :], x_[:, :], s_[:, :], g[:, :], ALU.mult, ALU.mult)
            P.dma("sp", k.y[ti * 128:(ti + 1) * 128, :], y_[:, :])

        prefetch_pipelined(NTL, load, body, 2, 2)
        P.flush()
def build(stages=99, layers=DEPTH):
    nc = bass.Bass("TRN2", target_bir_lowering=False)
    k = declare(nc)
    declare_ml(nc, k)
    declare_hy(nc, k)
    P = Prog(nc)
    with ExitStack() as ges:
        make_consts(P, ges)
        phase_init(P, k)
        if stages >= 0.5:
            phase_mod(P, k)
        for l in range(layers if stages >= 1 else 0):
            wco = l < DEPTH - 1
            phase_norm(P, k, l, k.xs, k.ln1_g, 1, 0)
            if stages >= 2:
                phase_inproj(P, k, l)
            if stages >= 3 and 'noattn' not in DEBUG:
                phase_ga_prep(P, k, l)
                phase_ga_attn(P, k, l, wco)
            if stages >= 4 and 'noattn' not in DEBUG:
                phase_wa_prep(P, k, l)
                phase_wa_attn(P, k, l, wco)
            if stages >= 5 and 'noml' not in DEBUG:
                phase_ml_prep(P, k, l)
                phase_ml_scan(P, k, l)
                phase_ml_finish(P, k, l, wco)
            if stages >= 6 and 'nohy' not in DEBUG:
                phase_hy_conv(P, k, l)
                phase_hy_filt(P, k, l)
                phase_hy_spec(P, k, l)
                phase_hy_data(P, k, l)
                if wco:
                    phase_hy_ctx(P, k, l)
            if 'ya_in' in DEBUG:
                ya_dbg = dram(nc, f"ya_dbg{l}", [512, T], BF16, "ExternalInput")
                P.dma("sp", k.yT[0, :, :], ya_dbg[:, :])
                P.flush()
            if stages >= 7:
                phase_merge(P, k, l, wco)
                phase_norm(P, k, l, k.xs, k.ln2_g, 4, 3, do_ctx=wco)
                phase_mlp(P, k, l, wco)
        if stages >= 8:
            phase_final(P, k)
        else:
            P.dma("sp", k.y[0:128, :], k.xs[0:128, :])
            P.flush()
    P.es.close()
    return nc


def make_inputs(inputs, b):
    f = lambda a: np.ascontiguousarray(np.asarray(a, dtype=np.float32))
    m = {
        "x": f(inputs["x"][b]),
        "ctx": f(inputs["ctx"][b]),
        "cvec": f(np.stack([inputs["c"][b], inputs["c_ctx"]])),
        "final_g": f(inputs["final_g"]).reshape(1, D),
        "ident": np.eye(128, dtype=np.float32),
    }
    m.update(CONSTS)
    for n in ["w_mod", "b_mod", "ln1_g", "ln2_g", "w_in", "w_up", "w_out", "mlp_w1", "mlp_b1", "mlp_w2", "mlp_b2",
              "ga_q_g", "ga_k_g", "wa_sink", "ml_conv_w", "ml_conv_b", "ml_gate_b", "ml_norm_g"] + HY_IN:
        m[n] = f(inputs[n])
    return m


_NC = [None]


def kernel(**inputs):
    if _NC[0] is None:
        _NC[0] = build()
    nc = _NC[0]
    in_maps = [make_inputs(inputs, b) for b in range(8)]
    res = run_bass_kernel_spmd(nc, in_maps, core_ids=list(range(8)))
    return np.stack([r["y"] for r in res.results], axis=0)
```
